# Optimizing a Trainium2 kernel written in Bass

```python
import math
import jax
import jax.numpy as jnp
from jax import lax
import numpy as np

D_MODEL = 1024
BATCH = 16
SEQ = 256
DEPTH = 1
DEC_BATCH = 8
DEC_SEQ = 4096
PAST_LEN = 512

GRID_W = 64
N_HEADS = 16
N_KV_HEADS = 4
HEAD_DIM = D_MODEL // N_HEADS
GQA_GROUP = N_HEADS // N_KV_HEADS
WINDOW = 128
BLOCK = 128
ROPE_BASE = 10000.0
D_HYENA = D_MODEL
HYENA_ORDER = 2
N_BANDS = 16
FILTER_EMB = 1 + 2 * N_BANDS
FILTER_WIDTH = 64
DECAY_MIN = math.log(100.0) / 1.5
DECAY_MAX = math.log(100.0) / 0.3
D_FF = 2816
N_MOD = 9
EPS = 1e-6
NEG_INF = -1e30

Q_COLS = N_HEADS * HEAD_DIM
KV_COLS = N_KV_HEADS * HEAD_DIM
HY_COLS = 3 * D_HYENA
GATE_COLS = 2 * D_MODEL
IN_COLS = Q_COLS + 2 * KV_COLS + HY_COLS + GATE_COLS
SPLIT_AT = (Q_COLS, Q_COLS + KV_COLS, Q_COLS + 2 * KV_COLS, Q_COLS + 2 * KV_COLS + HY_COLS)

kernel_name = 'hybrid_hyena_swa_dit_step'


def _rmsnorm(x, g):
    x32 = x.astype(jnp.float32)
    y = x32 * lax.rsqrt(jnp.mean(x32 * x32, axis=-1, keepdims=True) + EPS)
    return (y * g.astype(jnp.float32)).astype(x.dtype)


def _modulate(h, shift, scale):
    return h * (1.0 + scale) + shift


def _swiglu(h, w_gate_up, w_down):
    gate, up = jnp.split(h @ w_gate_up, 2, axis=-1)
    return (jax.nn.silu(gate) * up) @ w_down


def _axial_rope(x):
    n = x.shape[1]
    rows = n // GRID_W
    row = jnp.repeat(jnp.arange(rows), GRID_W)
    col = jnp.tile(jnp.arange(GRID_W), rows)
    n_freq = HEAD_DIM // 4
    inv = ROPE_BASE ** (-jnp.arange(n_freq, dtype=jnp.float32) / n_freq)

    def rot(xh, pos):
        ang = pos.astype(jnp.float32)[:, None] * inv[None, :]
        cos = jnp.cos(ang)[None, :, None, :]
        sin = jnp.sin(ang)[None, :, None, :]
        x1, x2 = jnp.split(xh.astype(jnp.float32), 2, axis=-1)
        return jnp.concatenate([x1 * cos - x2 * sin, x2 * cos + x1 * sin], axis=-1)

    xr, xc = jnp.split(x, 2, axis=-1)
    return jnp.concatenate([rot(xr, row), rot(xc, col)], axis=-1).astype(x.dtype)


def _attend(q, k, v, mask, sink):
    s = jnp.einsum('bqhgd,bkhd->bhgqk', q, k).astype(jnp.float32) * (1.0 / math.sqrt(HEAD_DIM))
    if mask is not None:
        s = jnp.where(mask, s, NEG_INF)
    sink_col = jnp.broadcast_to(sink.astype(jnp.float32).reshape(1, N_KV_HEADS, GQA_GROUP, 1, 1), s.shape[:-1] + (1,))
    p = jax.nn.softmax(jnp.concatenate([sink_col, s], axis=-1), axis=-1)[..., 1:]
    return jnp.einsum('bhgqk,bkhd->bqhgd', p.astype(v.dtype), v)


def _context_attention(q, k, v, sink):
    b, s_len = q.shape[:2]
    nb = s_len // BLOCK
    qb = q.reshape(b, nb, BLOCK, N_KV_HEADS, GQA_GROUP, HEAD_DIM)

    def block(i):
        qi = lax.dynamic_index_in_dim(qb, i, axis=1, keepdims=False)
        return _attend(qi, k, v, None, sink)

    o = lax.map(block, jnp.arange(nb))
    return jnp.moveaxis(o, 0, 1).reshape(b, s_len, Q_COLS)


def _latent_attention(q, k, v, ck, cv, sink):
    b, n = q.shape[:2]
    nb = n // BLOCK
    p_len = ck.shape[1]
    qb = q.reshape(b, nb, BLOCK, N_KV_HEADS, GQA_GROUP, HEAD_DIM)
    pad = ((0, 0), (BLOCK, BLOCK), (0, 0), (0, 0))
    kp = jnp.pad(k, pad)
    vp = jnp.pad(v, pad)
    ctx_mask = jnp.ones((BLOCK, p_len), dtype=bool)

    def block(i):
        qi = lax.dynamic_index_in_dim(qb, i, axis=1, keepdims=False)
        kw = lax.dynamic_slice_in_dim(kp, i * BLOCK, 3 * BLOCK, axis=1)
        vw = lax.dynamic_slice_in_dim(vp, i * BLOCK, 3 * BLOCK, axis=1)
        qpos = i * BLOCK + jnp.arange(BLOCK)
        kpos = (i - 1) * BLOCK + jnp.arange(3 * BLOCK)
        band = (jnp.abs(qpos[:, None] - kpos[None, :]) <= WINDOW) & (kpos[None, :] >= 0) & (kpos[None, :] < n)
        mask = jnp.concatenate([ctx_mask, band], axis=1)
        keys = jnp.concatenate([ck, kw], axis=1)
        vals = jnp.concatenate([cv, vw], axis=1)
        return _attend(qi, keys, vals, mask, sink)

    o = lax.map(block, jnp.arange(nb))
    return jnp.moveaxis(o, 0, 1).reshape(b, n, Q_COLS)


def _short_conv(u, w, bias):
    up = jnp.pad(u, ((0, 0), (1, 1), (0, 0)))
    return up[:, :-2] * w[0] + up[:, 1:-1] * w[1] + up[:, 2:] * w[2] + bias


def _hyena_filters_rfft(n, p):
    t = jnp.arange(n, dtype=jnp.float32) / max(n - 1, 1)
    bands = jnp.arange(1, N_BANDS + 1, dtype=jnp.float32)
    ang = 2.0 * math.pi * t[:, None] * bands[None, :]
    z = jnp.concatenate([t[:, None], jnp.cos(ang), jnp.sin(ang)], axis=-1)
    freq = p['filt_freq'].astype(jnp.float32)
    h = jnp.sin(freq * (z @ p['filt_w1'].astype(jnp.float32) + p['filt_b1'].astype(jnp.float32)))
    h = jnp.sin(freq * (h @ p['filt_w2'].astype(jnp.float32) + p['filt_b2'].astype(jnp.float32)))
    h = (h @ p['filt_w3'].astype(jnp.float32) + p['filt_b3'].astype(jnp.float32)).reshape(n, HYENA_ORDER, 2, D_HYENA)
    h = h * jnp.exp(-t[:, None, None, None] * jnp.abs(p['filt_decay'].astype(jnp.float32))[None])
    h = h / (jnp.sum(jnp.abs(h), axis=(0, 2), keepdims=True) + EPS)
    fwd = h[:, :, 0]
    bwd = h[:, :, 1]
    circ = jnp.concatenate([fwd, jnp.zeros_like(fwd[:1]), bwd[1:][::-1]], axis=0)
    return jnp.fft.rfft(circ, axis=0)


def _fftconv(z, kf, skip):
    n = z.shape[1]
    zf = jnp.fft.rfft(z, n=2 * n, axis=1)
    y = jnp.fft.irfft(zf * kf[None], n=2 * n, axis=1)[:, :n]
    return y + z * skip


def _hyena(u, p):
    n = u.shape[1]
    u32 = _short_conv(u, p['conv_w'], p['conv_b']).astype(jnp.float32)
    x1, x2, v = jnp.split(u32, 3, axis=-1)
    kf = _hyena_filters_rfft(n, p)
    skip = p['hyena_skip'].astype(jnp.float32)
    z = x1 * _fftconv(v, kf[:, 0], skip[0])
    z = x2 * _fftconv(z, kf[:, 1], skip[1])
    return z.astype(u.dtype)


def _layer(x, mod, p, ctx_kv):
    sh1, sc1, g1, sh2, sc2, g2, sh3, sc3, g3 = jnp.split(mod, N_MOD, axis=-1)
    h = _modulate(_rmsnorm(x, p['norm_ffn1']), sh1, sc1)
    x = x + 0.5 * g1 * _swiglu(h, p['ffn1_wi'], p['ffn1_wo'])

    h = _modulate(_rmsnorm(x, p['norm_mix']), sh2, sc2)
    b, n = h.shape[:2]
    q, k, v, hy_in, gates = jnp.split(h @ p['w_in'], SPLIT_AT, axis=-1)
    q = _rmsnorm(q.reshape(b, n, N_HEADS, HEAD_DIM), p['q_norm'])
    k = _rmsnorm(k.reshape(b, n, N_KV_HEADS, HEAD_DIM), p['k_norm'])
    v = v.reshape(b, n, N_KV_HEADS, HEAD_DIM)
    if ctx_kv is None:
        attn = _context_attention(q, k, v, p['attn_sink'])
        state = (k, v)
    else:
        attn = _latent_attention(_axial_rope(q), _axial_rope(k), v, ctx_kv[0], ctx_kv[1], p['attn_sink'])
        state = None
    hy = _hyena(hy_in, p)
    gate_a, gate_h = jnp.split(gates, 2, axis=-1)
    merged = jax.nn.sigmoid(gate_a) * (attn @ p['w_attn_branch']) + jax.nn.sigmoid(gate_h) * (hy @ p['w_hyena_branch'])
    x = x + g2 * (merged @ p['w_out'])

    h = _modulate(_rmsnorm(x, p['norm_ffn2']), sh3, sc3)
    x = x + 0.5 * g3 * _swiglu(h, p['ffn2_wi'], p['ffn2_wo'])
    return x, state


def _nrm(k, shape, scale=1.0):
    return scale * jax.random.normal(k, shape, jnp.float32)


def setup_inputs(seed: int = 0) -> dict:
    key = jax.random.key(seed)
    k = jax.random.split(key, 33)
    decay0 = jnp.broadcast_to(jnp.linspace(DECAY_MIN, DECAY_MAX, D_HYENA, dtype=jnp.float32), (DEPTH, HYENA_ORDER, 2, D_HYENA))
    return {
        'x_prompt': _nrm(k[0], (BATCH, SEQ, D_MODEL)),
        'x_sample': _nrm(k[1], (DEC_BATCH, DEC_SEQ, D_MODEL)),
        'cache_k': _nrm(k[2], (DEC_BATCH, DEPTH, PAST_LEN, N_KV_HEADS, HEAD_DIM)),
        'cache_v': _nrm(k[3], (DEC_BATCH, DEPTH, PAST_LEN, N_KV_HEADS, HEAD_DIM)),
        'c': _nrm(k[4], (DEC_BATCH, D_MODEL)),
        'c_ctx': _nrm(k[5], (D_MODEL,)),
        'w_mod': _nrm(k[6], (DEPTH, D_MODEL, N_MOD * D_MODEL), D_MODEL ** -0.5),
        'b_mod': _nrm(k[7], (DEPTH, N_MOD * D_MODEL), 0.02),
        'norm_ffn1': 1.0 + _nrm(k[8], (DEPTH, D_MODEL), 0.02),
        'ffn1_wi': _nrm(k[9], (DEPTH, D_MODEL, 2 * D_FF), D_MODEL ** -0.5),
        'ffn1_wo': _nrm(k[10], (DEPTH, D_FF, D_MODEL), D_FF ** -0.5),
        'norm_mix': 1.0 + _nrm(k[11], (DEPTH, D_MODEL), 0.02),
        'w_in': _nrm(k[12], (DEPTH, D_MODEL, IN_COLS), D_MODEL ** -0.5),
        'q_norm': 1.0 + _nrm(k[13], (DEPTH, HEAD_DIM), 0.02),
        'k_norm': 1.0 + _nrm(k[14], (DEPTH, HEAD_DIM), 0.02),
        'attn_sink': _nrm(k[15], (DEPTH, N_HEADS), 0.5),
        'conv_w': _nrm(k[16], (DEPTH, 3, HY_COLS), 3.0 ** -0.5),
        'conv_b': _nrm(k[17], (DEPTH, HY_COLS), 0.01),
        'filt_w1': _nrm(k[18], (DEPTH, FILTER_EMB, FILTER_WIDTH), FILTER_EMB ** -0.5),
        'filt_b1': _nrm(k[19], (DEPTH, FILTER_WIDTH), 0.02),
        'filt_w2': _nrm(k[20], (DEPTH, FILTER_WIDTH, FILTER_WIDTH), FILTER_WIDTH ** -0.5),
        'filt_b2': _nrm(k[21], (DEPTH, FILTER_WIDTH), 0.02),
        'filt_w3': _nrm(k[22], (DEPTH, FILTER_WIDTH, HYENA_ORDER * 2 * D_HYENA), FILTER_WIDTH ** -0.5),
        'filt_b3': _nrm(k[23], (DEPTH, HYENA_ORDER * 2 * D_HYENA), 0.02),
        'filt_freq': 1.0 + _nrm(k[24], (DEPTH, FILTER_WIDTH), 0.02),
        'filt_decay': decay0 + _nrm(k[25], (DEPTH, HYENA_ORDER, 2, D_HYENA), 0.1),
        'hyena_skip': _nrm(k[26], (DEPTH, HYENA_ORDER, D_HYENA), 0.1),
        'w_attn_branch': _nrm(k[27], (DEPTH, Q_COLS, D_MODEL), Q_COLS ** -0.5),
        'w_hyena_branch': _nrm(k[28], (DEPTH, D_HYENA, D_MODEL), D_HYENA ** -0.5),
        'w_out': _nrm(k[29], (DEPTH, D_MODEL, D_MODEL), D_MODEL ** -0.5),
        'norm_ffn2': 1.0 + _nrm(k[30], (DEPTH, D_MODEL), 0.02),
        'ffn2_wi': _nrm(k[31], (DEPTH, D_MODEL, 2 * D_FF), D_MODEL ** -0.5),
        'ffn2_wo': _nrm(k[32], (DEPTH, D_FF, D_MODEL), D_FF ** -0.5),
    }


def reference(x_prompt, x_sample, cache_k, cache_v, c, c_ctx, w_mod, b_mod, norm_ffn1, ffn1_wi, ffn1_wo,
              norm_mix, w_in, q_norm, k_norm, attn_sink, conv_w, conv_b, filt_w1, filt_b1, filt_w2, filt_b2,
              filt_w3, filt_b3, filt_freq, filt_decay, hyena_skip, w_attn_branch, w_hyena_branch, w_out,
              norm_ffn2, ffn2_wi, ffn2_wo):
    y_prompt = x_prompt
    y_sample = x_sample
    new_ks = []
    new_vs = []
    for l in range(DEPTH):
        p = {
            'norm_ffn1': norm_ffn1[l], 'ffn1_wi': ffn1_wi[l], 'ffn1_wo': ffn1_wo[l],
            'norm_mix': norm_mix[l], 'w_in': w_in[l], 'q_norm': q_norm[l], 'k_norm': k_norm[l],
            'attn_sink': attn_sink[l], 'conv_w': conv_w[l], 'conv_b': conv_b[l],
            'filt_w1': filt_w1[l], 'filt_b1': filt_b1[l], 'filt_w2': filt_w2[l], 'filt_b2': filt_b2[l],
            'filt_w3': filt_w3[l], 'filt_b3': filt_b3[l], 'filt_freq': filt_freq[l], 'filt_decay': filt_decay[l],
            'hyena_skip': hyena_skip[l], 'w_attn_branch': w_attn_branch[l], 'w_hyena_branch': w_hyena_branch[l],
            'w_out': w_out[l], 'norm_ffn2': norm_ffn2[l], 'ffn2_wi': ffn2_wi[l], 'ffn2_wo': ffn2_wo[l],
        }
        mod_ctx = jax.nn.silu(c_ctx) @ w_mod[l] + b_mod[l]
        mod_lat = (jax.nn.silu(c) @ w_mod[l] + b_mod[l])[:, None, :]
        y_prompt, (k_ctx, v_ctx) = _layer(y_prompt, mod_ctx, p, None)
        y_sample, _ = _layer(y_sample, mod_lat, p, (cache_k[:, l], cache_v[:, l]))
        new_ks.append(k_ctx)
        new_vs.append(v_ctx)
    new_k = jnp.stack(new_ks, axis=1)
    new_v = jnp.stack(new_vs, axis=1)
    return (y_prompt, y_sample, new_k, new_v)
```

```python
import math
from contextlib import ExitStack
import numpy as np
import ml_dtypes
import concourse.bass as bass
import concourse.mybir as mybir
from concourse.bass_utils import run_bass_kernel_spmd

F32 = mybir.dt.float32
BF16 = mybir.dt.bfloat16
AF = mybir.ActivationFunctionType
ALU = mybir.AluOpType
AX = mybir.AxisListType

D = 1024
NLAT = 4096
NCTX = 256
NTOK = NLAT + 2 * NCTX
NT = NTOK // 128
DFF = 2816
NH, NKV, HD = 16, 4, 64
PAST = 512
EPS = 1e-6
INCOLS = 6656
FW = 64
FE = 33
PI = math.pi


class Res:
    def __init__(self, name):
        self.name = name
        self.w = {}
        self.r = {}
        self.dsem = None
        self.ssem = None


class Tile(Res):
    def __init__(self, name, t):
        super().__init__(name)
        self.t = t

    def __getitem__(self, k):
        return self.t[k]


def _merge(d, s):
    for k, (sem, v) in s.items():
        if k not in d or d[k][1] < v:
            d[k] = (sem, v)


class Eng:
    def __init__(self, K, name, e):
        self.K = K
        self.name = name
        self.e = e
        self.sid, self.sem = K.new_sem(name)
        self.cnt = 0
        self.waited = {}

    def wait_for(self, deps, skip=None, include_self=False):
        for sid, (sem, val) in deps.items():
            if (sid == self.sid and not include_self) or sid == skip:
                continue
            if self.waited.get(sid, 0) >= val:
                continue
            self.e.wait_ge(sem, val)
            self.waited[sid] = val


class KB:
    def __init__(self, nc, es):
        self.nc = nc
        self.es = es
        self.sems = []
        self.free_sems = []
        self.PE = Eng(self, "pe", nc.tensor)
        self.ACT = Eng(self, "act", nc.scalar)
        self.DVE = Eng(self, "dve", nc.vector)
        self.POOL = Eng(self, "pool", nc.gpsimd)
        self.SP = Eng(self, "sp", nc.sync)
        self.engs = [self.PE, self.ACT, self.DVE, self.POOL, self.SP]
        self.dma_res = []
        self.uid = 0
        self.psum_t = es.enter_context(nc.psum_tensor("psum_all", [128, 4096], F32))
        self.banks = [Tile("bank%d" % i, self.psum_t) for i in range(8)]
        self.bank_set = list(range(8))
        self.bank_rr = 0

    def new_sem(self, name):
        s = self.es.enter_context(self.nc.semaphore("s_%s_%d" % (name, len(self.sems))))
        self.sems.append([s, 0])
        return len(self.sems) - 1, s

    def _alloc_dsem(self):
        if self.free_sems:
            return self.free_sems.pop()
        return self.new_sem("d")[0]

    def sb(self, name, shape, dtype, stack=None):
        self.uid += 1
        t = (stack or self.es).enter_context(self.nc.sbuf_tensor("%s_%d" % (name, self.uid), shape, dtype))
        return Tile(name, t)

    def set_banks(self, lst):
        self.bank_set = list(lst)
        self.bank_rr = 0

    def bank(self):
        b = self.bank_set[self.bank_rr % len(self.bank_set)]
        self.bank_rr += 1
        return b

    def bap(self, b, dtype=F32, nb=1):
        ap = self.psum_t[:, b * 512:(b + nb) * 512]
        if dtype == BF16:
            ap = ap.bitcast(BF16)
        return ap

    def group(self, E, reads, writes, fns):
        raw = {}
        for r in reads:
            _merge(raw, r.w)
        E.wait_for(raw, include_self=(E is not self.PE))
        deps = {}
        for r in writes:
            _merge(deps, r.w)
            _merge(deps, r.r)
        E.wait_for(deps)
        ins = None
        for fn in fns:
            ins = fn(E.e)
        E.cnt += 1
        ins.then_inc(E.sem, 1)
        ev = (E.sem, E.cnt)
        for r in reads:
            _merge(r.r, {E.sid: ev})
        for r in writes:
            r.w = {E.sid: ev}
            r.r = {}
        return ins

    def op(self, E, reads, writes, fn):
        return self.group(E, reads, writes, [fn])

    def dma(self, Q, out, in_, reads, writes, owner, store=False, **kw):
        if store:
            if owner.ssem is None:
                owner.ssem = self._alloc_dsem()
                self.dma_res.append(owner)
            sid = owner.ssem
        else:
            if owner.dsem is None:
                owner.dsem = self._alloc_dsem()
                self.dma_res.append(owner)
            sid = owner.dsem
        sem = self.sems[sid][0]
        raw = {}
        for r in reads:
            _merge(raw, r.w)
        Q.wait_for(raw, skip=sid, include_self=True)
        deps = {}
        for r in writes:
            _merge(deps, r.w)
            _merge(deps, r.r)
        Q.wait_for(deps, skip=sid)
        self.sems[sid][1] += 16
        ev = (sem, self.sems[sid][1])
        Q.e.dma_start(out=out, in_=in_, **kw).then_inc(sem, 16)
        for r in reads:
            _merge(r.r, {sid: ev})
        for r in writes:
            r.w = {sid: ev}
            r.r = {}

    def mm(self, bank, pairs, m, n, reads, off=0, first=True, last=True, nb=1, extra_writes=()):
        out = self.bap(bank, F32, nb)[0:m, off:off + n]
        nl = len(pairs) - 1
        fns = [(lambda e, l=l, r=r, i=i: e.matmul(out, lhsT=l, rhs=r, start=(first and i == 0), stop=(last and i == nl)))
               for i, (l, r) in enumerate(pairs)]
        self.group(self.PE, reads, [self.banks[bank + j] for j in range(nb)] + list(extra_writes), fns)

    def barrier(self):
        deps = {}
        for E in self.engs:
            if E.cnt > 0:
                deps[E.sid] = (E.sem, E.cnt)
        for res in self.dma_res:
            for sid in (res.dsem, res.ssem):
                if sid is not None:
                    deps[sid] = (self.sems[sid][0], self.sems[sid][1])
        self.DVE.wait_for(deps)
        self.DVE.cnt += 1
        self.nc.vector.memset(self.bar_t[:], 0.0).then_inc(self.DVE.sem, 1)
        ev = {self.DVE.sid: (self.DVE.sem, self.DVE.cnt)}
        for E in self.engs:
            E.wait_for(ev)
        for res in self.dma_res:
            for sid in (res.dsem, res.ssem):
                if sid is not None:
                    self.free_sems.append(sid)
            res.dsem = None
            res.ssem = None
        self.dma_res = []
        self.free_sems = sorted(set(self.free_sems))


def build_program(upto=99, debug=False):
    nc = bass.Bass("TRN2", target_bir_lowering=False)
    es = ExitStack()
    with es:
        K = KB(nc, es)
        K.bar_t = K.sb("bar", [1, 8], F32)
        _emit(nc, K, upto, debug)
        K.barrier()
    return nc


def _dram(nc, name, shape, dtype, kind):
    return nc.dram_tensor(name, list(shape), dtype, kind=kind).ap()


def _emit(nc, K, upto, debug):
    PE, ACT, DVE, POOL, SP = K.PE, K.ACT, K.DVE, K.POOL, K.SP
    ext = lambda name, shape, dt=F32: _dram(nc, name, shape, dt, "ExternalInput")
    outk = "ExternalOutput"
    scr = "ExternalOutput" if debug else "Internal"
    xin = ext("xin", [NTOK, D])
    cvec = ext("cvec", [2, D])
    cache_k = ext("cache_k", [PAST, NKV * HD])
    cache_v = ext("cache_v", [PAST, NKV * HD])
    w_mod = ext("w_mod", [D, 9 * D])
    b_mod = ext("b_mod", [9 * D])
    norms = ext("norms", [3, D])
    ffn_wi = [ext("ffn1_wi", [D, 2 * DFF]), ext("ffn2_wi", [D, 2 * DFF])]
    ffn_wo = [ext("ffn1_wo", [DFF, D]), ext("ffn2_wo", [DFF, D])]
    w_in = ext("w_in", [D, INCOLS])
    qk_norm = ext("qk_norm", [2, HD])
    attn_sink = ext("attn_sink", [NH])
    conv_w = ext("conv_w", [3, 3 * D])
    conv_b = ext("conv_b", [3 * D])
    filt_w1 = ext("filt_w1", [FE, FW])
    filt_b1 = ext("filt_b1", [FW])
    filt_w2 = ext("filt_w2", [FW, FW])
    filt_b2 = ext("filt_b2", [FW])
    filt_w3 = ext("filt_w3", [FW, 4 * D])
    filt_b3 = ext("filt_b3", [4 * D])
    filt_freq = ext("filt_freq", [FW])
    filt_decay = ext("filt_decay", [4 * D])
    hyena_skip = ext("hyena_skip", [2 * D])
    w_ab = ext("w_ab", [D, D])
    w_hb = ext("w_hb", [D, D])
    w_out = ext("w_out", [D, D])
    rope_c = ext("rope_c", [128, 32, 64])
    rope_s = ext("rope_s", [128, 32, 64])
    mats = {}
    for n_, tag in ((NLAT, "l"), (NCTX, "c")):
        nm_ = n_ // 256
        mats[n_] = {nm: ext("%s_%s" % (nm, tag), [nm_, 128, nm_, 128], BF16) for nm in ("ce", "se", "co", "so", "cot", "sot")}
    zemb = {NLAT: ext("zemb_l", [FE, NLAT]), NCTX: ext("zemb_c", [FE, NCTX])}
    tcol = {NLAT: ext("tcol_l", [128, 2, 16]), NCTX: ext("tcol_c", [128, 2, 1])}
    wk = {NLAT: ext("wk_l", [128, 2, 17]), NCTX: ext("wk_c", [128, 2, 2])}
    nyq = ext("nyq", [128, 128], BF16)
    nyqp = ext("nyqp", [128, 2], BF16)
    yout = _dram(nc, "yout", [NTOK, D], F32, outk)
    newk = _dram(nc, "newk", [2 * NCTX, NKV * HD], F32, outk)
    newv = _dram(nc, "newv", [2 * NCTX, NKV * HD], F32, outk)
    modd = _dram(nc, "modd", [2, 9 * D], F32, scr)
    x1s = _dram(nc, "x1s", [NTOK, D], F32, scr)
    x2s = _dram(nc, "x2s", [NTOK, D], F32, scr)
    hyT = _dram(nc, "hyT", [3 * D, NTOK], BF16, scr)
    qkvs = _dram(nc, "qkvs", [NTOK, 1536], F32, scr)
    sgT = _dram(nc, "sgT", [2 * D, NTOK], BF16, scr)
    attT = _dram(nc, "attT", [HD, NH, NTOK], BF16, scr)
    hyoT = _dram(nc, "hyoT", [D, NTOK], BF16, scr)
    kfs = {NLAT: _dram(nc, "kfs_l", [2, 17, 4, 128, D], BF16, scr), NCTX: _dram(nc, "kfs_c", [2, 2, 4, 128, D], BF16, scr)}
    R = {}

    def dres(name):
        if name not in R:
            R[name] = Res(name)
        return R[name]

    ident_f = K.sb("identf", [128, 128], F32)
    ident = K.sb("ident", [128, 128], BF16)
    K.op(POOL, [], [ident_f], lambda e: e.memset(ident_f[:], 0.0))
    K.op(POOL, [ident_f], [ident_f], lambda e: e.affine_select(out=ident_f[:], in_=ident_f[:], pattern=[[-1, 128]],
                                                        compare_op=ALU.not_equal, fill=1.0, base=0, channel_multiplier=1))
    K.op(POOL, [ident_f], [ident], lambda e: e.tensor_copy(out=ident[:], in_=ident_f[:]))
    mhalf = K.sb("mhalf", [128, 32], F32)
    K.op(POOL, [], [mhalf], lambda e: e.memset(mhalf[:], -0.5))
    ones_f = K.sb("onesf", [128, 128], F32)
    K.op(POOL, [], [ones_f], lambda e: e.memset(ones_f[:], 1.0))

    def group_of(tile):
        return 0 if tile < 32 else 1

    def load_w_bf16(stack, name, src, kchunks, ncols, col0=0, part=128):
        t = K.sb(name, [part, kchunks, ncols], BF16, stack)
        for k in range(kchunks):
            K.dma(POOL, t[:, k, :], src[k * part:(k + 1) * part, col0:col0 + ncols], [], [t], t)
        return t

    ps_w1 = ExitStack()
    pre_w1 = (load_w_bf16(ps_w1, "wgu", ffn_wi[0], 8, 2 * DFF), load_w_bf16(ps_w1, "wd", ffn_wo[0], 22, D))
    with ExitStack() as ps:
        cT = K.sb("cT", [128, 8, 2], F32, ps)
        with nc.allow_non_contiguous_dma(reason="tiny"):
            for g in range(2):
                K.dma(SP, cT[:, :, g], cvec[g].rearrange("(k p) -> p k", p=128), [], [cT], cT)
        sT = K.sb("sT", [128, 8, 2], F32, ps)
        K.op(ACT, [cT], [sT], lambda e: e.activation(out=sT[:], in_=cT[:], func=AF.Silu))
        nm = K.sb("nm", [2, 3 * D], F32, ps)
        K.dma(SP, nm[:], norms.rearrange("a d -> (a d)").partition_broadcast(2), [], [nm], nm)
        wmb = [K.sb("wm%d" % i, [128, 8, 512], F32, ps) for i in range(2)]
        bms = [K.sb("bm%d" % i, [2, 512], F32, ps) for i in range(2)]
        mos = [K.sb("mo%d" % i, [2, 512], F32, ps) for i in range(2)]
        for j in range(18):
            wt, bm, mo = wmb[j % 2], bms[j % 2], mos[j % 2]
            slot, half = j // 2, j % 2
            K.dma(SP, wt[:], w_mod[:, j * 512:(j + 1) * 512].rearrange("(k p) n -> p k n", p=128), [], [wt], wt)
            K.dma(SP, bm[:], b_mod[j * 512:(j + 1) * 512].partition_broadcast(2), [], [bm], bm)
            b = K.bank()
            K.mm(b, [(sT[:, k, :], wt[:, k, :]) for k in range(8)], 2, 512, [sT, wt])
            K.op(DVE, [K.banks[b], bm], [mo], lambda e, b=b, bm=bm, mo=mo: e.tensor_tensor(out=mo[:], in0=K.bap(b)[0:2, :], in1=bm[:], op=ALU.add))
            if slot % 3 == 1:
                i3 = slot // 3
                K.op(DVE, [mo, nm], [mo], lambda e, mo=mo, i3=i3, half=half: e.scalar_tensor_tensor(
                    out=mo[:], in0=mo[:], scalar=1.0, in1=nm[:, i3 * D + half * 512:i3 * D + (half + 1) * 512], op0=ALU.add, op1=ALU.mult))
            if slot in (2, 8):
                K.op(DVE, [mo], [mo], lambda e, mo=mo: e.tensor_scalar(out=mo[:], in0=mo[:], scalar1=0.5, scalar2=None, op0=ALU.mult))
            K.dma(ACT, modd[:, j * 512:(j + 1) * 512], mo[:], [mo], [dres("modd")], mo, store=True)
    K.barrier()
    if upto < 1:
        ps_w1.close()
        return

    def load_consts(tiles, i, g, which=(0, 1, 2)):
        for t, j in zip(tiles, which):
            K.dma(SP, t[:], modd[g, (3 * i + j) * D:(3 * i + j + 1) * D].partition_broadcast(128), [dres("modd")], [t], t)

    def load_w_bf16(stack, name, src, kchunks, ncols, col0=0, part=128):
        t = K.sb(name, [part, kchunks, ncols], BF16, stack)
        for k in range(kchunks):
            K.dma(POOL, t[:, k, :], src[k * part:(k + 1) * part, col0:col0 + ncols], [], [t], t)
        return t

    def norm_mod_T(bufs, xt, A, B, hT, col0, part=0):
        sq, ss, hb = bufs
        if part != 2:
            norm_mod_a(bufs, xt, A, B)
        if part != 1:
            norm_mod_b(bufs, hT, col0)

    def norm_mod_a(bufs, xt, A, B):
        sq, ss, hb = bufs
        K.op(ACT, [xt], [sq, ss], lambda e: e.activation(out=sq[:], in_=xt[:], func=AF.Square, accum_out=ss[:, 0:1]))
        K.op(DVE, [ss], [ss], lambda e: e.tensor_scalar(out=ss[:, 1:2], in0=ss[:, 0:1], scalar1=1.0 / D, scalar2=EPS,
                                                        op0=ALU.mult, op1=ALU.add))
        K.op(POOL, [ss, mhalf], [ss], lambda e: e.tensor_tensor(out=ss[:, 2:3], in0=ss[:, 1:2], in1=mhalf[:, 0:1], op=ALU.pow))
        K.op(DVE, [xt, ss, A], [sq], lambda e: e.scalar_tensor_tensor(out=sq[:], in0=xt[:], scalar=ss[:, 2:3], in1=A[:],
                                                                      op0=ALU.mult, op1=ALU.mult))
        K.op(POOL, [sq, B], [hb], lambda e: e.tensor_tensor(out=hb[:], in0=sq[:], in1=B[:], op=ALU.add))

    def norm_mod_b(bufs, hT, col0):
        sq, ss, hb = bufs
        b = K.bank()
        K.group(PE, [hb, ident], [K.banks[b]],
                [(lambda e, k=k: e.transpose(out=K.bap(b, BF16)[:, k * 128:(k + 1) * 128], in_=hb[:, k * 128:(k + 1) * 128],
                                             identity=ident[:])) for k in range(8)])
        K.op(ACT, [K.banks[b]], [hT],
             lambda e: e.activation(out=hT[:, :, col0:col0 + 128], in_=K.bap(b, BF16).rearrange("p (k t) -> p k t", k=8),
                                    func=AF.Copy))

    def ffn_phase(idx, src, dst, src_name, dst_name, pre=None):
        TC = 256
        NTC = TC // 128
        with ExitStack() as ps:
            cB, cA, cG = [K.sb("mc%d" % i, [128, D], F32, ps) for i in range(3)]
            if pre is not None:
                wgu, wd = pre
            else:
                wgu = load_w_bf16(ps, "wgu", ffn_wi[idx], 8, 2 * DFF)
                wd = load_w_bf16(ps, "wd", ffn_wo[idx], 22, D)
            xts = [K.sb("xt%d" % i, [128, D], F32, ps) for i in range(2)]
            xrs = [K.sb("xr%d" % i, [128, D], F32, ps) for i in range(2)]
            hTs = [K.sb("hT%d" % i, [128, 8, TC], BF16, ps) for i in range(2)]
            sq = K.sb("sq", [128, D], F32, ps)
            ss = K.sb("ss", [128, 4], F32, ps)
            hb = K.sb("hb", [128, D], BF16, ps)
            actT = K.sb("actT", [128, 22, TC], BF16, ps)
            sgs = [K.sb("sg%d" % i, [128, TC], F32, ps) for i in range(2)]
            tmp = [K.sb("tmp%d" % i, [128, 512], F32, ps) for i in range(2)]
            cnt = [0]

            hbs = [hb] + [K.sb("hbx%d" % i, [128, D], BF16, ps) for i in range(NTC - 1)]

            def prep_a(c):
                for t4 in range(NTC):
                    tile = c * NTC + t4
                    xt = xts[cnt[0] % 2]
                    cnt[0] += 1
                    K.dma(SP, xt[:], src[tile * 128:(tile + 1) * 128, :], [dres("%s%d" % (src_name, tile))], [xt], xt)
                    norm_mod_a((sq, ss, hbs[t4]), xt, cA, cB)

            def prep_b(c):
                hT = hTs[c % 2]
                for t4 in range(NTC):
                    norm_mod_b((sq, ss, hbs[t4]), hT, t4 * 128)

            def prep(c):
                prep_a(c)
                prep_b(c)

            def up(c, j0, j1):
                hT = hTs[c % 2]
                for j in range(j0, j1):
                    bg = K.bank()
                    K.mm(bg, [(wgu[:, k, j * 128:(j + 1) * 128], hT[:, k, :]) for k in range(8)], 128, TC, [wgu, hT])
                    bu = K.bank()
                    K.mm(bu, [(wgu[:, k, DFF + j * 128:DFF + (j + 1) * 128], hT[:, k, :]) for k in range(8)], 128, TC, [wgu, hT])
                    sg = sgs[j % 2]
                    K.op(ACT, [K.banks[bg]], [sg], lambda e, bg=bg, sg=sg: e.activation(out=sg[:], in_=K.bap(bg)[:, 0:TC], func=AF.Silu))
                    K.op(DVE, [K.banks[bu], sg], [actT],
                         lambda e, bu=bu, sg=sg, j=j: e.tensor_tensor(out=actT[:, j, :], in0=K.bap(bu)[:, 0:TC], in1=sg[:], op=ALU.mult))

            def down(c):
                for t4 in range(NTC):
                    tile = c * NTC + t4
                    xr = xrs[tile % 2]
                    K.dma(SP, xr[:], src[tile * 128:(tile + 1) * 128, :], [dres("%s%d" % (src_name, tile))], [xr], xr)
                    for nh in range(2):
                        by = K.bank()
                        K.mm(by, [(actT[:, j, t4 * 128:(t4 + 1) * 128], wd[:, j, nh * 512:(nh + 1) * 512]) for j in range(22)],
                             128, 512, [actT, wd])
                        tm = tmp[nh]
                        K.op(DVE, [K.banks[by], cG], [tm],
                             lambda e, by=by, tm=tm, nh=nh: e.tensor_tensor(out=tm[:], in0=K.bap(by), in1=cG[:, nh * 512:(nh + 1) * 512], op=ALU.mult))
                        K.op(POOL, [tm, xr], [xr],
                             lambda e, tm=tm, xr=xr, nh=nh: e.tensor_tensor(out=xr[:, nh * 512:(nh + 1) * 512], in0=xr[:, nh * 512:(nh + 1) * 512], in1=tm[:], op=ALU.add))
                    K.dma(POOL, dst[tile * 128:(tile + 1) * 128, :], xr[:], [xr], [dres("%s%d" % (dst_name, tile))], xr, store=True)

            for g, (c0, c1) in enumerate(((0, 32 // NTC), (32 // NTC, NT // NTC))):
                load_consts((cB, cA, cG), 0 if idx == 0 else 2, g)
                prep(c0)
                for c in range(c0, c1):
                    up(c, 0, 11)
                    if c + 1 < c1:
                        prep_a(c + 1)
                    up(c, 11, 22)
                    if c + 1 < c1:
                        prep_b(c + 1)
                    down(c)
        K.barrier()

    ffn_phase(0, xin, x1s, "xin", "x1s", pre=pre_w1)
    ps_w1.close()
    if upto < 2:
        return

    def p2():
        TC = 512
        with ExitStack() as ps:
            cB, cA = [K.sb("mc%d" % i, [128, D], F32, ps) for i in range(2)]
            wz = load_w_bf16(ps, "wz", w_in, 8, 5120, col0=1536)
            wq2 = load_w_bf16(ps, "wq2", w_in, 8, 1536, col0=0)
            qsts = [K.sb("qst%d" % i, [128, 1536], F32, ps) for i in range(2)]
            xts = [K.sb("xt%d" % i, [128, D], F32, ps) for i in range(2)]
            hTs = [K.sb("hT%d" % i, [128, 8, TC], BF16, ps) for i in range(2)]
            sq = K.sb("sq", [128, D], F32, ps)
            ss = K.sb("ss", [128, 4], F32, ps)
            hb = K.sb("hb", [128, D], BF16, ps)
            stg = [K.sb("stg%d" % i, [128, 8, TC], BF16, ps) for i in range(2)]
            cnt = [0, 0]

            hbs = [hb] + [K.sb("hbx%d" % i, [128, D], BF16, ps) for i in range(3)]

            def prep_a(c):
                for t4 in range(4):
                    tile = c * 4 + t4
                    xt = xts[cnt[0] % 2]
                    cnt[0] += 1
                    K.dma(SP, xt[:], x1s[tile * 128:(tile + 1) * 128, :], [dres("x1s%d" % tile)], [xt], xt)
                    norm_mod_a((sq, ss, hbs[t4]), xt, cA, cB)

            def prep_b(c):
                hT = hTs[c % 2]
                for t4 in range(4):
                    norm_mod_b((sq, ss, hbs[t4]), hT, t4 * 128)

            def prep(c):
                prep_a(c)
                prep_b(c)

            def body(c, m0, m1):
                hT = hTs[c % 2]
                for mg in range(m0, m1, 8):
                    st = stg[cnt[1] % 2]
                    cnt[1] += 1
                    for mi in range(8):
                        m = mg + mi
                        b = K.bank()
                        K.mm(b, [(wz[:, k, m * 128:(m + 1) * 128], hT[:, k, :]) for k in range(8)], 128, TC, [wz, hT])
                        if m < 24:
                            K.op(DVE, [K.banks[b]], [st], lambda e, b=b, st=st, mi=mi: e.tensor_copy(out=st[:, mi, :], in_=K.bap(b)))
                        else:
                            K.op(ACT, [K.banks[b]], [st], lambda e, b=b, st=st, mi=mi: e.activation(out=st[:, mi, :], in_=K.bap(b), func=AF.Sigmoid))
                    if mg < 24:
                        dst = hyT[mg * 128:(mg + 8) * 128, c * TC:(c + 1) * TC].rearrange("(m p) t -> p m t", p=128)
                        dr = dres("hyT%d" % c)
                    else:
                        dst = sgT[(mg - 24) * 128:(mg - 16) * 128, c * TC:(c + 1) * TC].rearrange("(m p) t -> p m t", p=128)
                        dr = dres("sgT%d" % c)
                    K.dma(POOL, dst, st[:], [st], [dr], st, store=True)

            nch = NT // 4
            load_consts((cB, cA), 1, 0, which=(0, 1))
            prep(0)
            for c in range(nch):
                body(c, 0, 16)
                if c + 1 < nch:
                    if c + 1 == 8:
                        load_consts((cB, cA), 1, 1, which=(0, 1))
                    prep_a(c + 1)
                body(c, 16, 40)
                hT_ = hTs[c % 2]
                for t4 in range(4):
                    tile = c * 4 + t4
                    qst = qsts[tile % 2]
                    for n in range(3):
                        b = K.bank()
                        K.mm(b, [(hT_[:, k, t4 * 128:(t4 + 1) * 128], wq2[:, k, n * 512:(n + 1) * 512]) for k in range(8)], 128, 512, [hT_, wq2])
                        if n == 1:
                            K.op(DVE, [K.banks[b]], [qst], lambda e, b=b, qst=qst, n=n: e.tensor_copy(out=qst[:, n * 512:(n + 1) * 512], in_=K.bap(b)))
                        else:
                            K.op(ACT, [K.banks[b]], [qst], lambda e, b=b, qst=qst, n=n: e.activation(out=qst[:, n * 512:(n + 1) * 512], in_=K.bap(b), func=AF.Copy))
                    K.dma(POOL, qkvs[tile * 128:(tile + 1) * 128, :], qst[:], [qst], [dres("qkvs%d" % tile)], qst, store=True)
                if c + 1 < nch:
                    prep_b(c + 1)
        K.barrier()

    p2()
    if upto < 3:
        return

    def p3():
        with ExitStack() as ps:
            cB, cA = [K.sb("mc%d" % i, [128, D], F32, ps) for i in range(2)]
            rc = K.sb("rc", [128, 32, 64], F32, ps)
            rs = K.sb("rs", [128, 32, 64], F32, ps)
            K.dma(SP, rc[:], rope_c, [], [rc], rc)
            K.dma(SP, rs[:], rope_s, [], [rs], rs)
            gq = K.sb("gq", [128, HD], F32, ps)
            gk = K.sb("gk", [128, HD], F32, ps)
            K.dma(SP, gq[:], qk_norm[0].partition_broadcast(128), [], [gq], gq)
            K.dma(SP, gk[:], qk_norm[1].partition_broadcast(128), [], [gk], gk)
            xts = [K.sb("xt%d" % i, [128, D], F32, ps) for i in range(2)]
            sq = K.sb("sq", [128, D], F32, ps)
            ss = K.sb("ss", [128, 4], F32, ps)
            hb = K.sb("hb", [128, D], BF16, ps)
            hT1 = [K.sb("hT1%d" % i, [128, 8, 128], BF16, ps) for i in range(2)]
            sqq = K.sb("sqq", [128, 1280], F32, ps)
            s1s = [K.sb("s1%d" % i, [128, 8], F32, ps) for i in range(3)]
            s2s = [K.sb("s2%d" % i, [128, 8], F32, ps) for i in range(3)]
            s3s = [K.sb("s3%d" % i, [128, 8], F32, ps) for i in range(3)]
            qf = K.sb("qf", [128, 1280], F32, ps)
            ta = K.sb("ta", [128, 1280], F32, ps)
            tb = K.sb("tb", [128, 1280], F32, ps)
            qb = K.sb("qb", [128, 1280], BF16, ps)
            vf = K.sb("vf", [128, 256], F32, ps)
            QTs = [K.sb("QT%d" % i, [128, 16, 128], BF16, ps) for i in range(3)]
            KT = K.sb("KT", [128, 4, NLAT], BF16, ps)
            V = K.sb("V", [128, 32, 4, 96], BF16, ps)
            cKT = K.sb("cKT", [128, 4, PAST], BF16, ps)
            cV = K.sb("cV", [128, 4, 4, 96], BF16, ps)
            ckb = K.sb("ckb", [128, 4, 256], BF16, ps)
            Pt = [K.sb("P%d" % i, [128, 512], BF16, ps) for i in range(4)]
            Osb = [K.sb("O%d" % i, [65, 512], F32, ps) for i in range(2)]
            rdens = [K.sb("rden%d" % i, [65, 512], F32, ps) for i in range(2)]
            rdbs = [K.sb("rdb%d" % i, [128, 512], BF16, ps) for i in range(2)]
            mones = K.sb("mones", [65, 512], F32, ps)
            K.op(POOL, [], [mones], lambda e: e.memset(mones[:], -1.0))
            ones_b = K.sb("ones_b", [128, 64], BF16, ps)
            K.op(POOL, [], [ones_b], lambda e: e.memset(ones_b[:], 1.0))
            aos = [K.sb("ao%d" % i, [64, 16, 128], BF16, ps) for i in range(2)]
            mprev = K.sb("mprev", [128, 128], BF16, ps)
            mnext = K.sb("mnext", [128, 128], BF16, ps)
            sinkexp = K.sb("sinkexp", [65, 16, 128], F32, ps)
            sk = K.sb("sk", [65, 16], F32, ps)
            K.op(POOL, [], [mprev], lambda e: e.memset(mprev[:], 1.0))
            K.op(POOL, [mprev], [mprev], lambda e: e.affine_select(out=mprev[:], in_=mprev[:], pattern=[[-1, 128]], compare_op=ALU.is_ge,
                                                              fill=0.0, base=0, channel_multiplier=1))
            K.op(POOL, [], [mnext], lambda e: e.memset(mnext[:], 1.0))
            K.op(POOL, [mnext], [mnext], lambda e: e.affine_select(out=mnext[:], in_=mnext[:], pattern=[[1, 128]], compare_op=ALU.is_ge,
                                                              fill=0.0, base=0, channel_multiplier=-1))
            mb_prev = K.sb("mb_prev", [128, 4, 128], BF16, ps)
            mb_next = K.sb("mb_next", [128, 4, 128], BF16, ps)
            for mb_, mk_ in ((mb_prev, mprev), (mb_next, mnext)):
                K.op(POOL, [mk_], [mb_], lambda e, mb_=mb_, mk_=mk_: e.tensor_scalar(out=mb_[:], in0=mk_[:].unsqueeze(1).to_broadcast([128, 4, 128]),
                                                                                  scalar1=-1.0, scalar2=30000.0, op0=ALU.add, op1=ALU.mult))
            K.op(POOL, [], [V], lambda e: e.memset(V[:], 0.0))
            K.op(POOL, [], [V], lambda e: e.memset(V[:, :, :, 64:65], 1.0))
            K.op(POOL, [], [cV], lambda e: e.memset(cV[:], 0.0))
            K.op(POOL, [], [cV], lambda e: e.memset(cV[:, :, :, 64:65], 1.0))
            K.op(POOL, [], [KT], lambda e: e.memset(KT[64:128], 0.0))
            K.op(POOL, [], [cKT], lambda e: e.memset(cKT[64:128], 0.0))
            for qt_ in QTs:
                K.op(POOL, [], [qt_], lambda e, qt_=qt_: e.memset(qt_[64:128], 0.0))
            for rb_ in rdbs:
                K.op(POOL, [], [rb_], lambda e, rb_=rb_: e.memset(rb_[:], 0.0))
            K.dma(SP, sk[64:65, :], attn_sink.rearrange("(o h) -> o h", o=1), [], [sk], sk)
            K.op(ACT, [sk], [sk], lambda e: e.activation(out=sk[64:65, :], in_=sk[64:65, :], func=AF.Exp))
            K.op(POOL, [sk], [sinkexp], lambda e: e.tensor_copy(out=sinkexp[64:65, :, :], in_=sk[64:65, :].unsqueeze(2).to_broadcast([1, 16, 128])))
            sel0 = K.sb("sel0", [1, 96], BF16, ps)
            K.op(POOL, [], [sel0], lambda e: e.memset(sel0[:], 0.0))
            K.op(POOL, [sel0], [sel0], lambda e: e.memset(sel0[0:1, 64:65], 1.0))
            sk0 = K.sb("sk0", [1, 16], F32, ps)
            K.dma(SP, sk0[:], attn_sink.rearrange("(o h) -> o h", o=1), [], [sk0], sk0)
            K.op(ACT, [sk0], [sk0], lambda e: e.activation(out=sk0[:], in_=sk0[:], func=AF.Exp))
            sinkb = K.sb("sinkb", [1, 16, 128], BF16, ps)
            K.op(POOL, [sk0], [sinkb], lambda e: e.tensor_copy(out=sinkb[:], in_=sk0[:].unsqueeze(2).to_broadcast([1, 16, 128])))
            K.dma(POOL, ckb[:], cache_k.rearrange("(t p) c -> p t c", p=128), [], [ckb], ckb)
            for t in range(4):
                K.dma(POOL, cV[:, t, :, 0:64], cache_v[t * 128:(t + 1) * 128, :].rearrange("p (g d) -> p g d", g=4), [], [cV], cV)
            for t in range(4):
                K.group(PE, [ckb, ident], [K.banks[2]],
                        [(lambda e, g=g, t=t: e.transpose(out=K.bap(2, BF16)[0:64, g * 128:(g + 1) * 128], in_=ckb[:, t, g * 64:(g + 1) * 64],
                                                          identity=ident[:])) for g in range(4)])
                K.op(ACT, [K.banks[2]], [cKT], lambda e, t=t: e.activation(out=cKT[0:64, :, t * 128:(t + 1) * 128],
                                                                           in_=K.bap(2, BF16)[0:64, 0:512].rearrange("p (g t) -> p g t", g=4), func=AF.Copy))
            xc = [0]

            def pre_norm(tile, part=0):
                xt = xts[tile % 2]
                h1 = hT1[tile % 2]
                if part != 2:
                    K.dma(SP, xt[:], x1s[tile * 128:(tile + 1) * 128, :], [dres("x1s%d" % tile)], [xt], xt)
                K.set_banks([3])
                norm_mod_T((sq, ss, hb), xt, cA, cB, h1, 0, part=part)

            qkts = [K.sb("qkt%d" % i, [128, 1536], F32, ps) for i in range(2)]

            def load_qkv(tile):
                qk = qkts[tile % 2]
                K.dma(SP, qk[:], qkvs[tile * 128:(tile + 1) * 128, :], [dres("qkvs%d" % tile)], [qk], qk)

            def prep(tile, qi, kslot, lat, ctx_row=None):
                qk = qkts[tile % 2]
                K.op(ACT, [qk], [sqq], lambda e: e.activation(out=sqq[:, 0:1280], in_=qk[:, 0:1280], func=AF.Square))
                K.op(ACT, [qk], [V], lambda e: e.activation(out=V[:, kslot, :, 0:64], in_=qk[:, 1280:1536].rearrange("p (g d) -> p g d", g=4), func=AF.Copy))
                yield
                segs = ((0, 8, 0), (512, 8, 1), (1024, 4, 2))
                for (c0, nh_, si) in segs:
                    a1, a2, a3 = s1s[si], s2s[si], s3s[si]
                    K.op(DVE, [sqq], [a1], lambda e, c0=c0, nh_=nh_, a1=a1: e.tensor_reduce(
                        out=a1[:, 0:nh_], in_=sqq[:, c0:c0 + nh_ * 64].rearrange("p (h d) -> p h d", d=64), axis=AX.X, op=ALU.add))
                    K.op(DVE, [a1], [a2], lambda e, nh_=nh_, a1=a1, a2=a2: e.tensor_scalar(out=a2[:, 0:nh_], in0=a1[:, 0:nh_], scalar1=1.0 / HD, scalar2=EPS,
                                                                                           op0=ALU.mult, op1=ALU.add))
                    K.op(POOL, [a2, mhalf], [a3], lambda e, nh_=nh_, a2=a2, a3=a3: e.tensor_tensor(out=a3[:, 0:nh_], in0=a2[:, 0:nh_], in1=mhalf[:, 0:nh_], op=ALU.pow))
                    src = qk[:, c0:c0 + nh_ * 64]
                    K.op(DVE, [qk, a3], [qf],
                         lambda e, c0=c0, nh_=nh_, a3=a3, src=src: e.tensor_tensor(
                             out=qf[:, c0:c0 + nh_ * 64].rearrange("p (h d) -> p h d", d=64), in0=src.rearrange("p (h d) -> p h d", d=64),
                             in1=a3[:, 0:nh_].unsqueeze(2).to_broadcast([128, nh_, 64]), op=ALU.mult))
                yield
                dstq = ta if lat else qb
                K.op(POOL, [qf, gq], [dstq], lambda e: e.tensor_tensor(out=dstq[:, 0:1024].rearrange("p (h d) -> p h d", d=64),
                                                                       in0=qf[:, 0:1024].rearrange("p (h d) -> p h d", d=64),
                                                                       in1=gq[:].unsqueeze(1).to_broadcast([128, 16, 64]), op=ALU.mult))
                dstk = ta if lat else tb
                K.op(POOL, [qf, gk], [dstk], lambda e: e.tensor_tensor(out=dstk[:, 1024:1280].rearrange("p (h d) -> p h d", d=64),
                                                                       in0=qf[:, 1024:1280].rearrange("p (h d) -> p h d", d=64),
                                                                       in1=gk[:].unsqueeze(1).to_broadcast([128, 4, 64]), op=ALU.mult))
                yield
                if lat:
                    v5 = lambda t_: t_[:].rearrange("p (h r x f) -> p h r x f", h=20, r=2, x=2, f=16)
                    cosb = rc[:, tile, :].unsqueeze(1).to_broadcast([128, 20, 64])
                    sn = rs[:, tile, :].rearrange("p (r x f) -> p r x f", r=2, x=2, f=16)
                    K.op(DVE, [ta, rc], [qf], lambda e: e.tensor_tensor(out=qf[:].rearrange("p (h d) -> p h d", d=64),
                                                                        in0=ta[:].rearrange("p (h d) -> p h d", d=64), in1=cosb, op=ALU.mult))
                    for x in range(2):
                        K.op(POOL if x == 0 else DVE, [ta, rs], [tb],
                             lambda e, x=x: e.tensor_tensor(out=v5(tb)[:, :, :, x, :], in0=v5(ta)[:, :, :, 1 - x, :],
                                                            in1=sn[:, :, x, :].unsqueeze(1).to_broadcast([128, 20, 2, 16]), op=ALU.mult))
                    K.op(DVE, [qf, tb], [qb], lambda e: e.tensor_tensor(out=qb[:], in0=qf[:], in1=tb[:], op=ALU.add))
                else:
                    K.dma(SP, newk[ctx_row:ctx_row + 128, :], tb[:, 1024:1280], [tb], [dres("newk%d" % ctx_row)], tb, store=True)
                    K.dma(SP, newv[ctx_row:ctx_row + 128, :], qk[:, 1280:1536], [qk], [dres("newv%d" % ctx_row)], qk, store=True)
                    K.op(POOL, [tb], [qb], lambda e: e.tensor_copy(out=qb[:, 1024:1280], in_=tb[:, 1024:1280]))
                yield
                QT = QTs[qi]
                for n in range(2):
                    K.group(PE, [qb, ident], [K.banks[n]],
                            [(lambda e, n=n, h=h: e.transpose(out=K.bap(n, BF16)[0:64, h * 128:(h + 1) * 128],
                                                              in_=qb[:, (n * 8 + h) * 64:(n * 8 + h + 1) * 64], identity=ident[:])) for h in range(8)])
                    K.op(DVE, [K.banks[n]], [QT],
                         (lambda e, n=n: e.tensor_copy(out=QT[0:64, n * 8:(n + 1) * 8, :], in_=K.bap(n, BF16)[0:64, :].rearrange("p (h t) -> p h t", h=8))))
                K.group(PE, [qb, ident], [K.banks[2]],
                        [(lambda e, g=g: e.transpose(out=K.bap(2, BF16)[0:64, g * 128:(g + 1) * 128],
                                                     in_=qb[:, 1024 + g * 64:1024 + (g + 1) * 64], identity=ident[:])) for g in range(4)])
                K.op(DVE, [K.banks[2]], [KT], lambda e: e.tensor_copy(out=KT[0:64, :, kslot * 128:(kslot + 1) * 128],
                                                                      in_=K.bap(2, BF16)[0:64, 0:512].rearrange("p (g t) -> p g t", g=4)))

            pc = [0]
            pp = [0]

            def attend(tile, qi, keys):
                QT = QTs[qi]
                ao = aos[tile % 2]
                nk = len(keys)
                pending = [None]

                def kv(g, key):
                    kind, slot, mask = key
                    if kind == 'c':
                        return cKT[:, g, slot * 128:(slot + 1) * 128], cV[:, slot, g, :], cKT, cV
                    return KT[:, g, slot * 128:(slot + 1) * 128], V[:, slot, g, :], KT, V

                for g in range(4):
                    sb_ = {}

                    def emit_S(i, g=g, sb_=sb_):
                        bs = (3, 4, 5, 7)[pc[0] % 4]
                        pc[0] += 1
                        sb_[i] = bs
                        kap, vap, kr, vr = kv(g, keys[i])
                        mk = keys[i][2]
                        pairs_ = [(kap, QT[:, 4 * g:4 * g + 4, :])]
                        rd_ = [kr, QT]
                        if mk is not None:
                            mbt = mb_prev if mk is mprev else mb_next
                            pairs_.append((ident[:], mbt[:].rearrange("p h t -> p (h t)")))
                            rd_ += [ident, mbt]
                        K.mm(bs, pairs_, 128, 512, rd_)

                    emit_S(0)
                    if nk > 1:
                        emit_S(1)
                    if nk > 2:
                        emit_S(2)
                    for i in range(nk):
                        if i + 3 < nk:
                            emit_S(i + 3)
                        bs = sb_[i]
                        P = Pt[pp[0] % 4]
                        pp[0] += 1
                        kap, vap, kr, vr = kv(g, keys[i])
                        mask = keys[i][2]
                        K.op(ACT, [K.banks[bs]], [P], lambda e, bs=bs, P=P: e.activation(out=P[:], in_=K.bap(bs), func=AF.Exp, scale=0.125))
                        K.mm(6, [(vap, P[:])], 96, 512, [vr, P], first=(i == 0), last=False)
                        if i == nk - 1:
                            K.mm(6, [(sel0[0:1, :], sinkb[0:1, 4 * g:4 * g + 4, :].rearrange("p h t -> p (h t)"))], 96, 512, [sel0, sinkb], first=False, last=True)
                        if i == 1 and pending[0] is not None:
                            pending[0]()
                            pending[0] = None
                    O = Osb[g % 2]
                    rd = rdens[g % 2]
                    K.op(ACT, [K.banks[6]], [rd], lambda e, rd=rd: e.activation(out=rd[64:65, :], in_=K.bap(6)[64:65, :], func=AF.Ln))
                    rdb = rdbs[g % 2]
                    K.op(ACT, [rd], [rdb], lambda e, rd=rd, rdb=rdb: e.activation(out=rdb[64:65, :], in_=rd[64:65, :], func=AF.Exp, scale=-1.0))
                    K.op(ACT, [K.banks[6]], [O], lambda e, O=O: e.activation(out=O[0:64, :], in_=K.bap(6)[0:64, :], func=AF.Copy))

                    def fin(O=O, rdb=rdb, g=g):
                        K.mm(2, [(ones_b[:, 0:64], rdb[:, :])], 64, 512, [ones_b, rdb])
                        K.op(DVE, [O, K.banks[2]], [ao], lambda e: e.tensor_tensor(
                            out=ao[:, 4 * g:4 * g + 4, :].rearrange("p h t -> p (h t)"), in0=O[0:64, :], in1=K.bap(2)[0:64, :], op=ALU.mult))
                    pending[0] = fin
                    if g < 3:
                        yield
                pending[0]()
                K.dma(SP, attT[:, :, tile * 128:(tile + 1) * 128], ao[:], [ao], [dres("attT%d" % (tile // 4))], ao, store=True)

            load_consts((cB, cA), 1, 0, which=(0, 1))
            ck = [('c', t, None) for t in range(4)]

            def lat_keys(j):
                keys = list(ck)
                if j >= 1:
                    keys.append(('l', j - 1, mprev))
                keys.append(('l', j, None))
                if j + 1 < 32:
                    keys.append(('l', j + 1, mnext))
                return keys

            def step(gen):
                if gen is not None:
                    next(gen, None)

            load_qkv(0)
            for i in range(34):
                j = i - 2
                P = prep(i, i % 3, i, True) if i < 32 else None
                A = attend(j, j % 3, lat_keys(j)) if j >= 0 else None
                step(P)
                step(A)
                step(P)
                step(A)
                step(P)
                if i + 1 < 32:
                    load_qkv(i + 1)
                step(A)
                step(P)
                step(A)
                step(P)
            load_consts((cB, cA), 1, 1, which=(0, 1))
            for sq_ in range(2):
                for t in range(2):
                    tl = 32 + 2 * sq_ + t
                    load_qkv(tl)
                    for _ in prep(tl, t, t, False, ctx_row=sq_ * 256 + t * 128):
                        pass
                for t in range(2):
                    for _ in attend(32 + 2 * sq_ + t, t, [('l', 0, None), ('l', 1, None)]):
                        pass
            K.set_banks(range(8))
        K.barrier()

    p3()
    if upto < 4:
        return


    def hyena_phases():
        nyq_t = K.sb("nyq_t", [128, 128], BF16)
        nyqp_t = K.sb("nyqp_t", [128, 2], BF16)
        K.dma(SP, nyq_t[:], nyq, [], [nyq_t], nyq_t)
        K.dma(SP, nyqp_t[:], nyqp, [], [nyqp_t], nyqp_t)
        for n in (NLAT, NCTX):
            filters(n, nyqp_t)
        def run_gens(gens):
            active = list(gens)
            while active:
                for g_ in list(active):
                    if next(g_, "end") == "end":
                        active.remove(g_)

        with ExitStack() as ps_:
            run_gens([hyconv(0, NLAT, nyq_t, nyqp_t, ps_)])
            K.barrier()
        with ExitStack() as ps_:
            run_gens([hyconv(NLAT + sq_ * NCTX, NCTX, nyq_t, nyqp_t, ps_) for sq_ in range(2)])
            K.barrier()

    def filters(n, nyqp_t):
        CH = min(512, n)
        nch = n // CH
        with ExitStack() as ps:
            w1 = K.sb("fw1", [FE, FW], F32, ps)
            w2 = K.sb("fw2", [FW, FW], F32, ps)
            fr = K.sb("ffr", [FW, 4], F32, ps)
            K.dma(SP, w1[:], filt_w1, [], [w1], w1)
            K.dma(SP, w2[:], filt_w2, [], [w2], w2)
            with nc.allow_non_contiguous_dma(reason="tiny"):
                K.dma(SP, fr[:, 0:1], filt_freq.rearrange("(p o) -> p o", o=1), [], [fr], fr)
                K.dma(SP, fr[:, 1:2], filt_b1.rearrange("(p o) -> p o", o=1), [], [fr], fr)
                K.dma(SP, fr[:, 2:3], filt_b2.rearrange("(p o) -> p o", o=1), [], [fr], fr)
            sc3 = K.sb("sc3", [FW, 4], F32, ps)
            K.op(DVE, [fr], [sc3], lambda e: e.tensor_scalar(out=sc3[:, 0:1], in0=fr[:, 0:1], scalar1=1.0 / 3.0, scalar2=None, op0=ALU.mult))
            K.op(DVE, [fr, sc3], [sc3], lambda e: e.tensor_tensor(out=sc3[:, 1:2], in0=fr[:, 1:2], in1=sc3[:, 0:1], op=ALU.mult))
            K.op(DVE, [fr, sc3], [sc3], lambda e: e.tensor_tensor(out=sc3[:, 2:3], in0=fr[:, 2:3], in1=sc3[:, 0:1], op=ALU.mult))
            h2T = K.sb("h2T", [FW, n], F32, ps)
            h2Tb = K.sb("h2Tb", [FW, n], BF16, ps)
            ps2 = ExitStack()
            zT = K.sb("zT", [FE, n], F32, ps2)
            K.dma(SP, zT[:], zemb[n], [], [zT], zT)
            h1T = K.sb("h1T", [FW, n], F32, ps2)
            ts = K.sb("ts", [FW, CH], F32, ps2)
            tu = K.sb("tu", [FW, CH], F32, ps2)
            for (wm, kdim, src, dst, bcol) in ((w1, FE, zT, h1T, 1), (w2, FW, h1T, h2T, 2)):
                for ch in range(nch):
                    b = K.bank()
                    K.mm(b, [(wm[0:kdim, :], src[0:kdim, ch * CH:(ch + 1) * CH])], FW, CH, [wm, src])
                    K.op(ACT, [K.banks[b], sc3], [ts], lambda e, b=b, bcol=bcol: e.activation(out=ts[:], in_=K.bap(b)[0:FW, 0:CH], func=AF.Sin,
                                                                                             bias=sc3[:, bcol:bcol + 1], scale=sc3[:, 0:1]))
                    K.op(DVE, [ts], [tu], lambda e: e.tensor_tensor(out=tu[:], in0=ts[:], in1=ts[:], op=ALU.mult))
                    K.op(DVE, [tu], [tu], lambda e: e.tensor_scalar(out=tu[:], in0=tu[:], scalar1=-4.0, scalar2=3.0, op0=ALU.mult, op1=ALU.add))
                    K.op(DVE, [ts, tu], [dst], lambda e, dst=dst, ch=ch: e.tensor_tensor(out=dst[:, ch * CH:(ch + 1) * CH], in0=ts[:], in1=tu[:], op=ALU.mult))
            K.op(ACT, [h2T], [h2Tb], lambda e: e.activation(out=h2Tb[:], in_=h2T[:], func=AF.Copy))
            K.barrier()
            ps2.close()
            nmt = n // 256
            CS = min(8, nmt)
            nq = nmt // CS
            M_ = mats[n]
            tc_ = K.sb("tc", [128, 2, nmt], F32, ps)
            K.dma(SP, tc_[:], tcol[n], [], [tc_], tc_)
            wkt = K.sb("wkt", [128, 2, nmt + 1], F32, ps)
            K.dma(SP, wkt[:], wk[n], [], [wkt], wkt)
            w3o = K.sb("w3o", [FW, 2, 512], BF16, ps)
            Eabs = [K.sb("Eab%d" % i, [128, 2, 512], BF16, ps) for i in range(2)]
            ones_bf = K.sb("ones_bf", [128, 2], BF16, ps)
            K.op(POOL, [], [ones_bf], lambda e: e.memset(ones_bf[:], 1.0))
            b3b = K.sb("b3b", [128, 2, 512], F32, ps)
            ndb = K.sb("ndb", [128, 2, 512], F32, ps)
            skb = K.sb("skb", [128, 512], F32, ps)
            FG = K.sb("FG", [128, 2, nmt, 2, 512], BF16, ps)
            FGp = [FG, Tile("FGg", FG.t)]
            hvs = [K.sb("hv%d" % i, [128, 2, 512], F32, ps) for i in range(2)]
            Ees = [K.sb("Ee%d" % i, [128, 2, 512], F32, ps) for i in range(2)]
            nrm = K.sb("nrm", [1, 512], F32, ps)
            rb = K.sb("rb", [128, 512], F32, ps)
            NR = 4
            mbuf = {nm: [K.sb("m%s%d" % (nm, i), [128, CS, 128], BF16, ps) for i in range(NR)] for nm in ("ce", "co", "se", "so")}
            e4 = K.sb("e4", [128, 4, 512], F32, ps)
            ksts = [K.sb("kst%d" % i, [128, 4, 512], F32, ps) for i in range(2)]
            kstb = [K.sb("kstb%d" % i, [128, 4, 512], BF16, ps) for i in range(2)]
            mc = [0]
            f2 = lambda t_: t_[:].rearrange("p a c -> p (a c)")
            for o in range(2):
              for chf in range(2):
                for d_ in range(2):
                    c0 = o * 2048 + d_ * 1024 + chf * 512
                    K.dma(POOL, w3o[:, d_, :], filt_w3[:, c0:c0 + 512], [], [w3o], w3o)
                    K.dma(SP, b3b[:, d_, :], filt_b3[c0:c0 + 512].partition_broadcast(128), [], [b3b], b3b)
                    K.dma(SP, ndb[:, d_, :], filt_decay[c0:c0 + 512].partition_broadcast(128), [], [ndb], ndb)
                K.dma(SP, skb[:], hyena_skip[o * 1024 + chf * 512:o * 1024 + (chf + 1) * 512].partition_broadcast(128), [], [skb], skb)
                K.op(DVE, [ndb], [ndb], lambda e: e.scalar_tensor_tensor(out=f2(ndb), in0=f2(ndb), scalar=-1.0, in1=f2(ndb), op0=ALU.mult, op1=ALU.min))
                K.set_banks([1, 2, 3, 4, 5, 6, 7])
                first_acc = [True]
                steps = [(par, tile) for par in range(2) for tile in range(nmt)]

                def front(i):
                    par, tile = steps[i]
                    hv, Ee = hvs[i % 2], Ees[i % 2]
                    K.op(ACT, [ndb, tc_], [Ee], lambda e: e.activation(out=f2(Ee), in_=f2(ndb), func=AF.Exp, scale=tc_[:, par, tile:tile + 1]))
                    tok = h2Tb[:, 256 * tile + par:256 * tile + 256:2]
                    for d_ in range(2):
                        b = K.bank()
                        K.mm(b, [(tok, w3o[:, d_, :])], 128, 512, [h2Tb, w3o])
                        K.op(DVE, [K.banks[b], b3b], [hv], lambda e, b=b, d_=d_: e.tensor_tensor(out=hv[:, d_, :], in0=K.bap(b), in1=b3b[:, d_, :], op=ALU.add))

                def back(i):
                    par, tile = steps[i]
                    hv, Ee = hvs[i % 2], Ees[i % 2]
                    last_t = (i == len(steps) - 1)
                    K.op(POOL, [hv, Ee], [hv], lambda e: e.tensor_tensor(out=hv[:, 0, :], in0=hv[:, 0, :], in1=Ee[:, 0, :], op=ALU.mult))
                    K.op(DVE, [hv, Ee], [hv], lambda e: e.tensor_tensor(out=hv[:, 1, :], in0=hv[:, 1, :], in1=Ee[:, 1, :], op=ALU.mult))
                    Eab = Eabs[i % 2]
                    K.op(ACT, [hv], [Eab], lambda e: e.activation(out=f2(Eab), in_=f2(hv), func=AF.Abs))
                    for d_ in range(2):
                        K.mm(0, [(ones_bf[:, 0:1], Eab[:, d_, :])], 1, 512, [Eab, ones_bf], first=first_acc[0], last=(last_t and d_ == 1))
                        first_acc[0] = False
                    if i == 0:
                        K.op(POOL, [Eab], [hv], lambda e: e.memset(hv[0:1, 1, :], 0.0))
                    K.op(DVE, [hv], [FGp[0]], lambda e: e.tensor_tensor(out=FG[:, par, tile, 0, :], in0=hv[:, 0, :], in1=hv[:, 1, :], op=ALU.add))
                    K.op(POOL, [hv], [FGp[1]], lambda e: e.tensor_tensor(out=FG[:, par, tile, 1, :], in0=hv[:, 1, :], in1=hv[:, 0, :], op=ALU.subtract))

                for i in range(len(steps) + 1):
                    if i < len(steps):
                        front(i)
                    if i >= 1:
                        back(i - 1)
                K.op(DVE, [K.banks[0]], [nrm], lambda e: e.tensor_scalar(out=nrm[:], in0=K.bap(0)[0:1, :], scalar1=EPS, scalar2=None, op0=ALU.add))
                K.op(DVE, [nrm], [nrm], lambda e: e.reciprocal(out=nrm[:], in_=nrm[:]))
                K.mm(1, [(ones_f[0:1, :], nrm[0:1, :])], 128, 512, [ones_f, nrm])
                K.op(ACT, [K.banks[1]], [rb], lambda e: e.activation(out=rb[:], in_=K.bap(1), func=AF.Copy))

                def finalize(kst, rows, kt):
                    r = slice(0, rows)
                    K.op(DVE, [e4], [kst], lambda e: e.tensor_tensor(out=kst[r, 0, :], in0=e4[r, 0, :], in1=e4[r, 1, :], op=ALU.add))
                    K.op(POOL, [e4], [kst], lambda e: e.tensor_tensor(out=kst[r, 2, :], in0=e4[r, 0, :], in1=e4[r, 1, :], op=ALU.subtract))
                    K.op(DVE, [e4], [kst], lambda e: e.tensor_tensor(out=kst[r, 1, :], in0=e4[r, 2, :], in1=e4[r, 3, :], op=ALU.add))
                    K.op(POOL, [e4], [kst], lambda e: e.tensor_tensor(out=kst[r, 3, :], in0=e4[r, 3, :], in1=e4[r, 2, :], op=ALU.subtract))
                    K.op(DVE, [kst, rb], [kst], lambda e: e.tensor_tensor(out=kst[r], in0=kst[r], in1=rb[r].unsqueeze(1).to_broadcast([rows, 4, 512]), op=ALU.mult))
                    K.op(POOL, [kst, skb], [kst], lambda e: e.tensor_tensor(out=kst[r, 0, :], in0=kst[r, 0, :], in1=skb[r], op=ALU.add))
                    K.op(POOL, [kst, skb], [kst], lambda e: e.tensor_tensor(out=kst[r, 2, :], in0=kst[r, 2, :], in1=skb[r], op=ALU.add))
                    kb = kstb[kt % 2]
                    K.op(ACT, [kst, wkt], [kb], lambda e: e.activation(out=kb[r, 0:2, :].rearrange("p a c -> p (a c)"), in_=kst[r, 0:2, :].rearrange("p a c -> p (a c)"),
                                                                      func=AF.Copy, scale=wkt[r, 0, kt:kt + 1]))
                    K.op(ACT, [kst, wkt], [kb], lambda e: e.activation(out=kb[r, 2:4, :].rearrange("p a c -> p (a c)"), in_=kst[r, 2:4, :].rearrange("p a c -> p (a c)"),
                                                                      func=AF.Copy, scale=wkt[r, 1, kt:kt + 1]))
                    K.dma(ACT, kfs[n][o, kt, :, r, chf * 512:(chf + 1) * 512].rearrange("a p c -> p a c"), kb[r], [kb], [dres("kfs%d_%d_%d_%d" % (n, o, kt, chf))], kb, store=True)

                K.set_banks(range(8))
                grp = (("ce", 0, 0), ("co", 1, 0), ("se", 0, 1), ("so", 1, 1))
                for kt in range(nmt):
                    base = 4 * (kt % 2)
                    for q in range(nq):
                        bufs = {}
                        for nm, par, pl in grp:
                            mb = mbuf[nm][mc[0] % NR]
                            K.dma(SP, mb[:], M_[nm][kt, :, q * CS:(q + 1) * CS, :], [], [mb], mb)
                            bufs[nm] = mb
                        mc[0] += 1
                        for gi, (nm, par, pl) in enumerate(grp):
                            mb = bufs[nm]
                            K.mm(base + gi, [(mb[:, i, :], FG[:, par, q * CS + i, pl, :]) for i in range(CS)], 128, 512, [mb, FGp[pl]], first=(q == 0), last=(q == nq - 1))
                    K.op(ACT, [K.banks[base + i] for i in range(4)], [e4], lambda e, base=base: e.activation(out=e4[:].rearrange("p a c -> p (a c)"), in_=K.bap(base, nb=4), func=AF.Copy))
                    finalize(ksts[kt % 2], 128, kt)
                K.op(POOL, [], [e4], lambda e: e.memset(e4[0:1].rearrange("p a c -> p (a c)"), 0.0))
                K.mm(0, [(nyqp_t[:, 0:1], FG[:, 0, mt, 0, :]) for mt in range(nmt)], 1, 512, [nyqp_t, FGp[0]])
                K.mm(1, [(nyqp_t[:, 0:1], FG[:, 1, mt, 1, :]) for mt in range(nmt)], 1, 512, [nyqp_t, FGp[1]])
                K.op(ACT, [K.banks[0]], [e4], lambda e: e.activation(out=e4[0:1, 0, :], in_=K.bap(0)[0:1, :], func=AF.Copy))
                K.op(ACT, [K.banks[1]], [e4], lambda e: e.activation(out=e4[0:1, 3, :], in_=K.bap(1)[0:1, :], func=AF.Copy))
                finalize(ksts[nmt % 2], 1, nmt)
            K.set_banks(range(8))
        K.barrier()

    hw = [0]

    def hyconv(tok0, n, nyq_t, nyqp_t, ps):
        nmt = n // 256
        CS = min(8, nmt)
        nq = nmt // CS
        M_ = mats[n]
        nh2 = max(n // 2, 128)
        if True:
            cw = K.sb("cw", [128, 24, 3], F32, ps)
            cb = K.sb("cb", [128, 24], F32, ps)
            with nc.allow_non_contiguous_dma(reason="tiny"):
                for j in range(3):
                    K.dma(SP, cw[:, :, j], conv_w[j].rearrange("(ct p) -> p ct", p=128), [], [cw], cw)
                K.dma(SP, cb[:], conv_b.rearrange("(ct p) -> p ct", p=128), [], [cb], cb)
            WM = 384 if n == NLAT else 512
            NJM = WM // 128
            u = K.sb("u", [128, n + 2], BF16, ps)
            K.op(POOL, [], [u], lambda e: e.memset(u[:], 0.0))
            acc = K.sb("acc", [128, nh2], F32, ps)
            fTs = [K.sb("fT%d" % i, [128, NJM, n], BF16, ps) for i in range(2)]
            z = K.sb("z", [128, 2, nmt, WM], BF16, ps)
            Y = K.sb("Y", [128, nmt, 4, WM], BF16, ps)
            Yp = [Tile("Yp%d" % i, Y.t) for i in range(4)]
            Yx = K.sb("Yx", [1, 2, WM], BF16, ps)
            kfx = K.sb("kfx", [1, 4, WM], BF16, ps)
            tx = [K.sb("tx%d" % i, [1, WM], F32, ps) for i in range(3)]
            mnames = ("ce", "co", "se", "so", "cot", "sot")
            mbuf = {nm: [K.sb("m%s%d" % (nm, i), [128, CS, 128], BF16, ps) for i in range(3)] for nm in ("ce", "co", "se", "so")}
            mbuf["cot"] = mbuf["co"]
            mbuf["sot"] = mbuf["so"]
            kft = [K.sb("kft%d" % i, [128, 4, WM], BF16, ps) for i in range(2)]
            e4s = [K.sb("e4%d" % i, [128, 4, WM], BF16, ps) for i in range(2)]
            tq = [K.sb("tq%d" % i, [128, WM], BF16, ps) for i in range(8)]
            xtl = [K.sb("xtl%d" % i, [128, WM], BF16, ps) for i in range(2)]
            hyt = [K.sb("hyt%d" % i, [128, WM], BF16, ps) for i in range(2)]
            hst = [K.sb("hst%d" % i, [128, NJM, 256], BF16, ps) for i in range(2)]
            mc = [0]
            hres = [dres("hyT%d" % c) for c in range(tok0 // 512, (tok0 + n + 511) // 512)]
            chunks = ((0, 3), (3, 3), (6, 2)) if n == NLAT else ((0, 4), (4, 4))

            def conv_steps(ci, blk):
                j0_, nj_ = chunks[ci]
                fT_ = fTs[ci % 2]
                for j in range(nj_):
                    ct = blk * 8 + j0_ + j
                    K.dma(SP, u[:, 1:n + 1], hyT[ct * 128:(ct + 1) * 128, tok0:tok0 + n], hres, [u], u)
                    for hh in range(n // nh2):
                        r0 = hh * nh2
                        K.op(DVE, [u, cw, cb], [acc], lambda e, ct=ct, r0=r0: e.tensor_scalar(out=acc[:], in0=u[:, 1 + r0:1 + r0 + nh2], scalar1=cw[:, ct, 1:2], scalar2=cb[:, ct:ct + 1],
                                                                                          op0=ALU.mult, op1=ALU.add))
                        K.op(DVE, [u, acc, cw], [acc], lambda e, ct=ct, r0=r0: e.scalar_tensor_tensor(out=acc[:], in0=u[:, r0:r0 + nh2], scalar=cw[:, ct, 0:1], in1=acc[:],
                                                                                                  op0=ALU.mult, op1=ALU.add))
                        K.op(DVE, [u, acc, cw], [fT_], lambda e, ct=ct, j=j, r0=r0: e.scalar_tensor_tensor(out=fT_[:, j, r0:r0 + nh2], in0=u[:, 2 + r0:2 + r0 + nh2], scalar=cw[:, ct, 2:3],
                                                                                                       in1=acc[:], op0=ALU.mult, op1=ALU.add))
                    yield

            def ftile_(ci, tau, par, dst_ap, dst_res, bsel):
                j0_, nj_ = chunks[ci]
                fT_ = fTs[ci % 2]
                b = 6 + (bsel % 2)
                K.group(PE, [fT_, ident], [K.banks[b]],
                        [(lambda e, j=j: e.transpose(out=K.bap(b, BF16)[:, j * 128:(j + 1) * 128], in_=fT_[:, j, 256 * tau + par:256 * tau + 256:2], identity=ident[:]))
                         for j in range(nj_)])
                K.op(ACT, [K.banks[b]], [dst_res], lambda e: e.activation(out=dst_ap, in_=K.bap(b, BF16)[:, 0:nj_ * 128], func=AF.Copy))

            def prologue(ci):
                j0_, nj_ = chunks[ci]
                yield from conv_steps(ci, 2)
                for tau in range(nmt):
                    for par in range(2):
                        ftile_(ci, tau, par, z[:, par, tau, 0:nj_ * 128], z, 2 * tau + par)
                    yield
                yield from conv_steps(ci, 0)

            yield from prologue(0)
            for ci, (j0, nj) in enumerate(chunks):
                W = nj * 128
                c0 = j0 * 128
                fT = fTs[ci % 2]
                nxt = [None]

                def ftile(tau, par, dst_ap, dst_res, bsel, ci=ci):
                    ftile_(ci, tau, par, dst_ap, dst_res, bsel)


                fgrp = (("ce", 0), ("co", 1), ("se", 0), ("so", 1))
                for o in range(2):
                    for kt in range(nmt):
                        base = 4 * (kt % 2)
                        for q in range(nq):
                            bufs = {}
                            for nm, par in fgrp:
                                mb = mbuf[nm][mc[0] % 3]
                                K.dma(SP, mb[:], M_[nm][kt, :, q * CS:(q + 1) * CS, :], [], [mb], mb)
                                bufs[nm] = mb
                            mc[0] += 1
                            for gi, (nm, par) in enumerate(fgrp):
                                mb = bufs[nm]
                                K.mm(base + gi, [(mb[:, i, :], z[:, par, q * CS + i, 0:W]) for i in range(CS)], 128, W, [mb, z], first=(q == 0), last=(q == nq - 1))
                        kf = kft[kt % 2]
                        K.dma(SP, kf[:, :, 0:W], kfs[n][o, kt, :, :, c0:c0 + W].rearrange("a p c -> p a c"), [dres("kfs%d_%d_%d_%d" % (n, o, kt, ch_)) for ch_ in range(2)], [kf], kf)
                        e4 = e4s[kt % 2]
                        for gi in range(4):
                            K.op(ACT, [K.banks[base + gi]], [e4], lambda e, base=base, gi=gi, e4=e4: e.activation(out=e4[:, gi, 0:W], in_=K.bap(base + gi)[:, 0:W], func=AF.Copy))
                        Ec, Oc, Es, Os = (e4[:, i, 0:W] for i in range(4))
                        t = [tq[i][:, 0:W] for i in range(8)]
                        tr = tq
                        TT = lambda E_, o_, a_, b_, op_, rd, wr: K.op(E_, rd, wr, lambda e: e.tensor_tensor(out=o_, in0=a_, in1=b_, op=op_))
                        TT(DVE, t[0], Ec, Oc, ALU.add, [e4], [tr[0]])
                        TT(POOL, t[1], Ec, Oc, ALU.subtract, [e4], [tr[1]])
                        TT(DVE, t[2], Es, Os, ALU.add, [e4], [tr[2]])
                        TT(POOL, t[3], Os, Es, ALU.subtract, [e4], [tr[3]])
                        KreA, KimA, KreB, KimB = (kf[:, i, 0:W] for i in range(4))
                        TT(DVE, t[4], t[0], KreA, ALU.mult, [tr[0], kf], [tr[4]])
                        TT(DVE, t[5], t[2], KimA, ALU.mult, [tr[2], kf], [tr[5]])
                        TT(DVE, t[4], t[4], t[5], ALU.add, [tr[4], tr[5]], [tr[4]])
                        TT(DVE, t[5], t[2], KreA, ALU.mult, [tr[2], kf], [tr[5]])
                        TT(DVE, t[0], t[0], KimA, ALU.mult, [tr[0], kf], [tr[0]])
                        TT(DVE, t[5], t[5], t[0], ALU.subtract, [tr[5], tr[0]], [tr[5]])
                        TT(POOL, t[6], t[1], KreB, ALU.mult, [tr[1], kf], [tr[6]])
                        TT(POOL, t[7], t[3], KimB, ALU.mult, [tr[3], kf], [tr[7]])
                        TT(POOL, t[6], t[6], t[7], ALU.add, [tr[6], tr[7]], [tr[6]])
                        TT(POOL, t[7], t[3], KreB, ALU.mult, [tr[3], kf], [tr[7]])
                        TT(POOL, t[1], t[1], KimB, ALU.mult, [tr[1], kf], [tr[1]])
                        TT(POOL, t[7], t[7], t[1], ALU.subtract, [tr[7], tr[1]], [tr[7]])
                        TT(DVE, Y[:, kt, 0, 0:W], t[4], t[6], ALU.add, [tr[4], tr[6]], [Yp[0]])
                        TT(POOL, Y[:, kt, 1, 0:W], t[4], t[6], ALU.subtract, [tr[4], tr[6]], [Yp[1]])
                        TT(DVE, Y[:, kt, 2, 0:W], t[5], t[7], ALU.subtract, [tr[5], tr[7]], [Yp[2]])
                        TT(POOL, Y[:, kt, 3, 0:W], t[5], t[7], ALU.add, [tr[5], tr[7]], [Yp[3]])
                        yield
                    K.mm(0, [(nyqp_t[:, 0:1], z[:, 0, mt, 0:W]) for mt in range(nmt)], 1, W, [nyqp_t, z])
                    K.mm(1, [(nyqp_t[:, 0:1], z[:, 1, mt, 0:W]) for mt in range(nmt)], 1, W, [nyqp_t, z])
                    K.dma(SP, kfx[:, :, 0:W], kfs[n][o, nmt, :, 0:1, c0:c0 + W].rearrange("a p c -> p a c"), [dres("kfs%d_%d_%d_%d" % (n, o, nmt, ch_)) for ch_ in range(2)], [kfx], kfx)
                    t0, t1, t2 = (tx[i][:, 0:W] for i in range(3))
                    K.op(DVE, [K.banks[0], kfx], [tx[0]], lambda e: e.tensor_tensor(out=t0, in0=K.bap(0)[0:1, 0:W], in1=kfx[:, 0, 0:W], op=ALU.mult))
                    K.op(DVE, [K.banks[1], kfx], [tx[1]], lambda e: e.tensor_tensor(out=t1, in0=K.bap(1)[0:1, 0:W], in1=kfx[:, 1, 0:W], op=ALU.mult))
                    K.op(DVE, [tx[0], tx[1]], [Yx], lambda e: e.tensor_tensor(out=Yx[:, 0, 0:W], in0=t0, in1=t1, op=ALU.add))
                    K.op(DVE, [K.banks[1], kfx], [tx[0]], lambda e: e.tensor_tensor(out=t0, in0=K.bap(1)[0:1, 0:W], in1=kfx[:, 0, 0:W], op=ALU.mult))
                    K.op(DVE, [K.banks[0], kfx], [tx[1]], lambda e: e.tensor_tensor(out=t1, in0=K.bap(0)[0:1, 0:W], in1=kfx[:, 1, 0:W], op=ALU.mult))
                    K.op(DVE, [tx[0], tx[1]], [Yx], lambda e: e.tensor_tensor(out=Yx[:, 1, 0:W], in0=t0, in1=t1, op=ALU.subtract))
                    igrp = (("ce", 0, 0), ("se", 0, 2), ("cot", 1, 1), ("sot", 1, 3))
                    if o == 1 and ci + 1 < len(chunks):
                        nxt[0] = prologue(ci + 1)
                    for tau in range(nmt):
                        if nxt[0] is not None:
                            for _ in range(2):
                                if next(nxt[0], "end") == "end":
                                    nxt[0] = None
                                    break
                        by = [0 + 2 * (tau % 2), 1 + 2 * (tau % 2)]
                        for q in range(nq):
                            bufs = {}
                            for nm, par, pl in igrp:
                                mb = mbuf[nm][mc[0] % 3]
                                K.dma(SP, mb[:], M_[nm][tau, :, q * CS:(q + 1) * CS, :], [], [mb], mb)
                                bufs[nm] = mb
                            mc[0] += 1
                            for par in range(2):
                                pairs = []
                                rd = list(Yp)
                                for nm, par_, pl in igrp:
                                    if par_ == par:
                                        pairs += [(bufs[nm][:, i, :], Y[:, q * CS + i, pl, 0:W]) for i in range(CS)]
                                        rd.append(bufs[nm])
                                K.mm(by[par], pairs, 128, W, rd, first=(q == 0), last=False)
                        for par in range(2):
                            K.mm(by[par], [(nyq_t[0:1, :], Yx[0:1, par, 0:W])], 128, W, [nyq_t, Yx], first=False, last=True)
                        hs = hst[tau % 2]
                        for par in range(2):
                            xt_ = xtl[par]
                            ftile(tau, par, xt_[:, 0:W], xt_, par)
                            if o == 0:
                                K.op(DVE, [K.banks[by[par]], xt_], [z], lambda e, par=par, xt_=xt_, tau=tau: e.tensor_tensor(out=z[:, par, tau, 0:W], in0=K.bap(by[par])[:, 0:W], in1=xt_[:, 0:W], op=ALU.mult))
                            else:
                                ht = hyt[par]
                                K.op(DVE, [K.banks[by[par]], xt_], [ht], lambda e, par=par, xt_=xt_, ht=ht: e.tensor_tensor(out=ht[:, 0:W], in0=K.bap(by[par])[:, 0:W], in1=xt_[:, 0:W], op=ALU.mult))
                                b = 4 + par
                                K.group(PE, [ht, ident], [K.banks[b]],
                                        [(lambda e, j=j, ht=ht, b=b: e.transpose(out=K.bap(b, BF16)[:, j * 128:(j + 1) * 128], in_=ht[:, j * 128:(j + 1) * 128], identity=ident[:]))
                                         for j in range(nj)])
                                K.op(ACT, [K.banks[b]], [hs], lambda e, b=b, hs=hs, par=par: e.activation(out=hs[:, 0:nj, par:256:2], in_=K.bap(b, BF16)[:, 0:W].rearrange("p (j t) -> p j t", j=nj), func=AF.Copy))
                        yield
                        if o == 1:
                            hw[0] += 1
                            K.dma(ACT, hyoT[c0:c0 + W, tok0 + tau * 256:tok0 + (tau + 1) * 256].rearrange("(j p) t -> p j t", p=128), hs[:, 0:nj, :],
                                  [hs], [dres("hyoT_w%d" % hw[0])], hs, store=True)
                    if o == 0:
                        yield from conv_steps(ci, 1)
                if nxt[0] is not None:
                    yield from nxt[0]


    def p6():
        TC = 512
        with ExitStack() as ps:
            cG = K.sb("mcG", [128, D], F32, ps)
            wab = load_w_bf16(ps, "wab", w_ab, 16, D, part=64)
            whb = load_w_bf16(ps, "whb", w_hb, 8, D)
            wo = load_w_bf16(ps, "wo", w_out, 8, D)
            ats = [K.sb("at%d" % i, [64, 16, TC], BF16, ps) for i in range(2)]
            hys = [K.sb("hy%d" % i, [128, 8, TC], BF16, ps) for i in range(2)]
            sgs = [K.sb("sg%d" % i, [128, 16, TC], BF16, ps) for i in range(2)]
            mT = K.sb("mT", [128, 8, TC], BF16, ps)
            t1s = [K.sb("t1%d" % i, [128, TC], F32, ps) for i in range(2)]
            t2s = [K.sb("t2%d" % i, [128, TC], F32, ps) for i in range(2)]
            xrs = [K.sb("xr%d" % i, [128, D], F32, ps) for i in range(2)]
            tmp = [K.sb("tmp%d" % i, [128, 512], F32, ps) for i in range(2)]
            nch = NT // 4

            def loads(c):
                at, hy, sg = ats[c % 2], hys[c % 2], sgs[c % 2]
                K.dma(SP, at[:], attT[:, :, c * TC:(c + 1) * TC], [dres("attT%d" % c)], [at], at)
                K.dma(SP, hy[:], hyoT[:, c * TC:(c + 1) * TC].rearrange("(k p) t -> p k t", p=128), [dres("hyoT")], [hy], hy)
                K.dma(SP, sg[:], sgT[:, c * TC:(c + 1) * TC].rearrange("(k p) t -> p k t", p=128), [dres("sgT%d" % c)], [sg], sg)

            load_consts((cG,), 1, 0, which=(2,))
            loads(0)
            for c in range(nch):
                if c + 1 < nch:
                    loads(c + 1)
                if c == 8:
                    load_consts((cG,), 1, 1, which=(2,))
                at, hy, sg = ats[c % 2], hys[c % 2], sgs[c % 2]
                for m in range(8):
                    ba = K.bank()
                    K.mm(ba, [(wab[:, h, m * 128:(m + 1) * 128], at[:, h, :]) for h in range(16)], 128, TC, [wab, at])
                    bh = K.bank()
                    K.mm(bh, [(whb[:, k, m * 128:(m + 1) * 128], hy[:, k, :]) for k in range(8)], 128, TC, [whb, hy])
                    t1, t2 = t1s[m % 2], t2s[m % 2]
                    K.op(DVE, [K.banks[ba], sg], [t1], lambda e, ba=ba, t1=t1, m=m, sg=sg: e.tensor_tensor(out=t1[:], in0=K.bap(ba), in1=sg[:, m, :], op=ALU.mult))
                    K.op(DVE, [K.banks[bh], sg], [t2], lambda e, bh=bh, t2=t2, m=m, sg=sg: e.tensor_tensor(out=t2[:], in0=K.bap(bh), in1=sg[:, 8 + m, :], op=ALU.mult))
                    K.op(POOL, [t1, t2], [mT], lambda e, t1=t1, t2=t2, m=m: e.tensor_tensor(out=mT[:, m, :], in0=t1[:], in1=t2[:], op=ALU.add))
                for t4 in range(4):
                    tile = c * 4 + t4
                    xr = xrs[tile % 2]
                    K.dma(SP, xr[:], x1s[tile * 128:(tile + 1) * 128, :], [dres("x1s%d" % tile)], [xr], xr)
                    for nh in range(2):
                        by = K.bank()
                        K.mm(by, [(mT[:, k, t4 * 128:(t4 + 1) * 128], wo[:, k, nh * 512:(nh + 1) * 512]) for k in range(8)], 128, 512, [mT, wo])
                        tm = tmp[nh]
                        K.op(DVE, [K.banks[by], cG], [tm], lambda e, by=by, tm=tm, nh=nh: e.tensor_tensor(out=tm[:], in0=K.bap(by), in1=cG[:, nh * 512:(nh + 1) * 512], op=ALU.mult))
                        K.op(POOL, [tm, xr], [xr], lambda e, tm=tm, xr=xr, nh=nh: e.tensor_tensor(out=xr[:, nh * 512:(nh + 1) * 512], in0=xr[:, nh * 512:(nh + 1) * 512], in1=tm[:], op=ALU.add))
                    K.dma(POOL, x2s[tile * 128:(tile + 1) * 128, :], xr[:], [xr], [dres("x2s%d" % tile)], xr, store=True)
        K.barrier()

    if upto >= 5:
        hyena_phases()
    if upto >= 6 or upto == -6:
        p6()
        ffn_phase(1, x2s, yout, "x2s", "yout")


_CONST_CACHE = {}


def _host_consts():
    if _CONST_CACHE:
        return _CONST_CACHE
    bf = ml_dtypes.bfloat16
    c = {}
    s = (np.arange(32)[None, :] * 128 + np.arange(128)[:, None]).astype(np.int64)
    inv = 10000.0 ** (-np.arange(16, dtype=np.float32) / 16.0)
    row = (s // 64).astype(np.float32)[..., None] * inv
    col = (s % 64).astype(np.float32)[..., None] * inv
    cr, sr, cc, sc = np.cos(row), np.sin(row), np.cos(col), np.sin(col)
    c["rope_c"] = np.concatenate([cr, cr, cc, cc], -1).astype(np.float32)
    c["rope_s"] = np.concatenate([-sr, sr, -sc, sc], -1).astype(np.float32)
    for n, tag in ((NLAT, "l"), (NCTX, "c")):
        N = 2 * n
        h = n // 2
        nmt = h // 128
        m = np.arange(h, dtype=np.int64)
        k = np.arange(h, dtype=np.int64)
        th = 2.0 * np.pi / N
        ae = th * ((2 * m[:, None] * k[None, :]) % N).astype(np.float64)
        ao = th * (((2 * m[:, None] + 1) * k[None, :]) % N).astype(np.float64)
        lay = lambda M: np.ascontiguousarray(M.reshape(nmt, 128, nmt, 128).transpose(2, 1, 0, 3)).astype(bf)
        c["ce_" + tag] = lay(np.cos(ae))
        c["se_" + tag] = lay(np.sin(ae))
        c["co_" + tag] = lay(np.cos(ao))
        c["so_" + tag] = lay(np.sin(ao))
        c["cot_" + tag] = lay(np.cos(ao).T)
        c["sot_" + tag] = lay(np.sin(ao).T)
        t = (np.arange(n, dtype=np.float32) / np.float32(max(n - 1, 1))).astype(np.float32)
        bands = np.arange(1, 17, dtype=np.float32)
        a = (np.float32(2.0 * math.pi) * t[:, None] * bands[None, :]).astype(np.float32)
        z = np.concatenate([t[:, None], np.cos(a), np.sin(a)], -1).astype(np.float32)
        c["zemb_" + tag] = np.ascontiguousarray(z.T)
        c["tcol_" + tag] = np.ascontiguousarray(t.reshape(nmt, 128, 2).transpose(1, 2, 0))
        wA = np.full((128, nmt + 1), 2.0 / N, np.float32)
        wB = np.full((128, nmt + 1), 2.0 / N, np.float32)
        wA[0, 0] = 1.0 / N
        wB[0, 0] = 1.0 / N
        wB[:, nmt] = 0.0
        c["wk_" + tag] = np.ascontiguousarray(np.stack([wA, wB], 1))
    c["nyq"] = np.tile(((-1.0) ** np.arange(128))[None, :], (128, 1)).astype(bf)
    c["nyqp"] = np.tile(((-1.0) ** np.arange(128))[:, None], (1, 2)).astype(bf)
    _CONST_CACHE.update(c)
    return c


def _core_inputs(core, inp, consts):
    f = lambda a: np.ascontiguousarray(np.asarray(a, dtype=np.float32))
    m = {}
    m["xin"] = np.concatenate([f(inp["x_sample"][core]), f(inp["x_prompt"][2 * core]), f(inp["x_prompt"][2 * core + 1])], 0)
    m["cvec"] = np.stack([f(inp["c"][core]), f(inp["c_ctx"])], 0)
    m["cache_k"] = f(inp["cache_k"][core, 0]).reshape(PAST, NKV * HD)
    m["cache_v"] = f(inp["cache_v"][core, 0]).reshape(PAST, NKV * HD)
    m["w_mod"] = f(inp["w_mod"][0])
    m["b_mod"] = f(inp["b_mod"][0])
    m["norms"] = np.stack([f(inp["norm_ffn1"][0]), f(inp["norm_mix"][0]), f(inp["norm_ffn2"][0])], 0)
    for k in ("ffn1_wi", "ffn1_wo", "ffn2_wi", "ffn2_wo", "w_in", "attn_sink", "conv_w", "conv_b", "filt_w1", "filt_b1",
              "filt_w2", "filt_b2", "filt_w3", "filt_b3", "filt_freq"):
        m[k] = f(inp[k][0])
    m["qk_norm"] = np.stack([f(inp["q_norm"][0]), f(inp["k_norm"][0])], 0)
    m["filt_decay"] = f(inp["filt_decay"][0]).reshape(4 * D)
    m["hyena_skip"] = f(inp["hyena_skip"][0]).reshape(2 * D)
    m["w_ab"] = f(inp["w_attn_branch"][0])
    m["w_hb"] = f(inp["w_hyena_branch"][0])
    m["w_out"] = f(inp["w_out"][0])
    m.update(consts)
    return m


_NC_CACHE = {}


def kernel(**inputs):
    consts = _host_consts()
    if "nc" not in _NC_CACHE:
        _NC_CACHE["nc"] = build_program()
    nc = _NC_CACHE["nc"]
    in_maps = [_core_inputs(c, inputs, consts) for c in range(8)]
    res = run_bass_kernel_spmd(nc, in_maps, core_ids=list(range(8)))
    y_prompt = np.zeros((16, NCTX, D), np.float32)
    y_sample = np.zeros((8, NLAT, D), np.float32)
    new_k = np.zeros((16, 1, NCTX, NKV, HD), np.float32)
    new_v = np.zeros((16, 1, NCTX, NKV, HD), np.float32)
    for c in range(8):
        r = res.results[c]
        y = np.asarray(r["yout"])
        y_sample[c] = y[:NLAT]
        y_prompt[2 * c] = y[NLAT:NLAT + NCTX]
        y_prompt[2 * c + 1] = y[NLAT + NCTX:]
        nk = np.asarray(r["newk"]).reshape(2, NCTX, NKV, HD)
        nv = np.asarray(r["newv"]).reshape(2, NCTX, NKV, HD)
        new_k[2 * c:2 * c + 2, 0] = nk
        new_v[2 * c:2 * c + 2, 0] = nv
    return (y_prompt, y_sample, new_k, new_v)
```

```python
import math
from contextlib import ExitStack
import numpy as np
import ml_dtypes
import concourse.bass as bass
import concourse.mybir as mybir
from concourse.bass_utils import run_bass_kernel_spmd

F32 = mybir.dt.float32
BF16 = mybir.dt.bfloat16
AF = mybir.ActivationFunctionType
ALU = mybir.AluOpType
AX = mybir.AxisListType

D = 1024
NLAT = 4096
NCTX = 256
NTOK = NLAT + 2 * NCTX
NT = NTOK // 128
DFF = 2816
NH, NKV, HD = 16, 4, 64
PAST = 512
EPS = 1e-6
INCOLS = 6656
FW = 64
FE = 33
PI = math.pi


class Res:
    def __init__(self, name):
        self.name = name
        self.w = {}
        self.r = {}
        self.dsem = None
        self.ssem = None


class Tile(Res):
    def __init__(self, name, t):
        super().__init__(name)
        self.t = t

    def __getitem__(self, k):
        return self.t[k]


def _merge(d, s):
    for k, (sem, v) in s.items():
        if k not in d or d[k][1] < v:
            d[k] = (sem, v)


class Eng:
    def __init__(self, K, name, e):
        self.K = K
        self.name = name
        self.e = e
        self.sid, self.sem = K.new_sem(name)
        self.cnt = 0
        self.waited = {}

    def wait_for(self, deps, skip=None, include_self=False):
        for sid, (sem, val) in deps.items():
            if (sid == self.sid and not include_self) or sid == skip:
                continue
            if self.waited.get(sid, 0) >= val:
                continue
            self.e.wait_ge(sem, val)
            self.waited[sid] = val


class KB:
    def __init__(self, nc, es):
        self.nc = nc
        self.es = es
        self.sems = []
        self.free_sems = []
        self.PE = Eng(self, "pe", nc.tensor)
        self.ACT = Eng(self, "act", nc.scalar)
        self.DVE = Eng(self, "dve", nc.vector)
        self.POOL = Eng(self, "pool", nc.gpsimd)
        self.SP = Eng(self, "sp", nc.sync)
        self.engs = [self.PE, self.ACT, self.DVE, self.POOL, self.SP]
        self.dma_res = []
        self.uid = 0
        self.psum_t = es.enter_context(nc.psum_tensor("psum_all", [128, 4096], F32))
        self.banks = [Tile("bank%d" % i, self.psum_t) for i in range(8)]
        self.bank_set = list(range(8))
        self.bank_rr = 0

    def new_sem(self, name):
        s = self.es.enter_context(self.nc.semaphore("s_%s_%d" % (name, len(self.sems))))
        self.sems.append([s, 0])
        return len(self.sems) - 1, s

    def _alloc_dsem(self):
        if self.free_sems:
            return self.free_sems.pop()
        return self.new_sem("d")[0]

    def sb(self, name, shape, dtype, stack=None):
        self.uid += 1
        t = (stack or self.es).enter_context(self.nc.sbuf_tensor("%s_%d" % (name, self.uid), shape, dtype))
        return Tile(name, t)

    def set_banks(self, lst):
        self.bank_set = list(lst)
        self.bank_rr = 0

    def bank(self):
        b = self.bank_set[self.bank_rr % len(self.bank_set)]
        self.bank_rr += 1
        return b

    def bap(self, b, dtype=F32, nb=1):
        ap = self.psum_t[:, b * 512:(b + nb) * 512]
        if dtype == BF16:
            ap = ap.bitcast(BF16)
        return ap

    def group(self, E, reads, writes, fns):
        raw = {}
        for r in reads:
            _merge(raw, r.w)
        E.wait_for(raw, include_self=(E is not self.PE))
        deps = {}
        for r in writes:
            _merge(deps, r.w)
            _merge(deps, r.r)
        E.wait_for(deps)
        ins = None
        for fn in fns:
            ins = fn(E.e)
        E.cnt += 1
        ins.then_inc(E.sem, 1)
        ev = (E.sem, E.cnt)
        for r in reads:
            _merge(r.r, {E.sid: ev})
        for r in writes:
            r.w = {E.sid: ev}
            r.r = {}
        return ins

    def op(self, E, reads, writes, fn):
        return self.group(E, reads, writes, [fn])

    def dma(self, Q, out, in_, reads, writes, owner, store=False, **kw):
        if store:
            if owner.ssem is None:
                owner.ssem = self._alloc_dsem()
                self.dma_res.append(owner)
            sid = owner.ssem
        else:
            if owner.dsem is None:
                owner.dsem = self._alloc_dsem()
                self.dma_res.append(owner)
            sid = owner.dsem
        sem = self.sems[sid][0]
        raw = {}
        for r in reads:
            _merge(raw, r.w)
        Q.wait_for(raw, skip=sid, include_self=True)
        deps = {}
        for r in writes:
            _merge(deps, r.w)
            _merge(deps, r.r)
        Q.wait_for(deps, skip=sid)
        self.sems[sid][1] += 16
        ev = (sem, self.sems[sid][1])
        Q.e.dma_start(out=out, in_=in_, **kw).then_inc(sem, 16)
        for r in reads:
            _merge(r.r, {sid: ev})
        for r in writes:
            r.w = {sid: ev}
            r.r = {}

    def mm(self, bank, pairs, m, n, reads, off=0, first=True, last=True, nb=1, extra_writes=()):
        out = self.bap(bank, F32, nb)[0:m, off:off + n]
        nl = len(pairs) - 1
        fns = [(lambda e, l=l, r=r, i=i: e.matmul(out, lhsT=l, rhs=r, start=(first and i == 0), stop=(last and i == nl)))
               for i, (l, r) in enumerate(pairs)]
        self.group(self.PE, reads, [self.banks[bank + j] for j in range(nb)] + list(extra_writes), fns)

    def barrier(self):
        deps = {}
        for E in self.engs:
            if E.cnt > 0:
                deps[E.sid] = (E.sem, E.cnt)
        for res in self.dma_res:
            for sid in (res.dsem, res.ssem):
                if sid is not None:
                    deps[sid] = (self.sems[sid][0], self.sems[sid][1])
        self.DVE.wait_for(deps)
        self.DVE.cnt += 1
        self.nc.vector.memset(self.bar_t[:], 0.0).then_inc(self.DVE.sem, 1)
        ev = {self.DVE.sid: (self.DVE.sem, self.DVE.cnt)}
        for E in self.engs:
            E.wait_for(ev)
        for res in self.dma_res:
            for sid in (res.dsem, res.ssem):
                if sid is not None:
                    self.free_sems.append(sid)
            res.dsem = None
            res.ssem = None
        self.dma_res = []
        self.free_sems = sorted(set(self.free_sems))


def build_program(upto=99, debug=False):
    nc = bass.Bass("TRN2", target_bir_lowering=False)
    es = ExitStack()
    with es:
        K = KB(nc, es)
        K.bar_t = K.sb("bar", [1, 8], F32)
        _emit(nc, K, upto, debug)
        K.barrier()
    return nc


def _dram(nc, name, shape, dtype, kind):
    return nc.dram_tensor(name, list(shape), dtype, kind=kind).ap()


def _emit(nc, K, upto, debug):
    PE, ACT, DVE, POOL, SP = K.PE, K.ACT, K.DVE, K.POOL, K.SP
    ext = lambda name, shape, dt=F32: _dram(nc, name, shape, dt, "ExternalInput")
    outk = "ExternalOutput"
    scr = "ExternalOutput" if debug else "Internal"
    xin = ext("xin", [NTOK, D])
    cvec = ext("cvec", [2, D])
    cache_k = ext("cache_k", [PAST, NKV * HD])
    cache_v = ext("cache_v", [PAST, NKV * HD])
    w_mod = ext("w_mod", [D, 9 * D])
    b_mod = ext("b_mod", [9 * D])
    norms = ext("norms", [3, D])
    ffn_wi = [ext("ffn1_wi", [D, 2 * DFF]), ext("ffn2_wi", [D, 2 * DFF])]
    ffn_wo = [ext("ffn1_wo", [DFF, D]), ext("ffn2_wo", [DFF, D])]
    w_in = ext("w_in", [D, INCOLS])
    qk_norm = ext("qk_norm", [2, HD])
    attn_sink = ext("attn_sink", [NH])
    conv_w = ext("conv_w", [3, 3 * D])
    conv_b = ext("conv_b", [3 * D])
    filt_w1 = ext("filt_w1", [FE, FW])
    filt_b1 = ext("filt_b1", [FW])
    filt_w2 = ext("filt_w2", [FW, FW])
    filt_b2 = ext("filt_b2", [FW])
    filt_w3 = ext("filt_w3", [FW, 4 * D])
    filt_b3 = ext("filt_b3", [4 * D])
    filt_freq = ext("filt_freq", [FW])
    filt_decay = ext("filt_decay", [4 * D])
    hyena_skip = ext("hyena_skip", [2 * D])
    w_ab = ext("w_ab", [D, D])
    w_hb = ext("w_hb", [D, D])
    w_out = ext("w_out", [D, D])
    rope_c = ext("rope_c", [128, 32, 64])
    rope_s = ext("rope_s", [128, 32, 64])
    mats = {}
    for n_, tag in ((NLAT, "l"), (NCTX, "c")):
        nm_ = n_ // 256
        mats[n_] = {nm: ext("%s_%s" % (nm, tag), [nm_, 128, nm_, 128], BF16) for nm in ("ce", "se", "co", "so", "cot", "sot")}
    zemb = {NLAT: ext("zemb_l", [FE, NLAT]), NCTX: ext("zemb_c", [FE, NCTX])}
    tcol = {NLAT: ext("tcol_l", [128, 2, 16]), NCTX: ext("tcol_c", [128, 2, 1])}
    wk = {NLAT: ext("wk_l", [128, 2, 17]), NCTX: ext("wk_c", [128, 2, 2])}
    nyq = ext("nyq", [128, 128], BF16)
    nyqp = ext("nyqp", [128, 2], BF16)
    yout = _dram(nc, "yout", [NTOK, D], F32, outk)
    newk = _dram(nc, "newk", [2 * NCTX, NKV * HD], F32, outk)
    newv = _dram(nc, "newv", [2 * NCTX, NKV * HD], F32, outk)
    modd = _dram(nc, "modd", [2, 9 * D], F32, scr)
    x1s = _dram(nc, "x1s", [NTOK, D], F32, scr)
    x2s = _dram(nc, "x2s", [NTOK, D], F32, scr)
    hyT = _dram(nc, "hyT", [3 * D, NTOK], BF16, scr)
    qkvs = _dram(nc, "qkvs", [NTOK, 1536], F32, scr)
    sgT = _dram(nc, "sgT", [2 * D, NTOK], BF16, scr)
    attT = _dram(nc, "attT", [HD, NH, NTOK], BF16, scr)
    hyoT = _dram(nc, "hyoT", [D, NTOK], BF16, scr)
    kfs = {NLAT: _dram(nc, "kfs_l", [2, 17, 4, 128, D], BF16, scr), NCTX: _dram(nc, "kfs_c", [2, 2, 4, 128, D], BF16, scr)}
    R = {}

    def dres(name):
        if name not in R:
            R[name] = Res(name)
        return R[name]

    ident_f = K.sb("identf", [128, 128], F32)
    ident = K.sb("ident", [128, 128], BF16)
    K.op(POOL, [], [ident_f], lambda e: e.memset(ident_f[:], 0.0))
    K.op(POOL, [ident_f], [ident_f], lambda e: e.affine_select(out=ident_f[:], in_=ident_f[:], pattern=[[-1, 128]],
                                                        compare_op=ALU.not_equal, fill=1.0, base=0, channel_multiplier=1))
    K.op(POOL, [ident_f], [ident], lambda e: e.tensor_copy(out=ident[:], in_=ident_f[:]))
    mhalf = K.sb("mhalf", [128, 32], F32)
    K.op(POOL, [], [mhalf], lambda e: e.memset(mhalf[:], -0.5))
    ones_f = K.sb("onesf", [128, 128], F32)
    K.op(POOL, [], [ones_f], lambda e: e.memset(ones_f[:], 1.0))

    def group_of(tile):
        return 0 if tile < 32 else 1

    def load_w_bf16(stack, name, src, kchunks, ncols, col0=0, part=128):
        t = K.sb(name, [part, kchunks, ncols], BF16, stack)
        for k in range(kchunks):
            K.dma(POOL, t[:, k, :], src[k * part:(k + 1) * part, col0:col0 + ncols], [], [t], t)
        return t

    ps_w1 = ExitStack()
    pre_w1 = (load_w_bf16(ps_w1, "wgu", ffn_wi[0], 8, 2 * DFF), load_w_bf16(ps_w1, "wd", ffn_wo[0], 22, D))
    with ExitStack() as ps:
        cT = K.sb("cT", [128, 8, 2], F32, ps)
        with nc.allow_non_contiguous_dma(reason="tiny"):
            for g in range(2):
                K.dma(SP, cT[:, :, g], cvec[g].rearrange("(k p) -> p k", p=128), [], [cT], cT)
        sT = K.sb("sT", [128, 8, 2], F32, ps)
        K.op(ACT, [cT], [sT], lambda e: e.activation(out=sT[:], in_=cT[:], func=AF.Silu))
        nm = K.sb("nm", [2, 3 * D], F32, ps)
        K.dma(SP, nm[:], norms.rearrange("a d -> (a d)").partition_broadcast(2), [], [nm], nm)
        wmb = [K.sb("wm%d" % i, [128, 8, 512], F32, ps) for i in range(2)]
        bms = [K.sb("bm%d" % i, [2, 512], F32, ps) for i in range(2)]
        mos = [K.sb("mo%d" % i, [2, 512], F32, ps) for i in range(2)]
        for j in range(18):
            wt, bm, mo = wmb[j % 2], bms[j % 2], mos[j % 2]
            slot, half = j // 2, j % 2
            K.dma(SP, wt[:], w_mod[:, j * 512:(j + 1) * 512].rearrange("(k p) n -> p k n", p=128), [], [wt], wt)
            K.dma(SP, bm[:], b_mod[j * 512:(j + 1) * 512].partition_broadcast(2), [], [bm], bm)
            b = K.bank()
            K.mm(b, [(sT[:, k, :], wt[:, k, :]) for k in range(8)], 2, 512, [sT, wt])
            K.op(DVE, [K.banks[b], bm], [mo], lambda e, b=b, bm=bm, mo=mo: e.tensor_tensor(out=mo[:], in0=K.bap(b)[0:2, :], in1=bm[:], op=ALU.add))
            if slot % 3 == 1:
                i3 = slot // 3
                K.op(DVE, [mo, nm], [mo], lambda e, mo=mo, i3=i3, half=half: e.scalar_tensor_tensor(
                    out=mo[:], in0=mo[:], scalar=1.0, in1=nm[:, i3 * D + half * 512:i3 * D + (half + 1) * 512], op0=ALU.add, op1=ALU.mult))
            if slot in (2, 8):
                K.op(DVE, [mo], [mo], lambda e, mo=mo: e.tensor_scalar(out=mo[:], in0=mo[:], scalar1=0.5, scalar2=None, op0=ALU.mult))
            K.dma(ACT, modd[:, j * 512:(j + 1) * 512], mo[:], [mo], [dres("modd")], mo, store=True)
    K.barrier()
    if upto < 1:
        ps_w1.close()
        return

    def load_consts(tiles, i, g, which=(0, 1, 2)):
        for t, j in zip(tiles, which):
            K.dma(SP, t[:], modd[g, (3 * i + j) * D:(3 * i + j + 1) * D].partition_broadcast(128), [dres("modd")], [t], t)

    def load_w_bf16(stack, name, src, kchunks, ncols, col0=0, part=128):
        t = K.sb(name, [part, kchunks, ncols], BF16, stack)
        for k in range(kchunks):
            K.dma(POOL, t[:, k, :], src[k * part:(k + 1) * part, col0:col0 + ncols], [], [t], t)
        return t

    def norm_mod_T(bufs, xt, A, B, hT, col0, part=0):
        sq, ss, hb = bufs
        if part != 2:
            norm_mod_a(bufs, xt, A, B)
        if part != 1:
            norm_mod_b(bufs, hT, col0)

    def norm_mod_a(bufs, xt, A, B):
        sq, ss, hb = bufs
        K.op(ACT, [xt], [sq, ss], lambda e: e.activation(out=sq[:], in_=xt[:], func=AF.Square, accum_out=ss[:, 0:1]))
        K.op(DVE, [ss], [ss], lambda e: e.tensor_scalar(out=ss[:, 1:2], in0=ss[:, 0:1], scalar1=1.0 / D, scalar2=EPS,
                                                        op0=ALU.mult, op1=ALU.add))
        K.op(POOL, [ss, mhalf], [ss], lambda e: e.tensor_tensor(out=ss[:, 2:3], in0=ss[:, 1:2], in1=mhalf[:, 0:1], op=ALU.pow))
        K.op(DVE, [xt, ss, A], [sq], lambda e: e.scalar_tensor_tensor(out=sq[:], in0=xt[:], scalar=ss[:, 2:3], in1=A[:],
                                                                      op0=ALU.mult, op1=ALU.mult))
        K.op(POOL, [sq, B], [hb], lambda e: e.tensor_tensor(out=hb[:], in0=sq[:], in1=B[:], op=ALU.add))

    def norm_mod_b(bufs, hT, col0):
        sq, ss, hb = bufs
        b = K.bank()
        K.group(PE, [hb, ident], [K.banks[b]],
                [(lambda e, k=k: e.transpose(out=K.bap(b, BF16)[:, k * 128:(k + 1) * 128], in_=hb[:, k * 128:(k + 1) * 128],
                                             identity=ident[:])) for k in range(8)])
        K.op(ACT, [K.banks[b]], [hT],
             lambda e: e.activation(out=hT[:, :, col0:col0 + 128], in_=K.bap(b, BF16).rearrange("p (k t) -> p k t", k=8),
                                    func=AF.Copy))

    def ffn_phase(idx, src, dst, src_name, dst_name, pre=None):
        TC = 256
        NTC = TC // 128
        with ExitStack() as ps:
            cB, cA, cG = [K.sb("mc%d" % i, [128, D], F32, ps) for i in range(3)]
            if pre is not None:
                wgu, wd = pre
            else:
                wgu = load_w_bf16(ps, "wgu", ffn_wi[idx], 8, 2 * DFF)
                wd = load_w_bf16(ps, "wd", ffn_wo[idx], 22, D)
            xts = [K.sb("xt%d" % i, [128, D], F32, ps) for i in range(2)]
            xrs = [K.sb("xr%d" % i, [128, D], F32, ps) for i in range(2)]
            hTs = [K.sb("hT%d" % i, [128, 8, TC], BF16, ps) for i in range(2)]
            sq = K.sb("sq", [128, D], F32, ps)
            ss = K.sb("ss", [128, 4], F32, ps)
            hb = K.sb("hb", [128, D], BF16, ps)
            actT = K.sb("actT", [128, 22, TC], BF16, ps)
            sgs = [K.sb("sg%d" % i, [128, TC], F32, ps) for i in range(2)]
            tmp = [K.sb("tmp%d" % i, [128, 512], F32, ps) for i in range(2)]
            cnt = [0]

            hbs = [hb] + [K.sb("hbx%d" % i, [128, D], BF16, ps) for i in range(NTC - 1)]

            def prep_a(c):
                for t4 in range(NTC):
                    tile = c * NTC + t4
                    xt = xts[cnt[0] % 2]
                    cnt[0] += 1
                    K.dma(SP, xt[:], src[tile * 128:(tile + 1) * 128, :], [dres("%s%d" % (src_name, tile))], [xt], xt)
                    norm_mod_a((sq, ss, hbs[t4]), xt, cA, cB)

            def prep_b(c):
                hT = hTs[c % 2]
                for t4 in range(NTC):
                    norm_mod_b((sq, ss, hbs[t4]), hT, t4 * 128)

            def prep(c):
                prep_a(c)
                prep_b(c)

            def up(c, j0, j1):
                hT = hTs[c % 2]
                for j in range(j0, j1):
                    bg = K.bank()
                    K.mm(bg, [(wgu[:, k, j * 128:(j + 1) * 128], hT[:, k, :]) for k in range(8)], 128, TC, [wgu, hT])
                    bu = K.bank()
                    K.mm(bu, [(wgu[:, k, DFF + j * 128:DFF + (j + 1) * 128], hT[:, k, :]) for k in range(8)], 128, TC, [wgu, hT])
                    sg = sgs[j % 2]
                    K.op(ACT, [K.banks[bg]], [sg], lambda e, bg=bg, sg=sg: e.activation(out=sg[:], in_=K.bap(bg)[:, 0:TC], func=AF.Silu))
                    K.op(DVE, [K.banks[bu], sg], [actT],
                         lambda e, bu=bu, sg=sg, j=j: e.tensor_tensor(out=actT[:, j, :], in0=K.bap(bu)[:, 0:TC], in1=sg[:], op=ALU.mult))

            def down(c):
                for t4 in range(NTC):
                    tile = c * NTC + t4
                    xr = xrs[tile % 2]
                    K.dma(SP, xr[:], src[tile * 128:(tile + 1) * 128, :], [dres("%s%d" % (src_name, tile))], [xr], xr)
                    for nh in range(2):
                        by = K.bank()
                        K.mm(by, [(actT[:, j, t4 * 128:(t4 + 1) * 128], wd[:, j, nh * 512:(nh + 1) * 512]) for j in range(22)],
                             128, 512, [actT, wd])
                        tm = tmp[nh]
                        K.op(DVE, [K.banks[by], cG], [tm],
                             lambda e, by=by, tm=tm, nh=nh: e.tensor_tensor(out=tm[:], in0=K.bap(by), in1=cG[:, nh * 512:(nh + 1) * 512], op=ALU.mult))
                        K.op(POOL, [tm, xr], [xr],
                             lambda e, tm=tm, xr=xr, nh=nh: e.tensor_tensor(out=xr[:, nh * 512:(nh + 1) * 512], in0=xr[:, nh * 512:(nh + 1) * 512], in1=tm[:], op=ALU.add))
                    K.dma(POOL, dst[tile * 128:(tile + 1) * 128, :], xr[:], [xr], [dres("%s%d" % (dst_name, tile))], xr, store=True)

            for g, (c0, c1) in enumerate(((0, 32 // NTC), (32 // NTC, NT // NTC))):
                load_consts((cB, cA, cG), 0 if idx == 0 else 2, g)
                prep(c0)
                for c in range(c0, c1):
                    up(c, 0, 11)
                    if c + 1 < c1:
                        prep_a(c + 1)
                    up(c, 11, 22)
                    if c + 1 < c1:
                        prep_b(c + 1)
                    down(c)
        K.barrier()

    ffn_phase(0, xin, x1s, "xin", "x1s", pre=pre_w1)
    ps_w1.close()
    if upto < 2:
        return

    def p2():
        TC = 512
        with ExitStack() as ps:
            cB, cA = [K.sb("mc%d" % i, [128, D], F32, ps) for i in range(2)]
            wz = load_w_bf16(ps, "wz", w_in, 8, 5120, col0=1536)
            wq2 = load_w_bf16(ps, "wq2", w_in, 8, 1536, col0=0)
            qsts = [K.sb("qst%d" % i, [128, 1536], F32, ps) for i in range(2)]
            xts = [K.sb("xt%d" % i, [128, D], F32, ps) for i in range(2)]
            hTs = [K.sb("hT%d" % i, [128, 8, TC], BF16, ps) for i in range(2)]
            sq = K.sb("sq", [128, D], F32, ps)
            ss = K.sb("ss", [128, 4], F32, ps)
            hb = K.sb("hb", [128, D], BF16, ps)
            stg = [K.sb("stg%d" % i, [128, 8, TC], BF16, ps) for i in range(2)]
            cnt = [0, 0]

            hbs = [hb] + [K.sb("hbx%d" % i, [128, D], BF16, ps) for i in range(3)]

            def prep_a(c):
                for t4 in range(4):
                    tile = c * 4 + t4
                    xt = xts[cnt[0] % 2]
                    cnt[0] += 1
                    K.dma(SP, xt[:], x1s[tile * 128:(tile + 1) * 128, :], [dres("x1s%d" % tile)], [xt], xt)
                    norm_mod_a((sq, ss, hbs[t4]), xt, cA, cB)

            def prep_b(c):
                hT = hTs[c % 2]
                for t4 in range(4):
                    norm_mod_b((sq, ss, hbs[t4]), hT, t4 * 128)

            def prep(c):
                prep_a(c)
                prep_b(c)

            def body(c, m0, m1):
                hT = hTs[c % 2]
                for mg in range(m0, m1, 8):
                    st = stg[cnt[1] % 2]
                    cnt[1] += 1
                    for mi in range(8):
                        m = mg + mi
                        b = K.bank()
                        K.mm(b, [(wz[:, k, m * 128:(m + 1) * 128], hT[:, k, :]) for k in range(8)], 128, TC, [wz, hT])
                        if m < 24:
                            K.op(DVE, [K.banks[b]], [st], lambda e, b=b, st=st, mi=mi: e.tensor_copy(out=st[:, mi, :], in_=K.bap(b)))
                        else:
                            K.op(ACT, [K.banks[b]], [st], lambda e, b=b, st=st, mi=mi: e.activation(out=st[:, mi, :], in_=K.bap(b), func=AF.Sigmoid))
                    if mg < 24:
                        dst = hyT[mg * 128:(mg + 8) * 128, c * TC:(c + 1) * TC].rearrange("(m p) t -> p m t", p=128)
                        dr = dres("hyT%d" % c)
                    else:
                        dst = sgT[(mg - 24) * 128:(mg - 16) * 128, c * TC:(c + 1) * TC].rearrange("(m p) t -> p m t", p=128)
                        dr = dres("sgT%d" % c)
                    K.dma(POOL, dst, st[:], [st], [dr], st, store=True)

            nch = NT // 4
            load_consts((cB, cA), 1, 0, which=(0, 1))
            prep(0)
            for c in range(nch):
                body(c, 0, 16)
                if c + 1 < nch:
                    if c + 1 == 8:
                        load_consts((cB, cA), 1, 1, which=(0, 1))
                    prep_a(c + 1)
                body(c, 16, 40)
                hT_ = hTs[c % 2]
                for t4 in range(4):
                    tile = c * 4 + t4
                    qst = qsts[tile % 2]
                    for n in range(3):
                        b = K.bank()
                        K.mm(b, [(hT_[:, k, t4 * 128:(t4 + 1) * 128], wq2[:, k, n * 512:(n + 1) * 512]) for k in range(8)], 128, 512, [hT_, wq2])
                        if n == 1:
                            K.op(DVE, [K.banks[b]], [qst], lambda e, b=b, qst=qst, n=n: e.tensor_copy(out=qst[:, n * 512:(n + 1) * 512], in_=K.bap(b)))
                        else:
                            K.op(ACT, [K.banks[b]], [qst], lambda e, b=b, qst=qst, n=n: e.activation(out=qst[:, n * 512:(n + 1) * 512], in_=K.bap(b), func=AF.Copy))
                    K.dma(POOL, qkvs[tile * 128:(tile + 1) * 128, :], qst[:], [qst], [dres("qkvs%d" % tile)], qst, store=True)
                if c + 1 < nch:
                    prep_b(c + 1)
        K.barrier()

    p2()
    if upto < 3:
        return

    def p3():
        with ExitStack() as ps:
            cB, cA = [K.sb("mc%d" % i, [128, D], F32, ps) for i in range(2)]
            rc = K.sb("rc", [128, 32, 64], F32, ps)
            rs = K.sb("rs", [128, 32, 64], F32, ps)
            K.dma(SP, rc[:], rope_c, [], [rc], rc)
            K.dma(SP, rs[:], rope_s, [], [rs], rs)
            gq = K.sb("gq", [128, HD], F32, ps)
            gk = K.sb("gk", [128, HD], F32, ps)
            K.dma(SP, gq[:], qk_norm[0].partition_broadcast(128), [], [gq], gq)
            K.dma(SP, gk[:], qk_norm[1].partition_broadcast(128), [], [gk], gk)
            xts = [K.sb("xt%d" % i, [128, D], F32, ps) for i in range(2)]
            sq = K.sb("sq", [128, D], F32, ps)
            ss = K.sb("ss", [128, 4], F32, ps)
            hb = K.sb("hb", [128, D], BF16, ps)
            hT1 = [K.sb("hT1%d" % i, [128, 8, 128], BF16, ps) for i in range(2)]
            sqq = K.sb("sqq", [128, 1280], F32, ps)
            s1s = [K.sb("s1%d" % i, [128, 8], F32, ps) for i in range(3)]
            s2s = [K.sb("s2%d" % i, [128, 8], F32, ps) for i in range(3)]
            s3s = [K.sb("s3%d" % i, [128, 8], F32, ps) for i in range(3)]
            qf = K.sb("qf", [128, 1280], F32, ps)
            ta = K.sb("ta", [128, 1280], F32, ps)
            tb = K.sb("tb", [128, 1280], F32, ps)
            qb = K.sb("qb", [128, 1280], BF16, ps)
            vf = K.sb("vf", [128, 256], F32, ps)
            QTs = [K.sb("QT%d" % i, [128, 16, 128], BF16, ps) for i in range(3)]
            KT = K.sb("KT", [128, 4, NLAT], BF16, ps)
            V = K.sb("V", [128, 32, 4, 96], BF16, ps)
            cKT = K.sb("cKT", [128, 4, PAST], BF16, ps)
            cV = K.sb("cV", [128, 4, 4, 96], BF16, ps)
            ckb = K.sb("ckb", [128, 4, 256], BF16, ps)
            Pt = [K.sb("P%d" % i, [128, 512], BF16, ps) for i in range(3)]
            Osb = [K.sb("O%d" % i, [65, 512], F32, ps) for i in range(2)]
            rdens = [K.sb("rden%d" % i, [65, 512], F32, ps) for i in range(2)]
            rdbs = [K.sb("rdb%d" % i, [128, 512], BF16, ps) for i in range(2)]
            mones = K.sb("mones", [65, 512], F32, ps)
            K.op(POOL, [], [mones], lambda e: e.memset(mones[:], -1.0))
            ones_b = K.sb("ones_b", [128, 64], BF16, ps)
            K.op(POOL, [], [ones_b], lambda e: e.memset(ones_b[:], 1.0))
            aos = [K.sb("ao%d" % i, [64, 16, 128], BF16, ps) for i in range(2)]
            mprev = K.sb("mprev", [128, 128], BF16, ps)
            mnext = K.sb("mnext", [128, 128], BF16, ps)
            sinkexp = K.sb("sinkexp", [65, 16, 128], F32, ps)
            sk = K.sb("sk", [65, 16], F32, ps)
            K.op(POOL, [], [mprev], lambda e: e.memset(mprev[:], 1.0))
            K.op(POOL, [mprev], [mprev], lambda e: e.affine_select(out=mprev[:], in_=mprev[:], pattern=[[-1, 128]], compare_op=ALU.is_ge,
                                                              fill=0.0, base=0, channel_multiplier=1))
            K.op(POOL, [], [mnext], lambda e: e.memset(mnext[:], 1.0))
            K.op(POOL, [mnext], [mnext], lambda e: e.affine_select(out=mnext[:], in_=mnext[:], pattern=[[1, 128]], compare_op=ALU.is_ge,
                                                              fill=0.0, base=0, channel_multiplier=-1))
            mb_prev = K.sb("mb_prev", [128, 4, 128], BF16, ps)
            mb_next = K.sb("mb_next", [128, 4, 128], BF16, ps)
            for mb_, mk_ in ((mb_prev, mprev), (mb_next, mnext)):
                K.op(POOL, [mk_], [mb_], lambda e, mb_=mb_, mk_=mk_: e.tensor_scalar(out=mb_[:], in0=mk_[:].unsqueeze(1).to_broadcast([128, 4, 128]),
                                                                                  scalar1=-1.0, scalar2=30000.0, op0=ALU.add, op1=ALU.mult))
            K.op(POOL, [], [V], lambda e: e.memset(V[:], 0.0))
            K.op(POOL, [], [V], lambda e: e.memset(V[:, :, :, 64:65], 1.0))
            K.op(POOL, [], [cV], lambda e: e.memset(cV[:], 0.0))
            K.op(POOL, [], [cV], lambda e: e.memset(cV[:, :, :, 64:65], 1.0))
            K.op(POOL, [], [KT], lambda e: e.memset(KT[64:128], 0.0))
            K.op(POOL, [], [cKT], lambda e: e.memset(cKT[64:128], 0.0))
            for qt_ in QTs:
                K.op(POOL, [], [qt_], lambda e, qt_=qt_: e.memset(qt_[64:128], 0.0))
            for rb_ in rdbs:
                K.op(POOL, [], [rb_], lambda e, rb_=rb_: e.memset(rb_[:], 0.0))
            K.dma(SP, sk[64:65, :], attn_sink.rearrange("(o h) -> o h", o=1), [], [sk], sk)
            K.op(ACT, [sk], [sk], lambda e: e.activation(out=sk[64:65, :], in_=sk[64:65, :], func=AF.Exp))
            K.op(POOL, [sk], [sinkexp], lambda e: e.tensor_copy(out=sinkexp[64:65, :, :], in_=sk[64:65, :].unsqueeze(2).to_broadcast([1, 16, 128])))
            sel0 = K.sb("sel0", [1, 96], BF16, ps)
            K.op(POOL, [], [sel0], lambda e: e.memset(sel0[:], 0.0))
            K.op(POOL, [sel0], [sel0], lambda e: e.memset(sel0[0:1, 64:65], 1.0))
            sk0 = K.sb("sk0", [1, 16], F32, ps)
            K.dma(SP, sk0[:], attn_sink.rearrange("(o h) -> o h", o=1), [], [sk0], sk0)
            K.op(ACT, [sk0], [sk0], lambda e: e.activation(out=sk0[:], in_=sk0[:], func=AF.Exp))
            sinkb = K.sb("sinkb", [1, 16, 128], BF16, ps)
            K.op(POOL, [sk0], [sinkb], lambda e: e.tensor_copy(out=sinkb[:], in_=sk0[:].unsqueeze(2).to_broadcast([1, 16, 128])))
            K.dma(POOL, ckb[:], cache_k.rearrange("(t p) c -> p t c", p=128), [], [ckb], ckb)
            for t in range(4):
                K.dma(POOL, cV[:, t, :, 0:64], cache_v[t * 128:(t + 1) * 128, :].rearrange("p (g d) -> p g d", g=4), [], [cV], cV)
            for t in range(4):
                K.group(PE, [ckb, ident], [K.banks[2]],
                        [(lambda e, g=g, t=t: e.transpose(out=K.bap(2, BF16)[0:64, g * 128:(g + 1) * 128], in_=ckb[:, t, g * 64:(g + 1) * 64],
                                                          identity=ident[:])) for g in range(4)])
                K.op(ACT, [K.banks[2]], [cKT], lambda e, t=t: e.activation(out=cKT[0:64, :, t * 128:(t + 1) * 128],
                                                                           in_=K.bap(2, BF16)[0:64, 0:512].rearrange("p (g t) -> p g t", g=4), func=AF.Copy))
            xc = [0]

            def pre_norm(tile, part=0):
                xt = xts[tile % 2]
                h1 = hT1[tile % 2]
                if part != 2:
                    K.dma(SP, xt[:], x1s[tile * 128:(tile + 1) * 128, :], [dres("x1s%d" % tile)], [xt], xt)
                K.set_banks([3])
                norm_mod_T((sq, ss, hb), xt, cA, cB, h1, 0, part=part)

            qkts = [K.sb("qkt%d" % i, [128, 1536], F32, ps) for i in range(2)]

            def load_qkv(tile):
                qk = qkts[tile % 2]
                K.dma(SP, qk[:], qkvs[tile * 128:(tile + 1) * 128, :], [dres("qkvs%d" % tile)], [qk], qk)

            def prep(tile, qi, kslot, lat, ctx_row=None):
                qk = qkts[tile % 2]
                K.op(ACT, [qk], [sqq], lambda e: e.activation(out=sqq[:, 0:1280], in_=qk[:, 0:1280], func=AF.Square))
                K.op(ACT, [qk], [V], lambda e: e.activation(out=V[:, kslot, :, 0:64], in_=qk[:, 1280:1536].rearrange("p (g d) -> p g d", g=4), func=AF.Copy))
                yield
                segs = ((0, 8, 0), (512, 8, 1), (1024, 4, 2))
                for (c0, nh_, si) in segs:
                    a1, a2, a3 = s1s[si], s2s[si], s3s[si]
                    K.op(DVE, [sqq], [a1], lambda e, c0=c0, nh_=nh_, a1=a1: e.tensor_reduce(
                        out=a1[:, 0:nh_], in_=sqq[:, c0:c0 + nh_ * 64].rearrange("p (h d) -> p h d", d=64), axis=AX.X, op=ALU.add))
                    K.op(DVE, [a1], [a2], lambda e, nh_=nh_, a1=a1, a2=a2: e.tensor_scalar(out=a2[:, 0:nh_], in0=a1[:, 0:nh_], scalar1=1.0 / HD, scalar2=EPS,
                                                                                           op0=ALU.mult, op1=ALU.add))
                    K.op(POOL, [a2, mhalf], [a3], lambda e, nh_=nh_, a2=a2, a3=a3: e.tensor_tensor(out=a3[:, 0:nh_], in0=a2[:, 0:nh_], in1=mhalf[:, 0:nh_], op=ALU.pow))
                    src = qk[:, c0:c0 + nh_ * 64]
                    K.op(DVE, [qk, a3], [qf],
                         lambda e, c0=c0, nh_=nh_, a3=a3, src=src: e.tensor_tensor(
                             out=qf[:, c0:c0 + nh_ * 64].rearrange("p (h d) -> p h d", d=64), in0=src.rearrange("p (h d) -> p h d", d=64),
                             in1=a3[:, 0:nh_].unsqueeze(2).to_broadcast([128, nh_, 64]), op=ALU.mult))
                yield
                dstq = ta if lat else qb
                K.op(POOL, [qf, gq], [dstq], lambda e: e.tensor_tensor(out=dstq[:, 0:1024].rearrange("p (h d) -> p h d", d=64),
                                                                       in0=qf[:, 0:1024].rearrange("p (h d) -> p h d", d=64),
                                                                       in1=gq[:].unsqueeze(1).to_broadcast([128, 16, 64]), op=ALU.mult))
                dstk = ta if lat else tb
                K.op(POOL, [qf, gk], [dstk], lambda e: e.tensor_tensor(out=dstk[:, 1024:1280].rearrange("p (h d) -> p h d", d=64),
                                                                       in0=qf[:, 1024:1280].rearrange("p (h d) -> p h d", d=64),
                                                                       in1=gk[:].unsqueeze(1).to_broadcast([128, 4, 64]), op=ALU.mult))
                yield
                if lat:
                    v5 = lambda t_: t_[:].rearrange("p (h r x f) -> p h r x f", h=20, r=2, x=2, f=16)
                    cosb = rc[:, tile, :].unsqueeze(1).to_broadcast([128, 20, 64])
                    sn = rs[:, tile, :].rearrange("p (r x f) -> p r x f", r=2, x=2, f=16)
                    K.op(DVE, [ta, rc], [qf], lambda e: e.tensor_tensor(out=qf[:].rearrange("p (h d) -> p h d", d=64),
                                                                        in0=ta[:].rearrange("p (h d) -> p h d", d=64), in1=cosb, op=ALU.mult))
                    for x in range(2):
                        K.op(POOL if x == 0 else DVE, [ta, rs], [tb],
                             lambda e, x=x: e.tensor_tensor(out=v5(tb)[:, :, :, x, :], in0=v5(ta)[:, :, :, 1 - x, :],
                                                            in1=sn[:, :, x, :].unsqueeze(1).to_broadcast([128, 20, 2, 16]), op=ALU.mult))
                    K.op(DVE, [qf, tb], [qb], lambda e: e.tensor_tensor(out=qb[:], in0=qf[:], in1=tb[:], op=ALU.add))
                else:
                    K.dma(SP, newk[ctx_row:ctx_row + 128, :], tb[:, 1024:1280], [tb], [dres("newk%d" % ctx_row)], tb, store=True)
                    K.dma(SP, newv[ctx_row:ctx_row + 128, :], qk[:, 1280:1536], [qk], [dres("newv%d" % ctx_row)], qk, store=True)
                    K.op(POOL, [tb], [qb], lambda e: e.tensor_copy(out=qb[:, 1024:1280], in_=tb[:, 1024:1280]))
                yield
                QT = QTs[qi]
                for n in range(2):
                    K.group(PE, [qb, ident], [K.banks[n]],
                            [(lambda e, n=n, h=h: e.transpose(out=K.bap(n, BF16)[0:64, h * 128:(h + 1) * 128],
                                                              in_=qb[:, (n * 8 + h) * 64:(n * 8 + h + 1) * 64], identity=ident[:])) for h in range(8)])
                    K.op(DVE, [K.banks[n]], [QT],
                         (lambda e, n=n: e.tensor_copy(out=QT[0:64, n * 8:(n + 1) * 8, :], in_=K.bap(n, BF16)[0:64, :].rearrange("p (h t) -> p h t", h=8))))
                K.group(PE, [qb, ident], [K.banks[2]],
                        [(lambda e, g=g: e.transpose(out=K.bap(2, BF16)[0:64, g * 128:(g + 1) * 128],
                                                     in_=qb[:, 1024 + g * 64:1024 + (g + 1) * 64], identity=ident[:])) for g in range(4)])
                K.op(DVE, [K.banks[2]], [KT], lambda e: e.tensor_copy(out=KT[0:64, :, kslot * 128:(kslot + 1) * 128],
                                                                      in_=K.bap(2, BF16)[0:64, 0:512].rearrange("p (g t) -> p g t", g=4)))

            pc = [0]
            pp = [0]

            def attend(tile, qi, keys):
                QT = QTs[qi]
                ao = aos[tile % 2]
                nk = len(keys)
                pending = [None]

                def kv(g, key):
                    kind, slot, mask = key
                    if kind == 'c':
                        return cKT[:, g, slot * 128:(slot + 1) * 128], cV[:, slot, g, :], cKT, cV
                    return KT[:, g, slot * 128:(slot + 1) * 128], V[:, slot, g, :], KT, V

                for g in range(4):
                    sb_ = {}

                    def emit_S(i, g=g, sb_=sb_):
                        bs = (3, 4, 5)[pc[0] % 3]
                        pc[0] += 1
                        sb_[i] = bs
                        kap, vap, kr, vr = kv(g, keys[i])
                        mk = keys[i][2]
                        pairs_ = [(kap, QT[:, 4 * g:4 * g + 4, :])]
                        rd_ = [kr, QT]
                        if mk is not None:
                            mbt = mb_prev if mk is mprev else mb_next
                            pairs_.append((ident[:], mbt[:].rearrange("p h t -> p (h t)")))
                            rd_ += [ident, mbt]
                        K.mm(bs, pairs_, 128, 512, rd_)

                    emit_S(0)
                    if nk > 1:
                        emit_S(1)
                    for i in range(nk):
                        if i + 2 < nk:
                            emit_S(i + 2)
                        bs = sb_[i]
                        P = Pt[pp[0] % 3]
                        pp[0] += 1
                        kap, vap, kr, vr = kv(g, keys[i])
                        mask = keys[i][2]
                        K.op(ACT, [K.banks[bs]], [P], lambda e, bs=bs, P=P: e.activation(out=P[:], in_=K.bap(bs), func=AF.Exp, scale=0.125))
                        K.mm(6, [(vap, P[:])], 96, 512, [vr, P], first=(i == 0), last=False)
                        if i == nk - 1:
                            K.mm(6, [(sel0[0:1, :], sinkb[0:1, 4 * g:4 * g + 4, :].rearrange("p h t -> p (h t)"))], 96, 512, [sel0, sinkb], first=False, last=True)
                        if i == 1 and pending[0] is not None:
                            pending[0]()
                            pending[0] = None
                    O = Osb[g % 2]
                    rd = rdens[g % 2]
                    K.op(ACT, [K.banks[6]], [rd], lambda e, rd=rd: e.activation(out=rd[64:65, :], in_=K.bap(6)[64:65, :], func=AF.Ln))
                    rdb = rdbs[g % 2]
                    K.op(ACT, [rd], [rdb], lambda e, rd=rd, rdb=rdb: e.activation(out=rdb[64:65, :], in_=rd[64:65, :], func=AF.Exp, scale=-1.0))
                    K.op(ACT, [K.banks[6]], [O], lambda e, O=O: e.activation(out=O[0:64, :], in_=K.bap(6)[0:64, :], func=AF.Copy))

                    def fin(O=O, rdb=rdb, g=g):
                        K.mm(7, [(ones_b[:, 0:64], rdb[:, :])], 64, 512, [ones_b, rdb])
                        K.op(DVE, [O, K.banks[7]], [ao], lambda e: e.tensor_tensor(
                            out=ao[:, 4 * g:4 * g + 4, :].rearrange("p h t -> p (h t)"), in0=O[0:64, :], in1=K.bap(7)[0:64, :], op=ALU.mult))
                    pending[0] = fin
                    if g < 3:
                        yield
                pending[0]()
                K.dma(SP, attT[:, :, tile * 128:(tile + 1) * 128], ao[:], [ao], [dres("attT%d" % (tile // 4))], ao, store=True)

            load_consts((cB, cA), 1, 0, which=(0, 1))
            ck = [('c', t, None) for t in range(4)]

            def lat_keys(j):
                keys = list(ck)
                if j >= 1:
                    keys.append(('l', j - 1, mprev))
                keys.append(('l', j, None))
                if j + 1 < 32:
                    keys.append(('l', j + 1, mnext))
                return keys

            def step(gen):
                if gen is not None:
                    next(gen, None)

            load_qkv(0)
            for i in range(34):
                j = i - 2
                P = prep(i, i % 3, i, True) if i < 32 else None
                A = attend(j, j % 3, lat_keys(j)) if j >= 0 else None
                step(P)
                step(A)
                step(P)
                step(A)
                step(P)
                if i + 1 < 32:
                    load_qkv(i + 1)
                step(A)
                step(P)
                step(A)
                step(P)
            load_consts((cB, cA), 1, 1, which=(0, 1))
            for sq_ in range(2):
                for t in range(2):
                    tl = 32 + 2 * sq_ + t
                    load_qkv(tl)
                    for _ in prep(tl, t, t, False, ctx_row=sq_ * 256 + t * 128):
                        pass
                for t in range(2):
                    for _ in attend(32 + 2 * sq_ + t, t, [('l', 0, None), ('l', 1, None)]):
                        pass
            K.set_banks(range(8))
        K.barrier()

    p3()
    if upto < 4:
        return


    def hyena_phases():
        nyq_t = K.sb("nyq_t", [128, 128], BF16)
        nyqp_t = K.sb("nyqp_t", [128, 2], BF16)
        K.dma(SP, nyq_t[:], nyq, [], [nyq_t], nyq_t)
        K.dma(SP, nyqp_t[:], nyqp, [], [nyqp_t], nyqp_t)
        for n in (NLAT, NCTX):
            filters(n, nyqp_t)
        def run_gens(gens):
            active = list(gens)
            while active:
                for g_ in list(active):
                    if next(g_, "end") == "end":
                        active.remove(g_)

        with ExitStack() as ps_:
            run_gens([hyconv(0, NLAT, nyq_t, nyqp_t, ps_)])
            K.barrier()
        with ExitStack() as ps_:
            run_gens([hyconv(NLAT + sq_ * NCTX, NCTX, nyq_t, nyqp_t, ps_) for sq_ in range(2)])
            K.barrier()

    def filters(n, nyqp_t):
        CH = min(512, n)
        nch = n // CH
        with ExitStack() as ps:
            w1 = K.sb("fw1", [FE, FW], F32, ps)
            w2 = K.sb("fw2", [FW, FW], F32, ps)
            fr = K.sb("ffr", [FW, 4], F32, ps)
            K.dma(SP, w1[:], filt_w1, [], [w1], w1)
            K.dma(SP, w2[:], filt_w2, [], [w2], w2)
            with nc.allow_non_contiguous_dma(reason="tiny"):
                K.dma(SP, fr[:, 0:1], filt_freq.rearrange("(p o) -> p o", o=1), [], [fr], fr)
                K.dma(SP, fr[:, 1:2], filt_b1.rearrange("(p o) -> p o", o=1), [], [fr], fr)
                K.dma(SP, fr[:, 2:3], filt_b2.rearrange("(p o) -> p o", o=1), [], [fr], fr)
            sc3 = K.sb("sc3", [FW, 4], F32, ps)
            K.op(DVE, [fr], [sc3], lambda e: e.tensor_scalar(out=sc3[:, 0:1], in0=fr[:, 0:1], scalar1=1.0 / 3.0, scalar2=None, op0=ALU.mult))
            K.op(DVE, [fr, sc3], [sc3], lambda e: e.tensor_tensor(out=sc3[:, 1:2], in0=fr[:, 1:2], in1=sc3[:, 0:1], op=ALU.mult))
            K.op(DVE, [fr, sc3], [sc3], lambda e: e.tensor_tensor(out=sc3[:, 2:3], in0=fr[:, 2:3], in1=sc3[:, 0:1], op=ALU.mult))
            h2T = K.sb("h2T", [FW, n], F32, ps)
            h2Tb = K.sb("h2Tb", [FW, n], BF16, ps)
            ps2 = ExitStack()
            zT = K.sb("zT", [FE, n], F32, ps2)
            K.dma(SP, zT[:], zemb[n], [], [zT], zT)
            h1T = K.sb("h1T", [FW, n], F32, ps2)
            ts = K.sb("ts", [FW, CH], F32, ps2)
            tu = K.sb("tu", [FW, CH], F32, ps2)
            for (wm, kdim, src, dst, bcol) in ((w1, FE, zT, h1T, 1), (w2, FW, h1T, h2T, 2)):
                for ch in range(nch):
                    b = K.bank()
                    K.mm(b, [(wm[0:kdim, :], src[0:kdim, ch * CH:(ch + 1) * CH])], FW, CH, [wm, src])
                    K.op(ACT, [K.banks[b], sc3], [ts], lambda e, b=b, bcol=bcol: e.activation(out=ts[:], in_=K.bap(b)[0:FW, 0:CH], func=AF.Sin,
                                                                                             bias=sc3[:, bcol:bcol + 1], scale=sc3[:, 0:1]))
                    K.op(DVE, [ts], [tu], lambda e: e.tensor_tensor(out=tu[:], in0=ts[:], in1=ts[:], op=ALU.mult))
                    K.op(DVE, [tu], [tu], lambda e: e.tensor_scalar(out=tu[:], in0=tu[:], scalar1=-4.0, scalar2=3.0, op0=ALU.mult, op1=ALU.add))
                    K.op(DVE, [ts, tu], [dst], lambda e, dst=dst, ch=ch: e.tensor_tensor(out=dst[:, ch * CH:(ch + 1) * CH], in0=ts[:], in1=tu[:], op=ALU.mult))
            K.op(ACT, [h2T], [h2Tb], lambda e: e.activation(out=h2Tb[:], in_=h2T[:], func=AF.Copy))
            K.barrier()
            ps2.close()
            nmt = n // 256
            CS = min(8, nmt)
            nq = nmt // CS
            M_ = mats[n]
            tc_ = K.sb("tc", [128, 2, nmt], F32, ps)
            K.dma(SP, tc_[:], tcol[n], [], [tc_], tc_)
            wkt = K.sb("wkt", [128, 2, nmt + 1], F32, ps)
            K.dma(SP, wkt[:], wk[n], [], [wkt], wkt)
            w3o = K.sb("w3o", [FW, 2, 512], BF16, ps)
            Eabs = [K.sb("Eab%d" % i, [128, 2, 512], BF16, ps) for i in range(2)]
            ones_bf = K.sb("ones_bf", [128, 2], BF16, ps)
            K.op(POOL, [], [ones_bf], lambda e: e.memset(ones_bf[:], 1.0))
            b3b = K.sb("b3b", [128, 2, 512], F32, ps)
            ndb = K.sb("ndb", [128, 2, 512], F32, ps)
            skb = K.sb("skb", [128, 512], F32, ps)
            FG = K.sb("FG", [128, 2, nmt, 2, 512], BF16, ps)
            FGp = [FG, Tile("FGg", FG.t)]
            hvs = [K.sb("hv%d" % i, [128, 2, 512], F32, ps) for i in range(2)]
            Ees = [K.sb("Ee%d" % i, [128, 2, 512], F32, ps) for i in range(2)]
            nrm = K.sb("nrm", [1, 512], F32, ps)
            rb = K.sb("rb", [128, 512], F32, ps)
            NR = 4
            mbuf = {nm: [K.sb("m%s%d" % (nm, i), [128, CS, 128], BF16, ps) for i in range(NR)] for nm in ("ce", "co", "se", "so")}
            e4 = K.sb("e4", [128, 4, 512], F32, ps)
            ksts = [K.sb("kst%d" % i, [128, 4, 512], F32, ps) for i in range(2)]
            kstb = [K.sb("kstb%d" % i, [128, 4, 512], BF16, ps) for i in range(2)]
            mc = [0]
            f2 = lambda t_: t_[:].rearrange("p a c -> p (a c)")
            for o in range(2):
              for chf in range(2):
                for d_ in range(2):
                    c0 = o * 2048 + d_ * 1024 + chf * 512
                    K.dma(POOL, w3o[:, d_, :], filt_w3[:, c0:c0 + 512], [], [w3o], w3o)
                    K.dma(SP, b3b[:, d_, :], filt_b3[c0:c0 + 512].partition_broadcast(128), [], [b3b], b3b)
                    K.dma(SP, ndb[:, d_, :], filt_decay[c0:c0 + 512].partition_broadcast(128), [], [ndb], ndb)
                K.dma(SP, skb[:], hyena_skip[o * 1024 + chf * 512:o * 1024 + (chf + 1) * 512].partition_broadcast(128), [], [skb], skb)
                K.op(DVE, [ndb], [ndb], lambda e: e.scalar_tensor_tensor(out=f2(ndb), in0=f2(ndb), scalar=-1.0, in1=f2(ndb), op0=ALU.mult, op1=ALU.min))
                K.set_banks([1, 2, 3, 4, 5, 6, 7])
                first_acc = [True]
                steps = [(par, tile) for par in range(2) for tile in range(nmt)]

                def front(i):
                    par, tile = steps[i]
                    hv, Ee = hvs[i % 2], Ees[i % 2]
                    K.op(ACT, [ndb, tc_], [Ee], lambda e: e.activation(out=f2(Ee), in_=f2(ndb), func=AF.Exp, scale=tc_[:, par, tile:tile + 1]))
                    tok = h2Tb[:, 256 * tile + par:256 * tile + 256:2]
                    for d_ in range(2):
                        b = K.bank()
                        K.mm(b, [(tok, w3o[:, d_, :])], 128, 512, [h2Tb, w3o])
                        K.op(DVE, [K.banks[b], b3b], [hv], lambda e, b=b, d_=d_: e.tensor_tensor(out=hv[:, d_, :], in0=K.bap(b), in1=b3b[:, d_, :], op=ALU.add))

                def back(i):
                    par, tile = steps[i]
                    hv, Ee = hvs[i % 2], Ees[i % 2]
                    last_t = (i == len(steps) - 1)
                    K.op(POOL, [hv, Ee], [hv], lambda e: e.tensor_tensor(out=hv[:, 0, :], in0=hv[:, 0, :], in1=Ee[:, 0, :], op=ALU.mult))
                    K.op(DVE, [hv, Ee], [hv], lambda e: e.tensor_tensor(out=hv[:, 1, :], in0=hv[:, 1, :], in1=Ee[:, 1, :], op=ALU.mult))
                    Eab = Eabs[i % 2]
                    K.op(ACT, [hv], [Eab], lambda e: e.activation(out=f2(Eab), in_=f2(hv), func=AF.Abs))
                    for d_ in range(2):
                        K.mm(0, [(ones_bf[:, 0:1], Eab[:, d_, :])], 1, 512, [Eab, ones_bf], first=first_acc[0], last=(last_t and d_ == 1))
                        first_acc[0] = False
                    if i == 0:
                        K.op(POOL, [Eab], [hv], lambda e: e.memset(hv[0:1, 1, :], 0.0))
                    K.op(DVE, [hv], [FGp[0]], lambda e: e.tensor_tensor(out=FG[:, par, tile, 0, :], in0=hv[:, 0, :], in1=hv[:, 1, :], op=ALU.add))
                    K.op(POOL, [hv], [FGp[1]], lambda e: e.tensor_tensor(out=FG[:, par, tile, 1, :], in0=hv[:, 1, :], in1=hv[:, 0, :], op=ALU.subtract))

                for i in range(len(steps) + 1):
                    if i < len(steps):
                        front(i)
                    if i >= 1:
                        back(i - 1)
                K.op(DVE, [K.banks[0]], [nrm], lambda e: e.tensor_scalar(out=nrm[:], in0=K.bap(0)[0:1, :], scalar1=EPS, scalar2=None, op0=ALU.add))
                K.op(DVE, [nrm], [nrm], lambda e: e.reciprocal(out=nrm[:], in_=nrm[:]))
                K.mm(1, [(ones_f[0:1, :], nrm[0:1, :])], 128, 512, [ones_f, nrm])
                K.op(ACT, [K.banks[1]], [rb], lambda e: e.activation(out=rb[:], in_=K.bap(1), func=AF.Copy))

                def finalize(kst, rows, kt):
                    r = slice(0, rows)
                    K.op(DVE, [e4], [kst], lambda e: e.tensor_tensor(out=kst[r, 0, :], in0=e4[r, 0, :], in1=e4[r, 1, :], op=ALU.add))
                    K.op(POOL, [e4], [kst], lambda e: e.tensor_tensor(out=kst[r, 2, :], in0=e4[r, 0, :], in1=e4[r, 1, :], op=ALU.subtract))
                    K.op(DVE, [e4], [kst], lambda e: e.tensor_tensor(out=kst[r, 1, :], in0=e4[r, 2, :], in1=e4[r, 3, :], op=ALU.add))
                    K.op(POOL, [e4], [kst], lambda e: e.tensor_tensor(out=kst[r, 3, :], in0=e4[r, 3, :], in1=e4[r, 2, :], op=ALU.subtract))
                    K.op(DVE, [kst, rb], [kst], lambda e: e.tensor_tensor(out=kst[r], in0=kst[r], in1=rb[r].unsqueeze(1).to_broadcast([rows, 4, 512]), op=ALU.mult))
                    K.op(POOL, [kst, skb], [kst], lambda e: e.tensor_tensor(out=kst[r, 0, :], in0=kst[r, 0, :], in1=skb[r], op=ALU.add))
                    K.op(POOL, [kst, skb], [kst], lambda e: e.tensor_tensor(out=kst[r, 2, :], in0=kst[r, 2, :], in1=skb[r], op=ALU.add))
                    kb = kstb[kt % 2]
                    K.op(ACT, [kst, wkt], [kb], lambda e: e.activation(out=kb[r, 0:2, :].rearrange("p a c -> p (a c)"), in_=kst[r, 0:2, :].rearrange("p a c -> p (a c)"),
                                                                      func=AF.Copy, scale=wkt[r, 0, kt:kt + 1]))
                    K.op(ACT, [kst, wkt], [kb], lambda e: e.activation(out=kb[r, 2:4, :].rearrange("p a c -> p (a c)"), in_=kst[r, 2:4, :].rearrange("p a c -> p (a c)"),
                                                                      func=AF.Copy, scale=wkt[r, 1, kt:kt + 1]))
                    K.dma(ACT, kfs[n][o, kt, :, r, chf * 512:(chf + 1) * 512].rearrange("a p c -> p a c"), kb[r], [kb], [dres("kfs%d_%d_%d_%d" % (n, o, kt, chf))], kb, store=True)

                K.set_banks(range(8))
                grp = (("ce", 0, 0), ("co", 1, 0), ("se", 0, 1), ("so", 1, 1))
                for kt in range(nmt):
                    base = 4 * (kt % 2)
                    for q in range(nq):
                        bufs = {}
                        for nm, par, pl in grp:
                            mb = mbuf[nm][mc[0] % NR]
                            K.dma(SP, mb[:], M_[nm][kt, :, q * CS:(q + 1) * CS, :], [], [mb], mb)
                            bufs[nm] = mb
                        mc[0] += 1
                        for gi, (nm, par, pl) in enumerate(grp):
                            mb = bufs[nm]
                            K.mm(base + gi, [(mb[:, i, :], FG[:, par, q * CS + i, pl, :]) for i in range(CS)], 128, 512, [mb, FGp[pl]], first=(q == 0), last=(q == nq - 1))
                    K.op(ACT, [K.banks[base + i] for i in range(4)], [e4], lambda e, base=base: e.activation(out=e4[:].rearrange("p a c -> p (a c)"), in_=K.bap(base, nb=4), func=AF.Copy))
                    finalize(ksts[kt % 2], 128, kt)
                K.op(POOL, [], [e4], lambda e: e.memset(e4[0:1].rearrange("p a c -> p (a c)"), 0.0))
                K.mm(0, [(nyqp_t[:, 0:1], FG[:, 0, mt, 0, :]) for mt in range(nmt)], 1, 512, [nyqp_t, FGp[0]])
                K.mm(1, [(nyqp_t[:, 0:1], FG[:, 1, mt, 1, :]) for mt in range(nmt)], 1, 512, [nyqp_t, FGp[1]])
                K.op(ACT, [K.banks[0]], [e4], lambda e: e.activation(out=e4[0:1, 0, :], in_=K.bap(0)[0:1, :], func=AF.Copy))
                K.op(ACT, [K.banks[1]], [e4], lambda e: e.activation(out=e4[0:1, 3, :], in_=K.bap(1)[0:1, :], func=AF.Copy))
                finalize(ksts[nmt % 2], 1, nmt)
            K.set_banks(range(8))
        K.barrier()

    hw = [0]

    def hyconv(tok0, n, nyq_t, nyqp_t, ps):
        nmt = n // 256
        CS = min(8, nmt)
        nq = nmt // CS
        M_ = mats[n]
        nh2 = max(n // 2, 128)
        if True:
            cw = K.sb("cw", [128, 24, 3], F32, ps)
            cb = K.sb("cb", [128, 24], F32, ps)
            with nc.allow_non_contiguous_dma(reason="tiny"):
                for j in range(3):
                    K.dma(SP, cw[:, :, j], conv_w[j].rearrange("(ct p) -> p ct", p=128), [], [cw], cw)
                K.dma(SP, cb[:], conv_b.rearrange("(ct p) -> p ct", p=128), [], [cb], cb)
            WM = 384 if n == NLAT else 512
            NJM = WM // 128
            u = K.sb("u", [128, n + 2], BF16, ps)
            K.op(POOL, [], [u], lambda e: e.memset(u[:], 0.0))
            acc = K.sb("acc", [128, nh2], F32, ps)
            fTs = [K.sb("fT%d" % i, [128, NJM, n], BF16, ps) for i in range(2)]
            z = K.sb("z", [128, 2, nmt, WM], BF16, ps)
            Y = K.sb("Y", [128, nmt, 4, WM], BF16, ps)
            Yp = [Tile("Yp%d" % i, Y.t) for i in range(4)]
            Yx = K.sb("Yx", [1, 2, WM], BF16, ps)
            kfx = K.sb("kfx", [1, 4, WM], BF16, ps)
            tx = [K.sb("tx%d" % i, [1, WM], F32, ps) for i in range(3)]
            mnames = ("ce", "co", "se", "so", "cot", "sot")
            mbuf = {nm: [K.sb("m%s%d" % (nm, i), [128, CS, 128], BF16, ps) for i in range(4)] for nm in ("ce", "co", "se", "so")}
            mbuf["cot"] = mbuf["co"]
            mbuf["sot"] = mbuf["so"]
            kft = [K.sb("kft%d" % i, [128, 4, WM], BF16, ps) for i in range(2)]
            e4s = [K.sb("e4%d" % i, [128, 4, WM], BF16, ps) for i in range(2)]
            tq = [K.sb("tq%d" % i, [128, WM], BF16, ps) for i in range(8)]
            xtl = [K.sb("xtl%d" % i, [128, WM], BF16, ps) for i in range(2)]
            hyt = [K.sb("hyt%d" % i, [128, WM], BF16, ps) for i in range(2)]
            hst = [K.sb("hst%d" % i, [128, NJM, 256], BF16, ps) for i in range(2)]
            mc = [0]
            hres = [dres("hyT%d" % c) for c in range(tok0 // 512, (tok0 + n + 511) // 512)]
            chunks = ((0, 3), (3, 3), (6, 2)) if n == NLAT else ((0, 4), (4, 4))

            def conv_steps(ci, blk):
                j0_, nj_ = chunks[ci]
                fT_ = fTs[ci % 2]
                for j in range(nj_):
                    ct = blk * 8 + j0_ + j
                    K.dma(SP, u[:, 1:n + 1], hyT[ct * 128:(ct + 1) * 128, tok0:tok0 + n], hres, [u], u)
                    for hh in range(n // nh2):
                        r0 = hh * nh2
                        K.op(DVE, [u, cw, cb], [acc], lambda e, ct=ct, r0=r0: e.tensor_scalar(out=acc[:], in0=u[:, 1 + r0:1 + r0 + nh2], scalar1=cw[:, ct, 1:2], scalar2=cb[:, ct:ct + 1],
                                                                                          op0=ALU.mult, op1=ALU.add))
                        K.op(DVE, [u, acc, cw], [acc], lambda e, ct=ct, r0=r0: e.scalar_tensor_tensor(out=acc[:], in0=u[:, r0:r0 + nh2], scalar=cw[:, ct, 0:1], in1=acc[:],
                                                                                                  op0=ALU.mult, op1=ALU.add))
                        K.op(DVE, [u, acc, cw], [fT_], lambda e, ct=ct, j=j, r0=r0: e.scalar_tensor_tensor(out=fT_[:, j, r0:r0 + nh2], in0=u[:, 2 + r0:2 + r0 + nh2], scalar=cw[:, ct, 2:3],
                                                                                                       in1=acc[:], op0=ALU.mult, op1=ALU.add))
                    yield

            def ftile_(ci, tau, par, dst_ap, dst_res, bsel):
                j0_, nj_ = chunks[ci]
                fT_ = fTs[ci % 2]
                b = 6 + (bsel % 2)
                K.group(PE, [fT_, ident], [K.banks[b]],
                        [(lambda e, j=j: e.transpose(out=K.bap(b, BF16)[:, j * 128:(j + 1) * 128], in_=fT_[:, j, 256 * tau + par:256 * tau + 256:2], identity=ident[:]))
                         for j in range(nj_)])
                K.op(ACT, [K.banks[b]], [dst_res], lambda e: e.activation(out=dst_ap, in_=K.bap(b, BF16)[:, 0:nj_ * 128], func=AF.Copy))

            def prologue(ci):
                j0_, nj_ = chunks[ci]
                yield from conv_steps(ci, 2)
                for tau in range(nmt):
                    for par in range(2):
                        ftile_(ci, tau, par, z[:, par, tau, 0:nj_ * 128], z, 2 * tau + par)
                    yield
                yield from conv_steps(ci, 0)

            yield from prologue(0)
            for ci, (j0, nj) in enumerate(chunks):
                W = nj * 128
                c0 = j0 * 128
                fT = fTs[ci % 2]
                nxt = [None]

                def ftile(tau, par, dst_ap, dst_res, bsel, ci=ci):
                    ftile_(ci, tau, par, dst_ap, dst_res, bsel)


                fgrp = (("ce", 0), ("co", 1), ("se", 0), ("so", 1))
                for o in range(2):
                    for kt in range(nmt):
                        base = 4 * (kt % 2)
                        for q in range(nq):
                            bufs = {}
                            for nm, par in fgrp:
                                mb = mbuf[nm][mc[0] % 4]
                                K.dma(SP, mb[:], M_[nm][kt, :, q * CS:(q + 1) * CS, :], [], [mb], mb)
                                bufs[nm] = mb
                            mc[0] += 1
                            for gi, (nm, par) in enumerate(fgrp):
                                mb = bufs[nm]
                                K.mm(base + gi, [(mb[:, i, :], z[:, par, q * CS + i, 0:W]) for i in range(CS)], 128, W, [mb, z], first=(q == 0), last=(q == nq - 1))
                        kf = kft[kt % 2]
                        K.dma(SP, kf[:, :, 0:W], kfs[n][o, kt, :, :, c0:c0 + W].rearrange("a p c -> p a c"), [dres("kfs%d_%d_%d_%d" % (n, o, kt, ch_)) for ch_ in range(2)], [kf], kf)
                        e4 = e4s[kt % 2]
                        for gi in range(4):
                            K.op(ACT, [K.banks[base + gi]], [e4], lambda e, base=base, gi=gi, e4=e4: e.activation(out=e4[:, gi, 0:W], in_=K.bap(base + gi)[:, 0:W], func=AF.Copy))
                        Ec, Oc, Es, Os = (e4[:, i, 0:W] for i in range(4))
                        t = [tq[i][:, 0:W] for i in range(8)]
                        tr = tq
                        TT = lambda E_, o_, a_, b_, op_, rd, wr: K.op(E_, rd, wr, lambda e: e.tensor_tensor(out=o_, in0=a_, in1=b_, op=op_))
                        TT(DVE, t[0], Ec, Oc, ALU.add, [e4], [tr[0]])
                        TT(POOL, t[1], Ec, Oc, ALU.subtract, [e4], [tr[1]])
                        TT(DVE, t[2], Es, Os, ALU.add, [e4], [tr[2]])
                        TT(POOL, t[3], Os, Es, ALU.subtract, [e4], [tr[3]])
                        KreA, KimA, KreB, KimB = (kf[:, i, 0:W] for i in range(4))
                        TT(DVE, t[4], t[0], KreA, ALU.mult, [tr[0], kf], [tr[4]])
                        TT(DVE, t[5], t[2], KimA, ALU.mult, [tr[2], kf], [tr[5]])
                        TT(DVE, t[4], t[4], t[5], ALU.add, [tr[4], tr[5]], [tr[4]])
                        TT(DVE, t[5], t[2], KreA, ALU.mult, [tr[2], kf], [tr[5]])
                        TT(DVE, t[0], t[0], KimA, ALU.mult, [tr[0], kf], [tr[0]])
                        TT(DVE, t[5], t[5], t[0], ALU.subtract, [tr[5], tr[0]], [tr[5]])
                        TT(POOL, t[6], t[1], KreB, ALU.mult, [tr[1], kf], [tr[6]])
                        TT(POOL, t[7], t[3], KimB, ALU.mult, [tr[3], kf], [tr[7]])
                        TT(POOL, t[6], t[6], t[7], ALU.add, [tr[6], tr[7]], [tr[6]])
                        TT(POOL, t[7], t[3], KreB, ALU.mult, [tr[3], kf], [tr[7]])
                        TT(POOL, t[1], t[1], KimB, ALU.mult, [tr[1], kf], [tr[1]])
                        TT(POOL, t[7], t[7], t[1], ALU.subtract, [tr[7], tr[1]], [tr[7]])
                        TT(DVE, Y[:, kt, 0, 0:W], t[4], t[6], ALU.add, [tr[4], tr[6]], [Yp[0]])
                        TT(POOL, Y[:, kt, 1, 0:W], t[4], t[6], ALU.subtract, [tr[4], tr[6]], [Yp[1]])
                        TT(DVE, Y[:, kt, 2, 0:W], t[5], t[7], ALU.subtract, [tr[5], tr[7]], [Yp[2]])
                        TT(POOL, Y[:, kt, 3, 0:W], t[5], t[7], ALU.add, [tr[5], tr[7]], [Yp[3]])
                        yield
                    K.mm(0, [(nyqp_t[:, 0:1], z[:, 0, mt, 0:W]) for mt in range(nmt)], 1, W, [nyqp_t, z])
                    K.mm(1, [(nyqp_t[:, 0:1], z[:, 1, mt, 0:W]) for mt in range(nmt)], 1, W, [nyqp_t, z])
                    K.dma(SP, kfx[:, :, 0:W], kfs[n][o, nmt, :, 0:1, c0:c0 + W].rearrange("a p c -> p a c"), [dres("kfs%d_%d_%d_%d" % (n, o, nmt, ch_)) for ch_ in range(2)], [kfx], kfx)
                    t0, t1, t2 = (tx[i][:, 0:W] for i in range(3))
                    K.op(DVE, [K.banks[0], kfx], [tx[0]], lambda e: e.tensor_tensor(out=t0, in0=K.bap(0)[0:1, 0:W], in1=kfx[:, 0, 0:W], op=ALU.mult))
                    K.op(DVE, [K.banks[1], kfx], [tx[1]], lambda e: e.tensor_tensor(out=t1, in0=K.bap(1)[0:1, 0:W], in1=kfx[:, 1, 0:W], op=ALU.mult))
                    K.op(DVE, [tx[0], tx[1]], [Yx], lambda e: e.tensor_tensor(out=Yx[:, 0, 0:W], in0=t0, in1=t1, op=ALU.add))
                    K.op(DVE, [K.banks[1], kfx], [tx[0]], lambda e: e.tensor_tensor(out=t0, in0=K.bap(1)[0:1, 0:W], in1=kfx[:, 0, 0:W], op=ALU.mult))
                    K.op(DVE, [K.banks[0], kfx], [tx[1]], lambda e: e.tensor_tensor(out=t1, in0=K.bap(0)[0:1, 0:W], in1=kfx[:, 1, 0:W], op=ALU.mult))
                    K.op(DVE, [tx[0], tx[1]], [Yx], lambda e: e.tensor_tensor(out=Yx[:, 1, 0:W], in0=t0, in1=t1, op=ALU.subtract))
                    igrp = (("ce", 0, 0), ("se", 0, 2), ("cot", 1, 1), ("sot", 1, 3))
                    if o == 1 and ci + 1 < len(chunks):
                        nxt[0] = prologue(ci + 1)
                    for tau in range(nmt):
                        if nxt[0] is not None:
                            for _ in range(2):
                                if next(nxt[0], "end") == "end":
                                    nxt[0] = None
                                    break
                        by = [0 + 2 * (tau % 2), 1 + 2 * (tau % 2)]
                        for q in range(nq):
                            bufs = {}
                            for nm, par, pl in igrp:
                                mb = mbuf[nm][mc[0] % 4]
                                K.dma(SP, mb[:], M_[nm][tau, :, q * CS:(q + 1) * CS, :], [], [mb], mb)
                                bufs[nm] = mb
                            mc[0] += 1
                            for par in range(2):
                                pairs = []
                                rd = list(Yp)
                                for nm, par_, pl in igrp:
                                    if par_ == par:
                                        pairs += [(bufs[nm][:, i, :], Y[:, q * CS + i, pl, 0:W]) for i in range(CS)]
                                        rd.append(bufs[nm])
                                K.mm(by[par], pairs, 128, W, rd, first=(q == 0), last=False)
                        for par in range(2):
                            K.mm(by[par], [(nyq_t[0:1, :], Yx[0:1, par, 0:W])], 128, W, [nyq_t, Yx], first=False, last=True)
                        hs = hst[tau % 2]
                        for par in range(2):
                            xt_ = xtl[par]
                            ftile(tau, par, xt_[:, 0:W], xt_, par)
                            if o == 0:
                                K.op(DVE, [K.banks[by[par]], xt_], [z], lambda e, par=par, xt_=xt_, tau=tau: e.tensor_tensor(out=z[:, par, tau, 0:W], in0=K.bap(by[par])[:, 0:W], in1=xt_[:, 0:W], op=ALU.mult))
                            else:
                                ht = hyt[par]
                                K.op(DVE, [K.banks[by[par]], xt_], [ht], lambda e, par=par, xt_=xt_, ht=ht: e.tensor_tensor(out=ht[:, 0:W], in0=K.bap(by[par])[:, 0:W], in1=xt_[:, 0:W], op=ALU.mult))
                                b = 4 + par
                                K.group(PE, [ht, ident], [K.banks[b]],
                                        [(lambda e, j=j, ht=ht, b=b: e.transpose(out=K.bap(b, BF16)[:, j * 128:(j + 1) * 128], in_=ht[:, j * 128:(j + 1) * 128], identity=ident[:]))
                                         for j in range(nj)])
                                K.op(ACT, [K.banks[b]], [hs], lambda e, b=b, hs=hs, par=par: e.activation(out=hs[:, 0:nj, par:256:2], in_=K.bap(b, BF16)[:, 0:W].rearrange("p (j t) -> p j t", j=nj), func=AF.Copy))
                        yield
                        if o == 1:
                            hw[0] += 1
                            K.dma(ACT, hyoT[c0:c0 + W, tok0 + tau * 256:tok0 + (tau + 1) * 256].rearrange("(j p) t -> p j t", p=128), hs[:, 0:nj, :],
                                  [hs], [dres("hyoT_w%d" % hw[0])], hs, store=True)
                    if o == 0:
                        yield from conv_steps(ci, 1)
                if nxt[0] is not None:
                    yield from nxt[0]


    def p6():
        TC = 512
        with ExitStack() as ps:
            cG = K.sb("mcG", [128, D], F32, ps)
            wab = load_w_bf16(ps, "wab", w_ab, 16, D, part=64)
            whb = load_w_bf16(ps, "whb", w_hb, 8, D)
            wo = load_w_bf16(ps, "wo", w_out, 8, D)
            ats = [K.sb("at%d" % i, [64, 16, TC], BF16, ps) for i in range(2)]
            hys = [K.sb("hy%d" % i, [128, 8, TC], BF16, ps) for i in range(2)]
            sgs = [K.sb("sg%d" % i, [128, 16, TC], BF16, ps) for i in range(2)]
            mT = K.sb("mT", [128, 8, TC], BF16, ps)
            t1s = [K.sb("t1%d" % i, [128, TC], F32, ps) for i in range(2)]
            t2s = [K.sb("t2%d" % i, [128, TC], F32, ps) for i in range(2)]
            xrs = [K.sb("xr%d" % i, [128, D], F32, ps) for i in range(2)]
            tmp = [K.sb("tmp%d" % i, [128, 512], F32, ps) for i in range(2)]
            nch = NT // 4

            def loads(c):
                at, hy, sg = ats[c % 2], hys[c % 2], sgs[c % 2]
                K.dma(SP, at[:], attT[:, :, c * TC:(c + 1) * TC], [dres("attT%d" % c)], [at], at)
                K.dma(SP, hy[:], hyoT[:, c * TC:(c + 1) * TC].rearrange("(k p) t -> p k t", p=128), [dres("hyoT")], [hy], hy)
                K.dma(SP, sg[:], sgT[:, c * TC:(c + 1) * TC].rearrange("(k p) t -> p k t", p=128), [dres("sgT%d" % c)], [sg], sg)

            load_consts((cG,), 1, 0, which=(2,))
            loads(0)
            for c in range(nch):
                if c + 1 < nch:
                    loads(c + 1)
                if c == 8:
                    load_consts((cG,), 1, 1, which=(2,))
                at, hy, sg = ats[c % 2], hys[c % 2], sgs[c % 2]
                for m in range(8):
                    ba = K.bank()
                    K.mm(ba, [(wab[:, h, m * 128:(m + 1) * 128], at[:, h, :]) for h in range(16)], 128, TC, [wab, at])
                    bh = K.bank()
                    K.mm(bh, [(whb[:, k, m * 128:(m + 1) * 128], hy[:, k, :]) for k in range(8)], 128, TC, [whb, hy])
                    t1, t2 = t1s[m % 2], t2s[m % 2]
                    K.op(DVE, [K.banks[ba], sg], [t1], lambda e, ba=ba, t1=t1, m=m, sg=sg: e.tensor_tensor(out=t1[:], in0=K.bap(ba), in1=sg[:, m, :], op=ALU.mult))
                    K.op(DVE, [K.banks[bh], sg], [t2], lambda e, bh=bh, t2=t2, m=m, sg=sg: e.tensor_tensor(out=t2[:], in0=K.bap(bh), in1=sg[:, 8 + m, :], op=ALU.mult))
                    K.op(POOL, [t1, t2], [mT], lambda e, t1=t1, t2=t2, m=m: e.tensor_tensor(out=mT[:, m, :], in0=t1[:], in1=t2[:], op=ALU.add))
                for t4 in range(4):
                    tile = c * 4 + t4
                    xr = xrs[tile % 2]
                    K.dma(SP, xr[:], x1s[tile * 128:(tile + 1) * 128, :], [dres("x1s%d" % tile)], [xr], xr)
                    for nh in range(2):
                        by = K.bank()
                        K.mm(by, [(mT[:, k, t4 * 128:(t4 + 1) * 128], wo[:, k, nh * 512:(nh + 1) * 512]) for k in range(8)], 128, 512, [mT, wo])
                        tm = tmp[nh]
                        K.op(DVE, [K.banks[by], cG], [tm], lambda e, by=by, tm=tm, nh=nh: e.tensor_tensor(out=tm[:], in0=K.bap(by), in1=cG[:, nh * 512:(nh + 1) * 512], op=ALU.mult))
                        K.op(POOL, [tm, xr], [xr], lambda e, tm=tm, xr=xr, nh=nh: e.tensor_tensor(out=xr[:, nh * 512:(nh + 1) * 512], in0=xr[:, nh * 512:(nh + 1) * 512], in1=tm[:], op=ALU.add))
                    K.dma(POOL, x2s[tile * 128:(tile + 1) * 128, :], xr[:], [xr], [dres("x2s%d" % tile)], xr, store=True)
        K.barrier()

    if upto >= 5:
        hyena_phases()
    if upto >= 6 or upto == -6:
        p6()
        ffn_phase(1, x2s, yout, "x2s", "yout")


_CONST_CACHE = {}


def _host_consts():
    if _CONST_CACHE:
        return _CONST_CACHE
    bf = ml_dtypes.bfloat16
    c = {}
    s = (np.arange(32)[None, :] * 128 + np.arange(128)[:, None]).astype(np.int64)
    inv = 10000.0 ** (-np.arange(16, dtype=np.float32) / 16.0)
    row = (s // 64).astype(np.float32)[..., None] * inv
    col = (s % 64).astype(np.float32)[..., None] * inv
    cr, sr, cc, sc = np.cos(row), np.sin(row), np.cos(col), np.sin(col)
    c["rope_c"] = np.concatenate([cr, cr, cc, cc], -1).astype(np.float32)
    c["rope_s"] = np.concatenate([-sr, sr, -sc, sc], -1).astype(np.float32)
    for n, tag in ((NLAT, "l"), (NCTX, "c")):
        N = 2 * n
        h = n // 2
        nmt = h // 128
        m = np.arange(h, dtype=np.int64)
        k = np.arange(h, dtype=np.int64)
        th = 2.0 * np.pi / N
        ae = th * ((2 * m[:, None] * k[None, :]) % N).astype(np.float64)
        ao = th * (((2 * m[:, None] + 1) * k[None, :]) % N).astype(np.float64)
        lay = lambda M: np.ascontiguousarray(M.reshape(nmt, 128, nmt, 128).transpose(2, 1, 0, 3)).astype(bf)
        c["ce_" + tag] = lay(np.cos(ae))
        c["se_" + tag] = lay(np.sin(ae))
        c["co_" + tag] = lay(np.cos(ao))
        c["so_" + tag] = lay(np.sin(ao))
        c["cot_" + tag] = lay(np.cos(ao).T)
        c["sot_" + tag] = lay(np.sin(ao).T)
        t = (np.arange(n, dtype=np.float32) / np.float32(max(n - 1, 1))).astype(np.float32)
        bands = np.arange(1, 17, dtype=np.float32)
        a = (np.float32(2.0 * math.pi) * t[:, None] * bands[None, :]).astype(np.float32)
        z = np.concatenate([t[:, None], np.cos(a), np.sin(a)], -1).astype(np.float32)
        c["zemb_" + tag] = np.ascontiguousarray(z.T)
        c["tcol_" + tag] = np.ascontiguousarray(t.reshape(nmt, 128, 2).transpose(1, 2, 0))
        wA = np.full((128, nmt + 1), 2.0 / N, np.float32)
        wB = np.full((128, nmt + 1), 2.0 / N, np.float32)
        wA[0, 0] = 1.0 / N
        wB[0, 0] = 1.0 / N
        wB[:, nmt] = 0.0
        c["wk_" + tag] = np.ascontiguousarray(np.stack([wA, wB], 1))
    c["nyq"] = np.tile(((-1.0) ** np.arange(128))[None, :], (128, 1)).astype(bf)
    c["nyqp"] = np.tile(((-1.0) ** np.arange(128))[:, None], (1, 2)).astype(bf)
    _CONST_CACHE.update(c)
    return c


def _core_inputs(core, inp, consts):
    f = lambda a: np.ascontiguousarray(np.asarray(a, dtype=np.float32))
    m = {}
    m["xin"] = np.concatenate([f(inp["x_sample"][core]), f(inp["x_prompt"][2 * core]), f(inp["x_prompt"][2 * core + 1])], 0)
    m["cvec"] = np.stack([f(inp["c"][core]), f(inp["c_ctx"])], 0)
    m["cache_k"] = f(inp["cache_k"][core, 0]).reshape(PAST, NKV * HD)
    m["cache_v"] = f(inp["cache_v"][core, 0]).reshape(PAST, NKV * HD)
    m["w_mod"] = f(inp["w_mod"][0])
    m["b_mod"] = f(inp["b_mod"][0])
    m["norms"] = np.stack([f(inp["norm_ffn1"][0]), f(inp["norm_mix"][0]), f(inp["norm_ffn2"][0])], 0)
    for k in ("ffn1_wi", "ffn1_wo", "ffn2_wi", "ffn2_wo", "w_in", "attn_sink", "conv_w", "conv_b", "filt_w1", "filt_b1",
              "filt_w2", "filt_b2", "filt_w3", "filt_b3", "filt_freq"):
        m[k] = f(inp[k][0])
    m["qk_norm"] = np.stack([f(inp["q_norm"][0]), f(inp["k_norm"][0])], 0)
    m["filt_decay"] = f(inp["filt_decay"][0]).reshape(4 * D)
    m["hyena_skip"] = f(inp["hyena_skip"][0]).reshape(2 * D)
    m["w_ab"] = f(inp["w_attn_branch"][0])
    m["w_hb"] = f(inp["w_hyena_branch"][0])
    m["w_out"] = f(inp["w_out"][0])
    m.update(consts)
    return m


_NC_CACHE = {}


def kernel(**inputs):
    consts = _host_consts()
    if "nc" not in _NC_CACHE:
        _NC_CACHE["nc"] = build_program()
    nc = _NC_CACHE["nc"]
    in_maps = [_core_inputs(c, inputs, consts) for c in range(8)]
    res = run_bass_kernel_spmd(nc, in_maps, core_ids=list(range(8)))
    y_prompt = np.zeros((16, NCTX, D), np.float32)
    y_sample = np.zeros((8, NLAT, D), np.float32)
    new_k = np.zeros((16, 1, NCTX, NKV, HD), np.float32)
    new_v = np.zeros((16, 1, NCTX, NKV, HD), np.float32)
    for c in range(8):
        r = res.results[c]
        y = np.asarray(r["yout"])
        y_sample[c] = y[:NLAT]
        y_prompt[2 * c] = y[NLAT:NLAT + NCTX]
        y_prompt[2 * c + 1] = y[NLAT + NCTX:]
        nk = np.asarray(r["newk"]).reshape(2, NCTX, NKV, HD)
        nv = np.asarray(r["newv"]).reshape(2, NCTX, NKV, HD)
        new_k[2 * c:2 * c + 2, 0] = nk
        new_v[2 * c:2 * c + 2, 0] = nv
    return (y_prompt, y_sample, new_k, new_v)
```

```python
import math
from contextlib import ExitStack
import numpy as np
import ml_dtypes
import concourse.bass as bass
import concourse.mybir as mybir
from concourse.bass_utils import run_bass_kernel_spmd

F32 = mybir.dt.float32
BF16 = mybir.dt.bfloat16
AF = mybir.ActivationFunctionType
ALU = mybir.AluOpType
AX = mybir.AxisListType

D = 1024
NLAT = 4096
NCTX = 256
NTOK = NLAT + 2 * NCTX
NT = NTOK // 128
DFF = 2816
NH, NKV, HD = 16, 4, 64
PAST = 512
EPS = 1e-6
INCOLS = 6656
FW = 64
FE = 33
PI = math.pi


class Res:
    def __init__(self, name):
        self.name = name
        self.w = {}
        self.r = {}
        self.dsem = None
        self.ssem = None


class Tile(Res):
    def __init__(self, name, t):
        super().__init__(name)
        self.t = t

    def __getitem__(self, k):
        return self.t[k]


def _merge(d, s):
    for k, (sem, v) in s.items():
        if k not in d or d[k][1] < v:
            d[k] = (sem, v)


class Eng:
    def __init__(self, K, name, e):
        self.K = K
        self.name = name
        self.e = e
        self.sid, self.sem = K.new_sem(name)
        self.cnt = 0
        self.waited = {}

    def wait_for(self, deps, skip=None, include_self=False):
        for sid, (sem, val) in deps.items():
            if (sid == self.sid and not include_self) or sid == skip:
                continue
            if self.waited.get(sid, 0) >= val:
                continue
            self.e.wait_ge(sem, val)
            self.waited[sid] = val


class KB:
    def __init__(self, nc, es):
        self.nc = nc
        self.es = es
        self.sems = []
        self.free_sems = []
        self.PE = Eng(self, "pe", nc.tensor)
        self.ACT = Eng(self, "act", nc.scalar)
        self.DVE = Eng(self, "dve", nc.vector)
        self.POOL = Eng(self, "pool", nc.gpsimd)
        self.SP = Eng(self, "sp", nc.sync)
        self.engs = [self.PE, self.ACT, self.DVE, self.POOL, self.SP]
        self.dma_res = []
        self.uid = 0
        self.psum_t = es.enter_context(nc.psum_tensor("psum_all", [128, 4096], F32))
        self.banks = [Tile("bank%d" % i, self.psum_t) for i in range(8)]
        self.bank_set = list(range(8))
        self.bank_rr = 0

    def new_sem(self, name):
        s = self.es.enter_context(self.nc.semaphore("s_%s_%d" % (name, len(self.sems))))
        self.sems.append([s, 0])
        return len(self.sems) - 1, s

    def _alloc_dsem(self):
        if self.free_sems:
            return self.free_sems.pop()
        return self.new_sem("d")[0]

    def sb(self, name, shape, dtype, stack=None):
        self.uid += 1
        t = (stack or self.es).enter_context(self.nc.sbuf_tensor("%s_%d" % (name, self.uid), shape, dtype))
        return Tile(name, t)

    def set_banks(self, lst):
        self.bank_set = list(lst)
        self.bank_rr = 0

    def bank(self):
        b = self.bank_set[self.bank_rr % len(self.bank_set)]
        self.bank_rr += 1
        return b

    def bap(self, b, dtype=F32, nb=1):
        ap = self.psum_t[:, b * 512:(b + nb) * 512]
        if dtype == BF16:
            ap = ap.bitcast(BF16)
        return ap

    def group(self, E, reads, writes, fns):
        raw = {}
        for r in reads:
            _merge(raw, r.w)
        E.wait_for(raw, include_self=(E is not self.PE))
        deps = {}
        for r in writes:
            _merge(deps, r.w)
            _merge(deps, r.r)
        E.wait_for(deps)
        ins = None
        for fn in fns:
            ins = fn(E.e)
        E.cnt += 1
        ins.then_inc(E.sem, 1)
        ev = (E.sem, E.cnt)
        for r in reads:
            _merge(r.r, {E.sid: ev})
        for r in writes:
            r.w = {E.sid: ev}
            r.r = {}
        return ins

    def op(self, E, reads, writes, fn):
        return self.group(E, reads, writes, [fn])

    def dma(self, Q, out, in_, reads, writes, owner, store=False, **kw):
        if store:
            if owner.ssem is None:
                owner.ssem = self._alloc_dsem()
                self.dma_res.append(owner)
            sid = owner.ssem
        else:
            if owner.dsem is None:
                owner.dsem = self._alloc_dsem()
                self.dma_res.append(owner)
            sid = owner.dsem
        sem = self.sems[sid][0]
        raw = {}
        for r in reads:
            _merge(raw, r.w)
        Q.wait_for(raw, skip=sid, include_self=True)
        deps = {}
        for r in writes:
            _merge(deps, r.w)
            _merge(deps, r.r)
        Q.wait_for(deps, skip=sid)
        self.sems[sid][1] += 16
        ev = (sem, self.sems[sid][1])
        Q.e.dma_start(out=out, in_=in_, **kw).then_inc(sem, 16)
        for r in reads:
            _merge(r.r, {sid: ev})
        for r in writes:
            r.w = {sid: ev}
            r.r = {}

    def mm(self, bank, pairs, m, n, reads, off=0, first=True, last=True, nb=1, extra_writes=()):
        out = self.bap(bank, F32, nb)[0:m, off:off + n]
        nl = len(pairs) - 1
        fns = [(lambda e, l=l, r=r, i=i: e.matmul(out, lhsT=l, rhs=r, start=(first and i == 0), stop=(last and i == nl)))
               for i, (l, r) in enumerate(pairs)]
        self.group(self.PE, reads, [self.banks[bank + j] for j in range(nb)] + list(extra_writes), fns)

    def barrier(self):
        deps = {}
        for E in self.engs:
            if E.cnt > 0:
                deps[E.sid] = (E.sem, E.cnt)
        for res in self.dma_res:
            for sid in (res.dsem, res.ssem):
                if sid is not None:
                    deps[sid] = (self.sems[sid][0], self.sems[sid][1])
        self.DVE.wait_for(deps)
        self.DVE.cnt += 1
        self.nc.vector.memset(self.bar_t[:], 0.0).then_inc(self.DVE.sem, 1)
        ev = {self.DVE.sid: (self.DVE.sem, self.DVE.cnt)}
        for E in self.engs:
            E.wait_for(ev)
        for res in self.dma_res:
            for sid in (res.dsem, res.ssem):
                if sid is not None:
                    self.free_sems.append(sid)
            res.dsem = None
            res.ssem = None
        self.dma_res = []
        self.free_sems = sorted(set(self.free_sems))


def build_program(upto=99, debug=False):
    nc = bass.Bass("TRN2", target_bir_lowering=False)
    es = ExitStack()
    with es:
        K = KB(nc, es)
        K.bar_t = K.sb("bar", [1, 8], F32)
        _emit(nc, K, upto, debug)
        K.barrier()
    return nc


def _dram(nc, name, shape, dtype, kind):
    return nc.dram_tensor(name, list(shape), dtype, kind=kind).ap()


def _emit(nc, K, upto, debug):
    PE, ACT, DVE, POOL, SP = K.PE, K.ACT, K.DVE, K.POOL, K.SP
    ext = lambda name, shape, dt=F32: _dram(nc, name, shape, dt, "ExternalInput")
    outk = "ExternalOutput"
    scr = "ExternalOutput" if debug else "Internal"
    xin = ext("xin", [NTOK, D])
    cvec = ext("cvec", [2, D])
    cache_k = ext("cache_k", [PAST, NKV * HD])
    cache_v = ext("cache_v", [PAST, NKV * HD])
    w_mod = ext("w_mod", [D, 9 * D])
    b_mod = ext("b_mod", [9 * D])
    norms = ext("norms", [3, D])
    ffn_wi = [ext("ffn1_wi", [D, 2 * DFF]), ext("ffn2_wi", [D, 2 * DFF])]
    ffn_wo = [ext("ffn1_wo", [DFF, D]), ext("ffn2_wo", [DFF, D])]
    w_in = ext("w_in", [D, INCOLS])
    qk_norm = ext("qk_norm", [2, HD])
    attn_sink = ext("attn_sink", [NH])
    conv_w = ext("conv_w", [3, 3 * D])
    conv_b = ext("conv_b", [3 * D])
    filt_w1 = ext("filt_w1", [FE, FW])
    filt_b1 = ext("filt_b1", [FW])
    filt_w2 = ext("filt_w2", [FW, FW])
    filt_b2 = ext("filt_b2", [FW])
    filt_w3 = ext("filt_w3", [FW, 4 * D])
    filt_b3 = ext("filt_b3", [4 * D])
    filt_freq = ext("filt_freq", [FW])
    filt_decay = ext("filt_decay", [4 * D])
    hyena_skip = ext("hyena_skip", [2 * D])
    w_ab = ext("w_ab", [D, D])
    w_hb = ext("w_hb", [D, D])
    w_out = ext("w_out", [D, D])
    rope_c = ext("rope_c", [128, 32, 64])
    rope_s = ext("rope_s", [128, 32, 64])
    mats = {}
    for n_, tag in ((NLAT, "l"), (NCTX, "c")):
        nm_ = n_ // 256
        mats[n_] = {nm: ext("%s_%s" % (nm, tag), [nm_, 128, nm_, 128], BF16) for nm in ("ce", "se", "co", "so", "cot", "sot")}
    zemb = {NLAT: ext("zemb_l", [FE, NLAT]), NCTX: ext("zemb_c", [FE, NCTX])}
    tcol = {NLAT: ext("tcol_l", [128, 2, 16]), NCTX: ext("tcol_c", [128, 2, 1])}
    wk = {NLAT: ext("wk_l", [128, 2, 17]), NCTX: ext("wk_c", [128, 2, 2])}
    nyq = ext("nyq", [128, 128], BF16)
    nyqp = ext("nyqp", [128, 2], BF16)
    yout = _dram(nc, "yout", [NTOK, D], F32, outk)
    newk = _dram(nc, "newk", [2 * NCTX, NKV * HD], F32, outk)
    newv = _dram(nc, "newv", [2 * NCTX, NKV * HD], F32, outk)
    modd = _dram(nc, "modd", [2, 9 * D], F32, scr)
    x1s = _dram(nc, "x1s", [NTOK, D], F32, scr)
    x2s = _dram(nc, "x2s", [NTOK, D], F32, scr)
    hyT = _dram(nc, "hyT", [3 * D, NTOK], BF16, scr)
    qkvs = _dram(nc, "qkvs", [NTOK, 1536], F32, scr)
    sgT = _dram(nc, "sgT", [2 * D, NTOK], BF16, scr)
    attT = _dram(nc, "attT", [HD, NH, NTOK], BF16, scr)
    hyoT = _dram(nc, "hyoT", [D, NTOK], BF16, scr)
    kfs = {NLAT: _dram(nc, "kfs_l", [2, 17, 4, 128, D], BF16, scr), NCTX: _dram(nc, "kfs_c", [2, 2, 4, 128, D], BF16, scr)}
    R = {}

    def dres(name):
        if name not in R:
            R[name] = Res(name)
        return R[name]

    ident_f = K.sb("identf", [128, 128], F32)
    ident = K.sb("ident", [128, 128], BF16)
    K.op(POOL, [], [ident_f], lambda e: e.memset(ident_f[:], 0.0))
    K.op(POOL, [ident_f], [ident_f], lambda e: e.affine_select(out=ident_f[:], in_=ident_f[:], pattern=[[-1, 128]],
                                                        compare_op=ALU.not_equal, fill=1.0, base=0, channel_multiplier=1))
    K.op(POOL, [ident_f], [ident], lambda e: e.tensor_copy(out=ident[:], in_=ident_f[:]))
    mhalf = K.sb("mhalf", [128, 32], F32)
    K.op(POOL, [], [mhalf], lambda e: e.memset(mhalf[:], -0.5))
    ones_f = K.sb("onesf", [128, 128], F32)
    K.op(POOL, [], [ones_f], lambda e: e.memset(ones_f[:], 1.0))

    def group_of(tile):
        return 0 if tile < 32 else 1

    def load_w_bf16(stack, name, src, kchunks, ncols, col0=0, part=128):
        t = K.sb(name, [part, kchunks, ncols], BF16, stack)
        for k in range(kchunks):
            K.dma(POOL, t[:, k, :], src[k * part:(k + 1) * part, col0:col0 + ncols], [], [t], t)
        return t

    ps_w1 = ExitStack()
    pre_w1 = (load_w_bf16(ps_w1, "wgu", ffn_wi[0], 8, 2 * DFF), load_w_bf16(ps_w1, "wd", ffn_wo[0], 22, D))
    with ExitStack() as ps:
        cT = K.sb("cT", [128, 8, 2], F32, ps)
        with nc.allow_non_contiguous_dma(reason="tiny"):
            for g in range(2):
                K.dma(SP, cT[:, :, g], cvec[g].rearrange("(k p) -> p k", p=128), [], [cT], cT)
        sT = K.sb("sT", [128, 8, 2], F32, ps)
        K.op(ACT, [cT], [sT], lambda e: e.activation(out=sT[:], in_=cT[:], func=AF.Silu))
        nm = K.sb("nm", [2, 3 * D], F32, ps)
        K.dma(SP, nm[:], norms.rearrange("a d -> (a d)").partition_broadcast(2), [], [nm], nm)
        wmb = [K.sb("wm%d" % i, [128, 8, 512], F32, ps) for i in range(2)]
        bms = [K.sb("bm%d" % i, [2, 512], F32, ps) for i in range(2)]
        mos = [K.sb("mo%d" % i, [2, 512], F32, ps) for i in range(2)]
        for j in range(18):
            wt, bm, mo = wmb[j % 2], bms[j % 2], mos[j % 2]
            slot, half = j // 2, j % 2
            K.dma(SP, wt[:], w_mod[:, j * 512:(j + 1) * 512].rearrange("(k p) n -> p k n", p=128), [], [wt], wt)
            K.dma(SP, bm[:], b_mod[j * 512:(j + 1) * 512].partition_broadcast(2), [], [bm], bm)
            b = K.bank()
            K.mm(b, [(sT[:, k, :], wt[:, k, :]) for k in range(8)], 2, 512, [sT, wt])
            K.op(DVE, [K.banks[b], bm], [mo], lambda e, b=b, bm=bm, mo=mo: e.tensor_tensor(out=mo[:], in0=K.bap(b)[0:2, :], in1=bm[:], op=ALU.add))
            if slot % 3 == 1:
                i3 = slot // 3
                K.op(DVE, [mo, nm], [mo], lambda e, mo=mo, i3=i3, half=half: e.scalar_tensor_tensor(
                    out=mo[:], in0=mo[:], scalar=1.0, in1=nm[:, i3 * D + half * 512:i3 * D + (half + 1) * 512], op0=ALU.add, op1=ALU.mult))
            if slot in (2, 8):
                K.op(DVE, [mo], [mo], lambda e, mo=mo: e.tensor_scalar(out=mo[:], in0=mo[:], scalar1=0.5, scalar2=None, op0=ALU.mult))
            K.dma(ACT, modd[:, j * 512:(j + 1) * 512], mo[:], [mo], [dres("modd")], mo, store=True)
    K.barrier()
    if upto < 1:
        ps_w1.close()
        return

    def load_consts(tiles, i, g, which=(0, 1, 2)):
        for t, j in zip(tiles, which):
            K.dma(SP, t[:], modd[g, (3 * i + j) * D:(3 * i + j + 1) * D].partition_broadcast(128), [dres("modd")], [t], t)

    def load_w_bf16(stack, name, src, kchunks, ncols, col0=0, part=128):
        t = K.sb(name, [part, kchunks, ncols], BF16, stack)
        for k in range(kchunks):
            K.dma(POOL, t[:, k, :], src[k * part:(k + 1) * part, col0:col0 + ncols], [], [t], t)
        return t

    def norm_mod_T(bufs, xt, A, B, hT, col0, part=0):
        sq, ss, hb = bufs
        if part != 2:
            norm_mod_a(bufs, xt, A, B)
        if part != 1:
            norm_mod_b(bufs, hT, col0)

    def norm_mod_a(bufs, xt, A, B):
        sq, ss, hb = bufs
        K.op(ACT, [xt], [sq, ss], lambda e: e.activation(out=sq[:], in_=xt[:], func=AF.Square, accum_out=ss[:, 0:1]))
        K.op(DVE, [ss], [ss], lambda e: e.tensor_scalar(out=ss[:, 1:2], in0=ss[:, 0:1], scalar1=1.0 / D, scalar2=EPS,
                                                        op0=ALU.mult, op1=ALU.add))
        K.op(POOL, [ss, mhalf], [ss], lambda e: e.tensor_tensor(out=ss[:, 2:3], in0=ss[:, 1:2], in1=mhalf[:, 0:1], op=ALU.pow))
        K.op(DVE, [xt, ss, A], [sq], lambda e: e.scalar_tensor_tensor(out=sq[:], in0=xt[:], scalar=ss[:, 2:3], in1=A[:],
                                                                      op0=ALU.mult, op1=ALU.mult))
        K.op(POOL, [sq, B], [hb], lambda e: e.tensor_tensor(out=hb[:], in0=sq[:], in1=B[:], op=ALU.add))

    def norm_mod_b(bufs, hT, col0):
        sq, ss, hb = bufs
        b = K.bank()
        K.group(PE, [hb, ident], [K.banks[b]],
                [(lambda e, k=k: e.transpose(out=K.bap(b, BF16)[:, k * 128:(k + 1) * 128], in_=hb[:, k * 128:(k + 1) * 128],
                                             identity=ident[:])) for k in range(8)])
        K.op(ACT, [K.banks[b]], [hT],
             lambda e: e.activation(out=hT[:, :, col0:col0 + 128], in_=K.bap(b, BF16).rearrange("p (k t) -> p k t", k=8),
                                    func=AF.Copy))

    def ffn_phase(idx, src, dst, src_name, dst_name, pre=None):
        TC = 256
        NTC = TC // 128
        with ExitStack() as ps:
            cB, cA, cG = [K.sb("mc%d" % i, [128, D], F32, ps) for i in range(3)]
            if pre is not None:
                wgu, wd = pre
            else:
                wgu = load_w_bf16(ps, "wgu", ffn_wi[idx], 8, 2 * DFF)
                wd = load_w_bf16(ps, "wd", ffn_wo[idx], 22, D)
            xts = [K.sb("xt%d" % i, [128, D], F32, ps) for i in range(2)]
            xrs = [K.sb("xr%d" % i, [128, D], F32, ps) for i in range(2)]
            hTs = [K.sb("hT%d" % i, [128, 8, TC], BF16, ps) for i in range(2)]
            sq = K.sb("sq", [128, D], F32, ps)
            ss = K.sb("ss", [128, 4], F32, ps)
            hb = K.sb("hb", [128, D], BF16, ps)
            actT = K.sb("actT", [128, 22, TC], BF16, ps)
            sgs = [K.sb("sg%d" % i, [128, TC], F32, ps) for i in range(2)]
            tmp = [K.sb("tmp%d" % i, [128, 512], F32, ps) for i in range(2)]
            cnt = [0]

            hbs = [hb] + [K.sb("hbx%d" % i, [128, D], BF16, ps) for i in range(NTC - 1)]

            def prep_a(c):
                for t4 in range(NTC):
                    tile = c * NTC + t4
                    xt = xts[cnt[0] % 2]
                    cnt[0] += 1
                    K.dma(SP, xt[:], src[tile * 128:(tile + 1) * 128, :], [dres("%s%d" % (src_name, tile))], [xt], xt)
                    norm_mod_a((sq, ss, hbs[t4]), xt, cA, cB)

            def prep_b(c):
                hT = hTs[c % 2]
                for t4 in range(NTC):
                    norm_mod_b((sq, ss, hbs[t4]), hT, t4 * 128)

            def prep(c):
                prep_a(c)
                prep_b(c)

            def up(c, j0, j1):
                hT = hTs[c % 2]
                for j in range(j0, j1):
                    bg = K.bank()
                    K.mm(bg, [(wgu[:, k, j * 128:(j + 1) * 128], hT[:, k, :]) for k in range(8)], 128, TC, [wgu, hT])
                    bu = K.bank()
                    K.mm(bu, [(wgu[:, k, DFF + j * 128:DFF + (j + 1) * 128], hT[:, k, :]) for k in range(8)], 128, TC, [wgu, hT])
                    sg = sgs[j % 2]
                    K.op(ACT, [K.banks[bg]], [sg], lambda e, bg=bg, sg=sg: e.activation(out=sg[:], in_=K.bap(bg)[:, 0:TC], func=AF.Silu))
                    K.op(DVE, [K.banks[bu], sg], [actT],
                         lambda e, bu=bu, sg=sg, j=j: e.tensor_tensor(out=actT[:, j, :], in0=K.bap(bu)[:, 0:TC], in1=sg[:], op=ALU.mult))

            def down(c):
                for t4 in range(NTC):
                    tile = c * NTC + t4
                    xr = xrs[tile % 2]
                    K.dma(SP, xr[:], src[tile * 128:(tile + 1) * 128, :], [dres("%s%d" % (src_name, tile))], [xr], xr)
                    for nh in range(2):
                        by = K.bank()
                        K.mm(by, [(actT[:, j, t4 * 128:(t4 + 1) * 128], wd[:, j, nh * 512:(nh + 1) * 512]) for j in range(22)],
                             128, 512, [actT, wd])
                        tm = tmp[nh]
                        K.op(DVE, [K.banks[by], cG], [tm],
                             lambda e, by=by, tm=tm, nh=nh: e.tensor_tensor(out=tm[:], in0=K.bap(by), in1=cG[:, nh * 512:(nh + 1) * 512], op=ALU.mult))
                        K.op(POOL, [tm, xr], [xr],
                             lambda e, tm=tm, xr=xr, nh=nh: e.tensor_tensor(out=xr[:, nh * 512:(nh + 1) * 512], in0=xr[:, nh * 512:(nh + 1) * 512], in1=tm[:], op=ALU.add))
                    K.dma(POOL, dst[tile * 128:(tile + 1) * 128, :], xr[:], [xr], [dres("%s%d" % (dst_name, tile))], xr, store=True)

            for g, (c0, c1) in enumerate(((0, 32 // NTC), (32 // NTC, NT // NTC))):
                load_consts((cB, cA, cG), 0 if idx == 0 else 2, g)
                prep(c0)
                for c in range(c0, c1):
                    up(c, 0, 11)
                    if c + 1 < c1:
                        prep_a(c + 1)
                    up(c, 11, 22)
                    if c + 1 < c1:
                        prep_b(c + 1)
                    down(c)
        K.barrier()

    ffn_phase(0, xin, x1s, "xin", "x1s", pre=pre_w1)
    ps_w1.close()
    if upto < 2:
        return

    def p2():
        TC = 512
        with ExitStack() as ps:
            cB, cA = [K.sb("mc%d" % i, [128, D], F32, ps) for i in range(2)]
            wz = load_w_bf16(ps, "wz", w_in, 8, 5120, col0=1536)
            wq2 = load_w_bf16(ps, "wq2", w_in, 8, 1536, col0=0)
            qsts = [K.sb("qst%d" % i, [128, 1536], F32, ps) for i in range(2)]
            xts = [K.sb("xt%d" % i, [128, D], F32, ps) for i in range(2)]
            hTs = [K.sb("hT%d" % i, [128, 8, TC], BF16, ps) for i in range(2)]
            sq = K.sb("sq", [128, D], F32, ps)
            ss = K.sb("ss", [128, 4], F32, ps)
            hb = K.sb("hb", [128, D], BF16, ps)
            stg = [K.sb("stg%d" % i, [128, 8, TC], BF16, ps) for i in range(2)]
            cnt = [0, 0]

            hbs = [hb] + [K.sb("hbx%d" % i, [128, D], BF16, ps) for i in range(3)]

            def prep_a(c):
                for t4 in range(4):
                    tile = c * 4 + t4
                    xt = xts[cnt[0] % 2]
                    cnt[0] += 1
                    K.dma(SP, xt[:], x1s[tile * 128:(tile + 1) * 128, :], [dres("x1s%d" % tile)], [xt], xt)
                    norm_mod_a((sq, ss, hbs[t4]), xt, cA, cB)

            def prep_b(c):
                hT = hTs[c % 2]
                for t4 in range(4):
                    norm_mod_b((sq, ss, hbs[t4]), hT, t4 * 128)

            def prep(c):
                prep_a(c)
                prep_b(c)

            def body(c, m0, m1):
                hT = hTs[c % 2]
                for mg in range(m0, m1, 8):
                    st = stg[cnt[1] % 2]
                    cnt[1] += 1
                    for mi in range(8):
                        m = mg + mi
                        b = K.bank()
                        K.mm(b, [(wz[:, k, m * 128:(m + 1) * 128], hT[:, k, :]) for k in range(8)], 128, TC, [wz, hT])
                        if m < 24:
                            K.op(DVE, [K.banks[b]], [st], lambda e, b=b, st=st, mi=mi: e.tensor_copy(out=st[:, mi, :], in_=K.bap(b)))
                        else:
                            K.op(ACT, [K.banks[b]], [st], lambda e, b=b, st=st, mi=mi: e.activation(out=st[:, mi, :], in_=K.bap(b), func=AF.Sigmoid))
                    if mg < 24:
                        dst = hyT[mg * 128:(mg + 8) * 128, c * TC:(c + 1) * TC].rearrange("(m p) t -> p m t", p=128)
                        dr = dres("hyT%d" % c)
                    else:
                        dst = sgT[(mg - 24) * 128:(mg - 16) * 128, c * TC:(c + 1) * TC].rearrange("(m p) t -> p m t", p=128)
                        dr = dres("sgT%d" % c)
                    K.dma(POOL, dst, st[:], [st], [dr], st, store=True)

            nch = NT // 4
            load_consts((cB, cA), 1, 0, which=(0, 1))
            prep(0)
            for c in range(nch):
                body(c, 0, 16)
                if c + 1 < nch:
                    if c + 1 == 8:
                        load_consts((cB, cA), 1, 1, which=(0, 1))
                    prep_a(c + 1)
                body(c, 16, 40)
                hT_ = hTs[c % 2]
                for t4 in range(4):
                    tile = c * 4 + t4
                    qst = qsts[tile % 2]
                    for n in range(3):
                        b = K.bank()
                        K.mm(b, [(hT_[:, k, t4 * 128:(t4 + 1) * 128], wq2[:, k, n * 512:(n + 1) * 512]) for k in range(8)], 128, 512, [hT_, wq2])
                        if n == 1:
                            K.op(DVE, [K.banks[b]], [qst], lambda e, b=b, qst=qst, n=n: e.tensor_copy(out=qst[:, n * 512:(n + 1) * 512], in_=K.bap(b)))
                        else:
                            K.op(ACT, [K.banks[b]], [qst], lambda e, b=b, qst=qst, n=n: e.activation(out=qst[:, n * 512:(n + 1) * 512], in_=K.bap(b), func=AF.Copy))
                    K.dma(POOL, qkvs[tile * 128:(tile + 1) * 128, :], qst[:], [qst], [dres("qkvs%d" % tile)], qst, store=True)
                if c + 1 < nch:
                    prep_b(c + 1)
        K.barrier()

    p2()
    if upto < 3:
        return

    def p3():
        with ExitStack() as ps:
            cB, cA = [K.sb("mc%d" % i, [128, D], F32, ps) for i in range(2)]
            rc = K.sb("rc", [128, 32, 64], F32, ps)
            rs = K.sb("rs", [128, 32, 64], F32, ps)
            K.dma(SP, rc[:], rope_c, [], [rc], rc)
            K.dma(SP, rs[:], rope_s, [], [rs], rs)
            gq = K.sb("gq", [128, HD], F32, ps)
            gk = K.sb("gk", [128, HD], F32, ps)
            K.dma(SP, gq[:], qk_norm[0].partition_broadcast(128), [], [gq], gq)
            K.dma(SP, gk[:], qk_norm[1].partition_broadcast(128), [], [gk], gk)
            xts = [K.sb("xt%d" % i, [128, D], F32, ps) for i in range(2)]
            sq = K.sb("sq", [128, D], F32, ps)
            ss = K.sb("ss", [128, 4], F32, ps)
            hb = K.sb("hb", [128, D], BF16, ps)
            hT1 = [K.sb("hT1%d" % i, [128, 8, 128], BF16, ps) for i in range(2)]
            sqq = K.sb("sqq", [128, 1280], F32, ps)
            s1s = [K.sb("s1%d" % i, [128, 8], F32, ps) for i in range(3)]
            s2s = [K.sb("s2%d" % i, [128, 8], F32, ps) for i in range(3)]
            s3s = [K.sb("s3%d" % i, [128, 8], F32, ps) for i in range(3)]
            qf = K.sb("qf", [128, 1280], F32, ps)
            ta = K.sb("ta", [128, 1280], F32, ps)
            tb = K.sb("tb", [128, 1280], F32, ps)
            qb = K.sb("qb", [128, 1280], BF16, ps)
            vf = K.sb("vf", [128, 256], F32, ps)
            QTs = [K.sb("QT%d" % i, [128, 16, 128], BF16, ps) for i in range(3)]
            KT = K.sb("KT", [128, 4, NLAT], BF16, ps)
            V = K.sb("V", [128, 32, 4, 96], BF16, ps)
            cKT = K.sb("cKT", [128, 4, PAST], BF16, ps)
            cV = K.sb("cV", [128, 4, 4, 96], BF16, ps)
            ckb = K.sb("ckb", [128, 4, 256], BF16, ps)
            Pt = [K.sb("P%d" % i, [128, 512], BF16, ps) for i in range(4)]
            Osb = [K.sb("O%d" % i, [65, 512], F32, ps) for i in range(2)]
            rdens = [K.sb("rden%d" % i, [65, 512], F32, ps) for i in range(2)]
            rdbs = [K.sb("rdb%d" % i, [128, 512], BF16, ps) for i in range(2)]
            mones = K.sb("mones", [65, 512], F32, ps)
            K.op(POOL, [], [mones], lambda e: e.memset(mones[:], -1.0))
            ones_b = K.sb("ones_b", [128, 64], BF16, ps)
            K.op(POOL, [], [ones_b], lambda e: e.memset(ones_b[:], 1.0))
            aos = [K.sb("ao%d" % i, [64, 16, 128], BF16, ps) for i in range(2)]
            mprev = K.sb("mprev", [128, 128], BF16, ps)
            mnext = K.sb("mnext", [128, 128], BF16, ps)
            sinkexp = K.sb("sinkexp", [65, 16, 128], F32, ps)
            sk = K.sb("sk", [65, 16], F32, ps)
            K.op(POOL, [], [mprev], lambda e: e.memset(mprev[:], 1.0))
            K.op(POOL, [mprev], [mprev], lambda e: e.affine_select(out=mprev[:], in_=mprev[:], pattern=[[-1, 128]], compare_op=ALU.is_ge,
                                                              fill=0.0, base=0, channel_multiplier=1))
            K.op(POOL, [], [mnext], lambda e: e.memset(mnext[:], 1.0))
            K.op(POOL, [mnext], [mnext], lambda e: e.affine_select(out=mnext[:], in_=mnext[:], pattern=[[1, 128]], compare_op=ALU.is_ge,
                                                              fill=0.0, base=0, channel_multiplier=-1))
            mb_prev = K.sb("mb_prev", [128, 4, 128], BF16, ps)
            mb_next = K.sb("mb_next", [128, 4, 128], BF16, ps)
            for mb_, mk_ in ((mb_prev, mprev), (mb_next, mnext)):
                K.op(POOL, [mk_], [mb_], lambda e, mb_=mb_, mk_=mk_: e.tensor_scalar(out=mb_[:], in0=mk_[:].unsqueeze(1).to_broadcast([128, 4, 128]),
                                                                                  scalar1=-1.0, scalar2=30000.0, op0=ALU.add, op1=ALU.mult))
            K.op(POOL, [], [V], lambda e: e.memset(V[:], 0.0))
            K.op(POOL, [], [V], lambda e: e.memset(V[:, :, :, 64:65], 1.0))
            K.op(POOL, [], [cV], lambda e: e.memset(cV[:], 0.0))
            K.op(POOL, [], [cV], lambda e: e.memset(cV[:, :, :, 64:65], 1.0))
            K.op(POOL, [], [KT], lambda e: e.memset(KT[64:128], 0.0))
            K.op(POOL, [], [cKT], lambda e: e.memset(cKT[64:128], 0.0))
            for qt_ in QTs:
                K.op(POOL, [], [qt_], lambda e, qt_=qt_: e.memset(qt_[64:128], 0.0))
            for rb_ in rdbs:
                K.op(POOL, [], [rb_], lambda e, rb_=rb_: e.memset(rb_[:], 0.0))
            K.dma(SP, sk[64:65, :], attn_sink.rearrange("(o h) -> o h", o=1), [], [sk], sk)
            K.op(ACT, [sk], [sk], lambda e: e.activation(out=sk[64:65, :], in_=sk[64:65, :], func=AF.Exp))
            K.op(POOL, [sk], [sinkexp], lambda e: e.tensor_copy(out=sinkexp[64:65, :, :], in_=sk[64:65, :].unsqueeze(2).to_broadcast([1, 16, 128])))
            sel0 = K.sb("sel0", [1, 96], BF16, ps)
            K.op(POOL, [], [sel0], lambda e: e.memset(sel0[:], 0.0))
            K.op(POOL, [sel0], [sel0], lambda e: e.memset(sel0[0:1, 64:65], 1.0))
            sk0 = K.sb("sk0", [1, 16], F32, ps)
            K.dma(SP, sk0[:], attn_sink.rearrange("(o h) -> o h", o=1), [], [sk0], sk0)
            K.op(ACT, [sk0], [sk0], lambda e: e.activation(out=sk0[:], in_=sk0[:], func=AF.Exp))
            sinkb = K.sb("sinkb", [1, 16, 128], BF16, ps)
            K.op(POOL, [sk0], [sinkb], lambda e: e.tensor_copy(out=sinkb[:], in_=sk0[:].unsqueeze(2).to_broadcast([1, 16, 128])))
            K.dma(POOL, ckb[:], cache_k.rearrange("(t p) c -> p t c", p=128), [], [ckb], ckb)
            for t in range(4):
                K.dma(POOL, cV[:, t, :, 0:64], cache_v[t * 128:(t + 1) * 128, :].rearrange("p (g d) -> p g d", g=4), [], [cV], cV)
            for t in range(4):
                K.group(PE, [ckb, ident], [K.banks[2]],
                        [(lambda e, g=g, t=t: e.transpose(out=K.bap(2, BF16)[0:64, g * 128:(g + 1) * 128], in_=ckb[:, t, g * 64:(g + 1) * 64],
                                                          identity=ident[:])) for g in range(4)])
                K.op(ACT, [K.banks[2]], [cKT], lambda e, t=t: e.activation(out=cKT[0:64, :, t * 128:(t + 1) * 128],
                                                                           in_=K.bap(2, BF16)[0:64, 0:512].rearrange("p (g t) -> p g t", g=4), func=AF.Copy))
            xc = [0]

            def pre_norm(tile, part=0):
                xt = xts[tile % 2]
                h1 = hT1[tile % 2]
                if part != 2:
                    K.dma(SP, xt[:], x1s[tile * 128:(tile + 1) * 128, :], [dres("x1s%d" % tile)], [xt], xt)
                K.set_banks([3])
                norm_mod_T((sq, ss, hb), xt, cA, cB, h1, 0, part=part)

            qkts = [K.sb("qkt%d" % i, [128, 1536], F32, ps) for i in range(2)]

            def load_qkv(tile):
                qk = qkts[tile % 2]
                K.dma(SP, qk[:], qkvs[tile * 128:(tile + 1) * 128, :], [dres("qkvs%d" % tile)], [qk], qk)

            def prep(tile, qi, kslot, lat, ctx_row=None):
                qk = qkts[tile % 2]
                K.op(ACT, [qk], [sqq], lambda e: e.activation(out=sqq[:, 0:1280], in_=qk[:, 0:1280], func=AF.Square))
                K.op(ACT, [qk], [V], lambda e: e.activation(out=V[:, kslot, :, 0:64], in_=qk[:, 1280:1536].rearrange("p (g d) -> p g d", g=4), func=AF.Copy))
                yield
                segs = ((0, 8, 0), (512, 8, 1), (1024, 4, 2))
                for (c0, nh_, si) in segs:
                    a1, a2, a3 = s1s[si], s2s[si], s3s[si]
                    K.op(DVE, [sqq], [a1], lambda e, c0=c0, nh_=nh_, a1=a1: e.tensor_reduce(
                        out=a1[:, 0:nh_], in_=sqq[:, c0:c0 + nh_ * 64].rearrange("p (h d) -> p h d", d=64), axis=AX.X, op=ALU.add))
                    K.op(DVE, [a1], [a2], lambda e, nh_=nh_, a1=a1, a2=a2: e.tensor_scalar(out=a2[:, 0:nh_], in0=a1[:, 0:nh_], scalar1=1.0 / HD, scalar2=EPS,
                                                                                           op0=ALU.mult, op1=ALU.add))
                    K.op(POOL, [a2, mhalf], [a3], lambda e, nh_=nh_, a2=a2, a3=a3: e.tensor_tensor(out=a3[:, 0:nh_], in0=a2[:, 0:nh_], in1=mhalf[:, 0:nh_], op=ALU.pow))
                    src = qk[:, c0:c0 + nh_ * 64]
                    K.op(DVE, [qk, a3], [qf],
                         lambda e, c0=c0, nh_=nh_, a3=a3, src=src: e.tensor_tensor(
                             out=qf[:, c0:c0 + nh_ * 64].rearrange("p (h d) -> p h d", d=64), in0=src.rearrange("p (h d) -> p h d", d=64),
                             in1=a3[:, 0:nh_].unsqueeze(2).to_broadcast([128, nh_, 64]), op=ALU.mult))
                yield
                dstq = ta if lat else qb
                K.op(DVE, [qf, gq], [dstq], lambda e: e.tensor_tensor(out=dstq[:, 0:1024].rearrange("p (h d) -> p h d", d=64),
                                                                       in0=qf[:, 0:1024].rearrange("p (h d) -> p h d", d=64),
                                                                       in1=gq[:].unsqueeze(1).to_broadcast([128, 16, 64]), op=ALU.mult))
                dstk = ta if lat else tb
                K.op(DVE, [qf, gk], [dstk], lambda e: e.tensor_tensor(out=dstk[:, 1024:1280].rearrange("p (h d) -> p h d", d=64),
                                                                       in0=qf[:, 1024:1280].rearrange("p (h d) -> p h d", d=64),
                                                                       in1=gk[:].unsqueeze(1).to_broadcast([128, 4, 64]), op=ALU.mult))
                yield
                if lat:
                    v5 = lambda t_: t_[:].rearrange("p (h r x f) -> p h r x f", h=20, r=2, x=2, f=16)
                    cosb = rc[:, tile, :].unsqueeze(1).to_broadcast([128, 20, 64])
                    sn = rs[:, tile, :].rearrange("p (r x f) -> p r x f", r=2, x=2, f=16)
                    K.op(DVE, [ta, rc], [qf], lambda e: e.tensor_tensor(out=qf[:].rearrange("p (h d) -> p h d", d=64),
                                                                        in0=ta[:].rearrange("p (h d) -> p h d", d=64), in1=cosb, op=ALU.mult))
                    for x in range(2):
                        K.op(DVE, [ta, rs], [tb],
                             lambda e, x=x: e.tensor_tensor(out=v5(tb)[:, :, :, x, :], in0=v5(ta)[:, :, :, 1 - x, :],
                                                            in1=sn[:, :, x, :].unsqueeze(1).to_broadcast([128, 20, 2, 16]), op=ALU.mult))
                    K.op(DVE, [qf, tb], [qb], lambda e: e.tensor_tensor(out=qb[:], in0=qf[:], in1=tb[:], op=ALU.add))
                else:
                    K.dma(SP, newk[ctx_row:ctx_row + 128, :], tb[:, 1024:1280], [tb], [dres("newk%d" % ctx_row)], tb, store=True)
                    K.dma(SP, newv[ctx_row:ctx_row + 128, :], qk[:, 1280:1536], [qk], [dres("newv%d" % ctx_row)], qk, store=True)
                    K.op(POOL, [tb], [qb], lambda e: e.tensor_copy(out=qb[:, 1024:1280], in_=tb[:, 1024:1280]))
                yield
                QT = QTs[qi]
                for n in range(2):
                    K.group(PE, [qb, ident], [K.banks[n]],
                            [(lambda e, n=n, h=h: e.transpose(out=K.bap(n, BF16)[0:64, h * 128:(h + 1) * 128],
                                                              in_=qb[:, (n * 8 + h) * 64:(n * 8 + h + 1) * 64], identity=ident[:])) for h in range(8)])
                    K.op(DVE, [K.banks[n]], [QT],
                         (lambda e, n=n: e.tensor_copy(out=QT[0:64, n * 8:(n + 1) * 8, :], in_=K.bap(n, BF16)[0:64, :].rearrange("p (h t) -> p h t", h=8))))
                K.group(PE, [qb, ident], [K.banks[2]],
                        [(lambda e, g=g: e.transpose(out=K.bap(2, BF16)[0:64, g * 128:(g + 1) * 128],
                                                     in_=qb[:, 1024 + g * 64:1024 + (g + 1) * 64], identity=ident[:])) for g in range(4)])
                K.op(DVE, [K.banks[2]], [KT], lambda e: e.tensor_copy(out=KT[0:64, :, kslot * 128:(kslot + 1) * 128],
                                                                      in_=K.bap(2, BF16)[0:64, 0:512].rearrange("p (g t) -> p g t", g=4)))

            pc = [0]
            pp = [0]

            def attend(tile, qi, keys):
                QT = QTs[qi]
                ao = aos[tile % 2]
                nk = len(keys)
                pending = [None]

                def kv(g, key):
                    kind, slot, mask = key
                    if kind == 'c':
                        return cKT[:, g, slot * 128:(slot + 1) * 128], cV[:, slot, g, :], cKT, cV
                    return KT[:, g, slot * 128:(slot + 1) * 128], V[:, slot, g, :], KT, V

                for g in range(4):
                    sb_ = {}

                    def emit_S(i, g=g, sb_=sb_):
                        bs = (3, 4, 5, 7)[pc[0] % 4]
                        pc[0] += 1
                        sb_[i] = bs
                        kap, vap, kr, vr = kv(g, keys[i])
                        mk = keys[i][2]
                        pairs_ = [(kap, QT[:, 4 * g:4 * g + 4, :])]
                        rd_ = [kr, QT]
                        if mk is not None:
                            mbt = mb_prev if mk is mprev else mb_next
                            pairs_.append((ident[:], mbt[:].rearrange("p h t -> p (h t)")))
                            rd_ += [ident, mbt]
                        K.mm(bs, pairs_, 128, 512, rd_)

                    emit_S(0)
                    if nk > 1:
                        emit_S(1)
                    if nk > 2:
                        emit_S(2)
                    for i in range(nk):
                        if i + 3 < nk:
                            emit_S(i + 3)
                        bs = sb_[i]
                        P = Pt[pp[0] % 4]
                        pp[0] += 1
                        kap, vap, kr, vr = kv(g, keys[i])
                        mask = keys[i][2]
                        K.op(ACT, [K.banks[bs]], [P], lambda e, bs=bs, P=P: e.activation(out=P[:], in_=K.bap(bs), func=AF.Exp, scale=0.125))
                        K.mm(6, [(vap, P[:])], 96, 512, [vr, P], first=(i == 0), last=False)
                        if i == nk - 1:
                            K.mm(6, [(sel0[0:1, :], sinkb[0:1, 4 * g:4 * g + 4, :].rearrange("p h t -> p (h t)"))], 96, 512, [sel0, sinkb], first=False, last=True)
                        if i == min(2, nk - 1) and pending[0] is not None:
                            pending[0]()
                            pending[0] = None
                    O = Osb[g % 2]
                    rd = rdens[g % 2]
                    K.op(ACT, [K.banks[6]], [rd], lambda e, rd=rd: e.activation(out=rd[64:65, :], in_=K.bap(6)[64:65, :], func=AF.Ln))
                    rdb = rdbs[g % 2]
                    K.op(ACT, [rd], [rdb], lambda e, rd=rd, rdb=rdb: e.activation(out=rdb[64:65, :], in_=rd[64:65, :], func=AF.Exp, scale=-1.0))
                    K.op(ACT, [K.banks[6]], [O], lambda e, O=O: e.activation(out=O[0:64, :], in_=K.bap(6)[0:64, :], func=AF.Copy))

                    def fin(O=O, rdb=rdb, g=g):
                        K.mm(2, [(ones_b[:, 0:64], rdb[:, :])], 64, 512, [ones_b, rdb])
                        K.op(DVE, [O, K.banks[2]], [ao], lambda e: e.tensor_tensor(
                            out=ao[:, 4 * g:4 * g + 4, :].rearrange("p h t -> p (h t)"), in0=O[0:64, :], in1=K.bap(2)[0:64, :], op=ALU.mult))
                    pending[0] = fin
                    if g < 3:
                        yield
                pending[0]()
                K.dma(SP, attT[:, :, tile * 128:(tile + 1) * 128], ao[:], [ao], [dres("attT%d" % (tile // 4))], ao, store=True)

            load_consts((cB, cA), 1, 0, which=(0, 1))
            ck = [('c', t, None) for t in range(4)]

            def lat_keys(j):
                keys = list(ck)
                if j >= 1:
                    keys.append(('l', j - 1, mprev))
                keys.append(('l', j, None))
                if j + 1 < 32:
                    keys.append(('l', j + 1, mnext))
                return keys

            def step(gen):
                if gen is not None:
                    next(gen, None)

            load_qkv(0)
            for i in range(34):
                j = i - 2
                P = prep(i, i % 3, i, True) if i < 32 else None
                A = attend(j, j % 3, lat_keys(j)) if j >= 0 else None
                step(P)
                step(A)
                step(P)
                step(A)
                step(P)
                if i + 1 < 32:
                    load_qkv(i + 1)
                step(A)
                step(P)
                step(A)
                step(P)
            load_consts((cB, cA), 1, 1, which=(0, 1))
            for sq_ in range(2):
                for t in range(2):
                    tl = 32 + 2 * sq_ + t
                    load_qkv(tl)
                    for _ in prep(tl, t, t, False, ctx_row=sq_ * 256 + t * 128):
                        pass
                for t in range(2):
                    for _ in attend(32 + 2 * sq_ + t, t, [('l', 0, None), ('l', 1, None)]):
                        pass
            K.set_banks(range(8))
        K.barrier()

    p3()
    if upto < 4:
        return


    def hyena_phases():
        nyq_t = K.sb("nyq_t", [128, 128], BF16)
        nyqp_t = K.sb("nyqp_t", [128, 2], BF16)
        K.dma(SP, nyq_t[:], nyq, [], [nyq_t], nyq_t)
        K.dma(SP, nyqp_t[:], nyqp, [], [nyqp_t], nyqp_t)
        for n in (NLAT, NCTX):
            filters(n, nyqp_t)
        def run_gens(gens):
            active = list(gens)
            while active:
                for g_ in list(active):
                    if next(g_, "end") == "end":
                        active.remove(g_)

        with ExitStack() as ps_:
            run_gens([hyconv(0, NLAT, nyq_t, nyqp_t, ps_)])
            K.barrier()
        with ExitStack() as ps_:
            run_gens([hyconv(NLAT + sq_ * NCTX, NCTX, nyq_t, nyqp_t, ps_) for sq_ in range(2)])
            K.barrier()

    def filters(n, nyqp_t):
        CH = min(512, n)
        nch = n // CH
        with ExitStack() as ps:
            w1 = K.sb("fw1", [FE, FW], F32, ps)
            w2 = K.sb("fw2", [FW, FW], F32, ps)
            fr = K.sb("ffr", [FW, 4], F32, ps)
            K.dma(SP, w1[:], filt_w1, [], [w1], w1)
            K.dma(SP, w2[:], filt_w2, [], [w2], w2)
            with nc.allow_non_contiguous_dma(reason="tiny"):
                K.dma(SP, fr[:, 0:1], filt_freq.rearrange("(p o) -> p o", o=1), [], [fr], fr)
                K.dma(SP, fr[:, 1:2], filt_b1.rearrange("(p o) -> p o", o=1), [], [fr], fr)
                K.dma(SP, fr[:, 2:3], filt_b2.rearrange("(p o) -> p o", o=1), [], [fr], fr)
            sc3 = K.sb("sc3", [FW, 4], F32, ps)
            K.op(DVE, [fr], [sc3], lambda e: e.tensor_scalar(out=sc3[:, 0:1], in0=fr[:, 0:1], scalar1=1.0 / 3.0, scalar2=None, op0=ALU.mult))
            K.op(DVE, [fr, sc3], [sc3], lambda e: e.tensor_tensor(out=sc3[:, 1:2], in0=fr[:, 1:2], in1=sc3[:, 0:1], op=ALU.mult))
            K.op(DVE, [fr, sc3], [sc3], lambda e: e.tensor_tensor(out=sc3[:, 2:3], in0=fr[:, 2:3], in1=sc3[:, 0:1], op=ALU.mult))
            h2T = K.sb("h2T", [FW, n], F32, ps)
            h2Tb = K.sb("h2Tb", [FW, n], BF16, ps)
            ps2 = ExitStack()
            zT = K.sb("zT", [FE, n], F32, ps2)
            K.dma(SP, zT[:], zemb[n], [], [zT], zT)
            h1T = K.sb("h1T", [FW, n], F32, ps2)
            ts = K.sb("ts", [FW, CH], F32, ps2)
            tu = K.sb("tu", [FW, CH], F32, ps2)
            for (wm, kdim, src, dst, bcol) in ((w1, FE, zT, h1T, 1), (w2, FW, h1T, h2T, 2)):
                for ch in range(nch):
                    b = K.bank()
                    K.mm(b, [(wm[0:kdim, :], src[0:kdim, ch * CH:(ch + 1) * CH])], FW, CH, [wm, src])
                    K.op(ACT, [K.banks[b], sc3], [ts], lambda e, b=b, bcol=bcol: e.activation(out=ts[:], in_=K.bap(b)[0:FW, 0:CH], func=AF.Sin,
                                                                                             bias=sc3[:, bcol:bcol + 1], scale=sc3[:, 0:1]))
                    K.op(DVE, [ts], [tu], lambda e: e.tensor_tensor(out=tu[:], in0=ts[:], in1=ts[:], op=ALU.mult))
                    K.op(DVE, [tu], [tu], lambda e: e.tensor_scalar(out=tu[:], in0=tu[:], scalar1=-4.0, scalar2=3.0, op0=ALU.mult, op1=ALU.add))
                    K.op(DVE, [ts, tu], [dst], lambda e, dst=dst, ch=ch: e.tensor_tensor(out=dst[:, ch * CH:(ch + 1) * CH], in0=ts[:], in1=tu[:], op=ALU.mult))
            K.op(ACT, [h2T], [h2Tb], lambda e: e.activation(out=h2Tb[:], in_=h2T[:], func=AF.Copy))
            K.barrier()
            ps2.close()
            nmt = n // 256
            CS = min(8, nmt)
            nq = nmt // CS
            M_ = mats[n]
            tc_ = K.sb("tc", [128, 2, nmt], F32, ps)
            K.dma(SP, tc_[:], tcol[n], [], [tc_], tc_)
            wkt = K.sb("wkt", [128, 2, nmt + 1], F32, ps)
            K.dma(SP, wkt[:], wk[n], [], [wkt], wkt)
            w3o = K.sb("w3o", [FW, 2, 512], BF16, ps)
            Eabs = [K.sb("Eab%d" % i, [128, 2, 512], BF16, ps) for i in range(2)]
            ones_bf = K.sb("ones_bf", [128, 2], BF16, ps)
            K.op(POOL, [], [ones_bf], lambda e: e.memset(ones_bf[:], 1.0))
            b3b = K.sb("b3b", [128, 2, 512], F32, ps)
            ndb = K.sb("ndb", [128, 2, 512], F32, ps)
            skb = K.sb("skb", [128, 512], F32, ps)
            FG = K.sb("FG", [128, 2, nmt, 2, 512], BF16, ps)
            FGp = [FG, Tile("FGg", FG.t)]
            hvs = [K.sb("hv%d" % i, [128, 2, 512], F32, ps) for i in range(2)]
            Ees = [K.sb("Ee%d" % i, [128, 2, 512], F32, ps) for i in range(2)]
            nrm = K.sb("nrm", [1, 512], F32, ps)
            rb = K.sb("rb", [128, 512], F32, ps)
            NR = 4
            mbuf = {nm: [K.sb("m%s%d" % (nm, i), [128, CS, 128], BF16, ps) for i in range(NR)] for nm in ("ce", "co", "se", "so")}
            e4 = K.sb("e4", [128, 4, 512], F32, ps)
            ksts = [K.sb("kst%d" % i, [128, 4, 512], F32, ps) for i in range(2)]
            kstb = [K.sb("kstb%d" % i, [128, 4, 512], BF16, ps) for i in range(2)]
            mc = [0]
            f2 = lambda t_: t_[:].rearrange("p a c -> p (a c)")
            for o in range(2):
              for chf in range(2):
                for d_ in range(2):
                    c0 = o * 2048 + d_ * 1024 + chf * 512
                    K.dma(POOL, w3o[:, d_, :], filt_w3[:, c0:c0 + 512], [], [w3o], w3o)
                    K.dma(SP, b3b[:, d_, :], filt_b3[c0:c0 + 512].partition_broadcast(128), [], [b3b], b3b)
                    K.dma(SP, ndb[:, d_, :], filt_decay[c0:c0 + 512].partition_broadcast(128), [], [ndb], ndb)
                K.dma(SP, skb[:], hyena_skip[o * 1024 + chf * 512:o * 1024 + (chf + 1) * 512].partition_broadcast(128), [], [skb], skb)
                K.op(DVE, [ndb], [ndb], lambda e: e.scalar_tensor_tensor(out=f2(ndb), in0=f2(ndb), scalar=-1.0, in1=f2(ndb), op0=ALU.mult, op1=ALU.min))
                K.set_banks([1, 2, 3, 4, 5, 6, 7])
                first_acc = [True]
                steps = [(par, tile) for par in range(2) for tile in range(nmt)]

                def front(i):
                    par, tile = steps[i]
                    hv, Ee = hvs[i % 2], Ees[i % 2]
                    K.op(ACT, [ndb, tc_], [Ee], lambda e: e.activation(out=f2(Ee), in_=f2(ndb), func=AF.Exp, scale=tc_[:, par, tile:tile + 1]))
                    tok = h2Tb[:, 256 * tile + par:256 * tile + 256:2]
                    for d_ in range(2):
                        b = K.bank()
                        K.mm(b, [(tok, w3o[:, d_, :])], 128, 512, [h2Tb, w3o])
                        K.op(DVE, [K.banks[b], b3b], [hv], lambda e, b=b, d_=d_: e.tensor_tensor(out=hv[:, d_, :], in0=K.bap(b), in1=b3b[:, d_, :], op=ALU.add))

                def back(i):
                    par, tile = steps[i]
                    hv, Ee = hvs[i % 2], Ees[i % 2]
                    last_t = (i == len(steps) - 1)
                    K.op(POOL, [hv, Ee], [hv], lambda e: e.tensor_tensor(out=hv[:, 0, :], in0=hv[:, 0, :], in1=Ee[:, 0, :], op=ALU.mult))
                    K.op(DVE, [hv, Ee], [hv], lambda e: e.tensor_tensor(out=hv[:, 1, :], in0=hv[:, 1, :], in1=Ee[:, 1, :], op=ALU.mult))
                    Eab = Eabs[i % 2]
                    K.op(ACT, [hv], [Eab], lambda e: e.activation(out=f2(Eab), in_=f2(hv), func=AF.Abs))
                    for d_ in range(2):
                        K.mm(0, [(ones_bf[:, 0:1], Eab[:, d_, :])], 1, 512, [Eab, ones_bf], first=first_acc[0], last=(last_t and d_ == 1))
                        first_acc[0] = False
                    if i == 0:
                        K.op(POOL, [Eab], [hv], lambda e: e.memset(hv[0:1, 1, :], 0.0))
                    K.op(DVE, [hv], [FGp[0]], lambda e: e.tensor_tensor(out=FG[:, par, tile, 0, :], in0=hv[:, 0, :], in1=hv[:, 1, :], op=ALU.add))
                    K.op(POOL, [hv], [FGp[1]], lambda e: e.tensor_tensor(out=FG[:, par, tile, 1, :], in0=hv[:, 1, :], in1=hv[:, 0, :], op=ALU.subtract))

                for i in range(len(steps) + 1):
                    if i < len(steps):
                        front(i)
                    if i >= 1:
                        back(i - 1)
                K.op(DVE, [K.banks[0]], [nrm], lambda e: e.tensor_scalar(out=nrm[:], in0=K.bap(0)[0:1, :], scalar1=EPS, scalar2=None, op0=ALU.add))
                K.op(DVE, [nrm], [nrm], lambda e: e.reciprocal(out=nrm[:], in_=nrm[:]))
                K.mm(1, [(ones_f[0:1, :], nrm[0:1, :])], 128, 512, [ones_f, nrm])
                K.op(ACT, [K.banks[1]], [rb], lambda e: e.activation(out=rb[:], in_=K.bap(1), func=AF.Copy))

                def finalize(kst, rows, kt):
                    r = slice(0, rows)
                    K.op(DVE, [e4], [kst], lambda e: e.tensor_tensor(out=kst[r, 0, :], in0=e4[r, 0, :], in1=e4[r, 1, :], op=ALU.add))
                    K.op(POOL, [e4], [kst], lambda e: e.tensor_tensor(out=kst[r, 2, :], in0=e4[r, 0, :], in1=e4[r, 1, :], op=ALU.subtract))
                    K.op(DVE, [e4], [kst], lambda e: e.tensor_tensor(out=kst[r, 1, :], in0=e4[r, 2, :], in1=e4[r, 3, :], op=ALU.add))
                    K.op(POOL, [e4], [kst], lambda e: e.tensor_tensor(out=kst[r, 3, :], in0=e4[r, 3, :], in1=e4[r, 2, :], op=ALU.subtract))
                    K.op(DVE, [kst, rb], [kst], lambda e: e.tensor_tensor(out=kst[r], in0=kst[r], in1=rb[r].unsqueeze(1).to_broadcast([rows, 4, 512]), op=ALU.mult))
                    K.op(POOL, [kst, skb], [kst], lambda e: e.tensor_tensor(out=kst[r, 0, :], in0=kst[r, 0, :], in1=skb[r], op=ALU.add))
                    K.op(POOL, [kst, skb], [kst], lambda e: e.tensor_tensor(out=kst[r, 2, :], in0=kst[r, 2, :], in1=skb[r], op=ALU.add))
                    kb = kstb[kt % 2]
                    K.op(ACT, [kst, wkt], [kb], lambda e: e.activation(out=kb[r, 0:2, :].rearrange("p a c -> p (a c)"), in_=kst[r, 0:2, :].rearrange("p a c -> p (a c)"),
                                                                      func=AF.Copy, scale=wkt[r, 0, kt:kt + 1]))
                    K.op(ACT, [kst, wkt], [kb], lambda e: e.activation(out=kb[r, 2:4, :].rearrange("p a c -> p (a c)"), in_=kst[r, 2:4, :].rearrange("p a c -> p (a c)"),
                                                                      func=AF.Copy, scale=wkt[r, 1, kt:kt + 1]))
                    K.dma(ACT, kfs[n][o, kt, :, r, chf * 512:(chf + 1) * 512].rearrange("a p c -> p a c"), kb[r], [kb], [dres("kfs%d_%d_%d_%d" % (n, o, kt, chf))], kb, store=True)

                K.set_banks(range(8))
                grp = (("ce", 0, 0), ("co", 1, 0), ("se", 0, 1), ("so", 1, 1))
                for kt in range(nmt):
                    base = 4 * (kt % 2)
                    for q in range(nq):
                        bufs = {}
                        for nm, par, pl in grp:
                            mb = mbuf[nm][mc[0] % NR]
                            K.dma(SP, mb[:], M_[nm][kt, :, q * CS:(q + 1) * CS, :], [], [mb], mb)
                            bufs[nm] = mb
                        mc[0] += 1
                        for gi, (nm, par, pl) in enumerate(grp):
                            mb = bufs[nm]
                            K.mm(base + gi, [(mb[:, i, :], FG[:, par, q * CS + i, pl, :]) for i in range(CS)], 128, 512, [mb, FGp[pl]], first=(q == 0), last=(q == nq - 1))
                    K.op(ACT, [K.banks[base + i] for i in range(4)], [e4], lambda e, base=base: e.activation(out=e4[:].rearrange("p a c -> p (a c)"), in_=K.bap(base, nb=4), func=AF.Copy))
                    finalize(ksts[kt % 2], 128, kt)
                K.op(POOL, [], [e4], lambda e: e.memset(e4[0:1].rearrange("p a c -> p (a c)"), 0.0))
                K.mm(0, [(nyqp_t[:, 0:1], FG[:, 0, mt, 0, :]) for mt in range(nmt)], 1, 512, [nyqp_t, FGp[0]])
                K.mm(1, [(nyqp_t[:, 0:1], FG[:, 1, mt, 1, :]) for mt in range(nmt)], 1, 512, [nyqp_t, FGp[1]])
                K.op(ACT, [K.banks[0]], [e4], lambda e: e.activation(out=e4[0:1, 0, :], in_=K.bap(0)[0:1, :], func=AF.Copy))
                K.op(ACT, [K.banks[1]], [e4], lambda e: e.activation(out=e4[0:1, 3, :], in_=K.bap(1)[0:1, :], func=AF.Copy))
                finalize(ksts[nmt % 2], 1, nmt)
            K.set_banks(range(8))
        K.barrier()

    hw = [0]

    def hyconv(tok0, n, nyq_t, nyqp_t, ps):
        nmt = n // 256
        CS = min(8, nmt)
        nq = nmt // CS
        M_ = mats[n]
        nh2 = max(n // 2, 128)
        if True:
            cw = K.sb("cw", [128, 24, 3], F32, ps)
            cb = K.sb("cb", [128, 24], F32, ps)
            with nc.allow_non_contiguous_dma(reason="tiny"):
                for j in range(3):
                    K.dma(SP, cw[:, :, j], conv_w[j].rearrange("(ct p) -> p ct", p=128), [], [cw], cw)
                K.dma(SP, cb[:], conv_b.rearrange("(ct p) -> p ct", p=128), [], [cb], cb)
            WM = 384 if n == NLAT else 512
            NJM = WM // 128
            u = K.sb("u", [128, n + 2], BF16, ps)
            K.op(POOL, [], [u], lambda e: e.memset(u[:], 0.0))
            acc = K.sb("acc", [128, nh2], F32, ps)
            fTs = [K.sb("fT%d" % i, [128, NJM, n], BF16, ps) for i in range(2)]
            z = K.sb("z", [128, 2, nmt, WM], BF16, ps)
            Y = K.sb("Y", [128, nmt, 4, WM], BF16, ps)
            Yp = [Tile("Yp%d" % i, Y.t) for i in range(4)]
            Yx = K.sb("Yx", [1, 2, WM], BF16, ps)
            kfx = K.sb("kfx", [1, 4, WM], BF16, ps)
            tx = [K.sb("tx%d" % i, [1, WM], F32, ps) for i in range(3)]
            mnames = ("ce", "co", "se", "so", "cot", "sot")
            mbuf = {nm: [K.sb("m%s%d" % (nm, i), [128, CS, 128], BF16, ps) for i in range(4)] for nm in ("ce", "co", "se", "so")}
            mbuf["cot"] = mbuf["co"]
            mbuf["sot"] = mbuf["so"]
            kft = [K.sb("kft%d" % i, [128, 4, WM], BF16, ps) for i in range(2)]
            e4s = [K.sb("e4%d" % i, [128, 4, WM], BF16, ps) for i in range(2)]
            tq = [K.sb("tq%d" % i, [128, WM], BF16, ps) for i in range(8)]
            xtl = [K.sb("xtl%d" % i, [128, WM], BF16, ps) for i in range(2)]
            hyt = [K.sb("hyt%d" % i, [128, WM], BF16, ps) for i in range(2)]
            hst = [K.sb("hst%d" % i, [128, NJM, 256], BF16, ps) for i in range(2)]
            mc = [0]
            hres = [dres("hyT%d" % c) for c in range(tok0 // 512, (tok0 + n + 511) // 512)]
            chunks = ((0, 3), (3, 3), (6, 2)) if n == NLAT else ((0, 4), (4, 4))

            def conv_steps(ci, blk):
                j0_, nj_ = chunks[ci]
                fT_ = fTs[ci % 2]
                for j in range(nj_):
                    ct = blk * 8 + j0_ + j
                    K.dma(SP, u[:, 1:n + 1], hyT[ct * 128:(ct + 1) * 128, tok0:tok0 + n], hres, [u], u)
                    for hh in range(n // nh2):
                        r0 = hh * nh2
                        K.op(DVE, [u, cw, cb], [acc], lambda e, ct=ct, r0=r0: e.tensor_scalar(out=acc[:], in0=u[:, 1 + r0:1 + r0 + nh2], scalar1=cw[:, ct, 1:2], scalar2=cb[:, ct:ct + 1],
                                                                                          op0=ALU.mult, op1=ALU.add))
                        K.op(DVE, [u, acc, cw], [acc], lambda e, ct=ct, r0=r0: e.scalar_tensor_tensor(out=acc[:], in0=u[:, r0:r0 + nh2], scalar=cw[:, ct, 0:1], in1=acc[:],
                                                                                                  op0=ALU.mult, op1=ALU.add))
                        K.op(DVE, [u, acc, cw], [fT_], lambda e, ct=ct, j=j, r0=r0: e.scalar_tensor_tensor(out=fT_[:, j, r0:r0 + nh2], in0=u[:, 2 + r0:2 + r0 + nh2], scalar=cw[:, ct, 2:3],
                                                                                                       in1=acc[:], op0=ALU.mult, op1=ALU.add))
                    yield

            def ftile_(ci, tau, par, dst_ap, dst_res, bsel):
                j0_, nj_ = chunks[ci]
                fT_ = fTs[ci % 2]
                b = 6 + (bsel % 2)
                K.group(PE, [fT_, ident], [K.banks[b]],
                        [(lambda e, j=j: e.transpose(out=K.bap(b, BF16)[:, j * 128:(j + 1) * 128], in_=fT_[:, j, 256 * tau + par:256 * tau + 256:2], identity=ident[:]))
                         for j in range(nj_)])
                K.op(ACT, [K.banks[b]], [dst_res], lambda e: e.activation(out=dst_ap, in_=K.bap(b, BF16)[:, 0:nj_ * 128], func=AF.Copy))

            def prologue(ci):
                j0_, nj_ = chunks[ci]
                yield from conv_steps(ci, 2)
                for tau in range(nmt):
                    for par in range(2):
                        ftile_(ci, tau, par, z[:, par, tau, 0:nj_ * 128], z, 2 * tau + par)
                    yield
                yield from conv_steps(ci, 0)

            yield from prologue(0)
            for ci, (j0, nj) in enumerate(chunks):
                W = nj * 128
                c0 = j0 * 128
                fT = fTs[ci % 2]
                nxt = [None]

                def ftile(tau, par, dst_ap, dst_res, bsel, ci=ci):
                    ftile_(ci, tau, par, dst_ap, dst_res, bsel)


                fgrp = (("ce", 0), ("co", 1), ("se", 0), ("so", 1))
                for o in range(2):
                    for kt in range(nmt):
                        base = 4 * (kt % 2)
                        for q in range(nq):
                            bufs = {}
                            for nm, par in fgrp:
                                mb = mbuf[nm][mc[0] % 4]
                                K.dma(SP, mb[:], M_[nm][kt, :, q * CS:(q + 1) * CS, :], [], [mb], mb)
                                bufs[nm] = mb
                            mc[0] += 1
                            for gi, (nm, par) in enumerate(fgrp):
                                mb = bufs[nm]
                                K.mm(base + gi, [(mb[:, i, :], z[:, par, q * CS + i, 0:W]) for i in range(CS)], 128, W, [mb, z], first=(q == 0), last=(q == nq - 1))
                        kf = kft[kt % 2]
                        K.dma(SP, kf[:, :, 0:W], kfs[n][o, kt, :, :, c0:c0 + W].rearrange("a p c -> p a c"), [dres("kfs%d_%d_%d_%d" % (n, o, kt, ch_)) for ch_ in range(2)], [kf], kf)
                        e4 = e4s[kt % 2]
                        for gi in range(4):
                            K.op(ACT, [K.banks[base + gi]], [e4], lambda e, base=base, gi=gi, e4=e4: e.activation(out=e4[:, gi, 0:W], in_=K.bap(base + gi)[:, 0:W], func=AF.Copy))
                        Ec, Oc, Es, Os = (e4[:, i, 0:W] for i in range(4))
                        t = [tq[i][:, 0:W] for i in range(8)]
                        tr = tq
                        TT = lambda E_, o_, a_, b_, op_, rd, wr: K.op(E_, rd, wr, lambda e: e.tensor_tensor(out=o_, in0=a_, in1=b_, op=op_))
                        TT(DVE, t[0], Ec, Oc, ALU.add, [e4], [tr[0]])
                        TT(POOL, t[1], Ec, Oc, ALU.subtract, [e4], [tr[1]])
                        TT(DVE, t[2], Es, Os, ALU.add, [e4], [tr[2]])
                        TT(POOL, t[3], Os, Es, ALU.subtract, [e4], [tr[3]])
                        KreA, KimA, KreB, KimB = (kf[:, i, 0:W] for i in range(4))
                        TT(DVE, t[4], t[0], KreA, ALU.mult, [tr[0], kf], [tr[4]])
                        TT(DVE, t[5], t[2], KimA, ALU.mult, [tr[2], kf], [tr[5]])
                        TT(DVE, t[4], t[4], t[5], ALU.add, [tr[4], tr[5]], [tr[4]])
                        TT(DVE, t[5], t[2], KreA, ALU.mult, [tr[2], kf], [tr[5]])
                        TT(DVE, t[0], t[0], KimA, ALU.mult, [tr[0], kf], [tr[0]])
                        TT(DVE, t[5], t[5], t[0], ALU.subtract, [tr[5], tr[0]], [tr[5]])
                        TT(POOL, t[6], t[1], KreB, ALU.mult, [tr[1], kf], [tr[6]])
                        TT(POOL, t[7], t[3], KimB, ALU.mult, [tr[3], kf], [tr[7]])
                        TT(POOL, t[6], t[6], t[7], ALU.add, [tr[6], tr[7]], [tr[6]])
                        TT(POOL, t[7], t[3], KreB, ALU.mult, [tr[3], kf], [tr[7]])
                        TT(POOL, t[1], t[1], KimB, ALU.mult, [tr[1], kf], [tr[1]])
                        TT(POOL, t[7], t[7], t[1], ALU.subtract, [tr[7], tr[1]], [tr[7]])
                        TT(DVE, Y[:, kt, 0, 0:W], t[4], t[6], ALU.add, [tr[4], tr[6]], [Yp[0]])
                        TT(POOL, Y[:, kt, 1, 0:W], t[4], t[6], ALU.subtract, [tr[4], tr[6]], [Yp[1]])
                        TT(DVE, Y[:, kt, 2, 0:W], t[5], t[7], ALU.subtract, [tr[5], tr[7]], [Yp[2]])
                        TT(POOL, Y[:, kt, 3, 0:W], t[5], t[7], ALU.add, [tr[5], tr[7]], [Yp[3]])
                        yield
                    K.mm(0, [(nyqp_t[:, 0:1], z[:, 0, mt, 0:W]) for mt in range(nmt)], 1, W, [nyqp_t, z])
                    K.mm(1, [(nyqp_t[:, 0:1], z[:, 1, mt, 0:W]) for mt in range(nmt)], 1, W, [nyqp_t, z])
                    K.dma(SP, kfx[:, :, 0:W], kfs[n][o, nmt, :, 0:1, c0:c0 + W].rearrange("a p c -> p a c"), [dres("kfs%d_%d_%d_%d" % (n, o, nmt, ch_)) for ch_ in range(2)], [kfx], kfx)
                    t0, t1, t2 = (tx[i][:, 0:W] for i in range(3))
                    K.op(DVE, [K.banks[0], kfx], [tx[0]], lambda e: e.tensor_tensor(out=t0, in0=K.bap(0)[0:1, 0:W], in1=kfx[:, 0, 0:W], op=ALU.mult))
                    K.op(DVE, [K.banks[1], kfx], [tx[1]], lambda e: e.tensor_tensor(out=t1, in0=K.bap(1)[0:1, 0:W], in1=kfx[:, 1, 0:W], op=ALU.mult))
                    K.op(DVE, [tx[0], tx[1]], [Yx], lambda e: e.tensor_tensor(out=Yx[:, 0, 0:W], in0=t0, in1=t1, op=ALU.add))
                    K.op(DVE, [K.banks[1], kfx], [tx[0]], lambda e: e.tensor_tensor(out=t0, in0=K.bap(1)[0:1, 0:W], in1=kfx[:, 0, 0:W], op=ALU.mult))
                    K.op(DVE, [K.banks[0], kfx], [tx[1]], lambda e: e.tensor_tensor(out=t1, in0=K.bap(0)[0:1, 0:W], in1=kfx[:, 1, 0:W], op=ALU.mult))
                    K.op(DVE, [tx[0], tx[1]], [Yx], lambda e: e.tensor_tensor(out=Yx[:, 1, 0:W], in0=t0, in1=t1, op=ALU.subtract))
                    igrp = (("ce", 0, 0), ("se", 0, 2), ("cot", 1, 1), ("sot", 1, 3))
                    if o == 1 and ci + 1 < len(chunks):
                        nxt[0] = prologue(ci + 1)
                    for tau in range(nmt):
                        if nxt[0] is not None:
                            for _ in range(2):
                                if next(nxt[0], "end") == "end":
                                    nxt[0] = None
                                    break
                        by = [0 + 2 * (tau % 2), 1 + 2 * (tau % 2)]
                        for q in range(nq):
                            bufs = {}
                            for nm, par, pl in igrp:
                                mb = mbuf[nm][mc[0] % 4]
                                K.dma(SP, mb[:], M_[nm][tau, :, q * CS:(q + 1) * CS, :], [], [mb], mb)
                                bufs[nm] = mb
                            mc[0] += 1
                            for par in range(2):
                                pairs = []
                                rd = list(Yp)
                                for nm, par_, pl in igrp:
                                    if par_ == par:
                                        pairs += [(bufs[nm][:, i, :], Y[:, q * CS + i, pl, 0:W]) for i in range(CS)]
                                        rd.append(bufs[nm])
                                K.mm(by[par], pairs, 128, W, rd, first=(q == 0), last=False)
                        for par in range(2):
                            K.mm(by[par], [(nyq_t[0:1, :], Yx[0:1, par, 0:W])], 128, W, [nyq_t, Yx], first=False, last=True)
                        hs = hst[tau % 2]
                        for par in range(2):
                            xt_ = xtl[par]
                            ftile(tau, par, xt_[:, 0:W], xt_, par)
                            if o == 0:
                                K.op(DVE, [K.banks[by[par]], xt_], [z], lambda e, par=par, xt_=xt_, tau=tau: e.tensor_tensor(out=z[:, par, tau, 0:W], in0=K.bap(by[par])[:, 0:W], in1=xt_[:, 0:W], op=ALU.mult))
                            else:
                                ht = hyt[par]
                                K.op(DVE, [K.banks[by[par]], xt_], [ht], lambda e, par=par, xt_=xt_, ht=ht: e.tensor_tensor(out=ht[:, 0:W], in0=K.bap(by[par])[:, 0:W], in1=xt_[:, 0:W], op=ALU.mult))
                                b = 4 + par
                                K.group(PE, [ht, ident], [K.banks[b]],
                                        [(lambda e, j=j, ht=ht, b=b: e.transpose(out=K.bap(b, BF16)[:, j * 128:(j + 1) * 128], in_=ht[:, j * 128:(j + 1) * 128], identity=ident[:]))
                                         for j in range(nj)])
                                K.op(ACT, [K.banks[b]], [hs], lambda e, b=b, hs=hs, par=par: e.activation(out=hs[:, 0:nj, par:256:2], in_=K.bap(b, BF16)[:, 0:W].rearrange("p (j t) -> p j t", j=nj), func=AF.Copy))
                        yield
                        if o == 1:
                            hw[0] += 1
                            K.dma(ACT, hyoT[c0:c0 + W, tok0 + tau * 256:tok0 + (tau + 1) * 256].rearrange("(j p) t -> p j t", p=128), hs[:, 0:nj, :],
                                  [hs], [dres("hyoT_w%d" % hw[0])], hs, store=True)
                    if o == 0:
                        yield from conv_steps(ci, 1)
                if nxt[0] is not None:
                    yield from nxt[0]


    def p6():
        TC = 512
        with ExitStack() as ps:
            cG = K.sb("mcG", [128, D], F32, ps)
            wab = load_w_bf16(ps, "wab", w_ab, 16, D, part=64)
            whb = load_w_bf16(ps, "whb", w_hb, 8, D)
            wo = load_w_bf16(ps, "wo", w_out, 8, D)
            ats = [K.sb("at%d" % i, [64, 16, TC], BF16, ps) for i in range(2)]
            hys = [K.sb("hy%d" % i, [128, 8, TC], BF16, ps) for i in range(2)]
            sgs = [K.sb("sg%d" % i, [128, 16, TC], BF16, ps) for i in range(2)]
            mT = K.sb("mT", [128, 8, TC], BF16, ps)
            t1s = [K.sb("t1%d" % i, [128, TC], F32, ps) for i in range(2)]
            t2s = [K.sb("t2%d" % i, [128, TC], F32, ps) for i in range(2)]
            xrs = [K.sb("xr%d" % i, [128, D], F32, ps) for i in range(2)]
            tmp = [K.sb("tmp%d" % i, [128, 512], F32, ps) for i in range(2)]
            nch = NT // 4

            def loads(c):
                at, hy, sg = ats[c % 2], hys[c % 2], sgs[c % 2]
                K.dma(SP, at[:], attT[:, :, c * TC:(c + 1) * TC], [dres("attT%d" % c)], [at], at)
                K.dma(SP, hy[:], hyoT[:, c * TC:(c + 1) * TC].rearrange("(k p) t -> p k t", p=128), [dres("hyoT")], [hy], hy)
                K.dma(SP, sg[:], sgT[:, c * TC:(c + 1) * TC].rearrange("(k p) t -> p k t", p=128), [dres("sgT%d" % c)], [sg], sg)

            load_consts((cG,), 1, 0, which=(2,))
            loads(0)
            for c in range(nch):
                if c + 1 < nch:
                    loads(c + 1)
                if c == 8:
                    load_consts((cG,), 1, 1, which=(2,))
                at, hy, sg = ats[c % 2], hys[c % 2], sgs[c % 2]
                for m in range(8):
                    ba = K.bank()
                    K.mm(ba, [(wab[:, h, m * 128:(m + 1) * 128], at[:, h, :]) for h in range(16)], 128, TC, [wab, at])
                    bh = K.bank()
                    K.mm(bh, [(whb[:, k, m * 128:(m + 1) * 128], hy[:, k, :]) for k in range(8)], 128, TC, [whb, hy])
                    t1, t2 = t1s[m % 2], t2s[m % 2]
                    K.op(DVE, [K.banks[ba], sg], [t1], lambda e, ba=ba, t1=t1, m=m, sg=sg: e.tensor_tensor(out=t1[:], in0=K.bap(ba), in1=sg[:, m, :], op=ALU.mult))
                    K.op(DVE, [K.banks[bh], sg], [t2], lambda e, bh=bh, t2=t2, m=m, sg=sg: e.tensor_tensor(out=t2[:], in0=K.bap(bh), in1=sg[:, 8 + m, :], op=ALU.mult))
                    K.op(POOL, [t1, t2], [mT], lambda e, t1=t1, t2=t2, m=m: e.tensor_tensor(out=mT[:, m, :], in0=t1[:], in1=t2[:], op=ALU.add))
                for t4 in range(4):
                    tile = c * 4 + t4
                    xr = xrs[tile % 2]
                    K.dma(SP, xr[:], x1s[tile * 128:(tile + 1) * 128, :], [dres("x1s%d" % tile)], [xr], xr)
                    for nh in range(2):
                        by = K.bank()
                        K.mm(by, [(mT[:, k, t4 * 128:(t4 + 1) * 128], wo[:, k, nh * 512:(nh + 1) * 512]) for k in range(8)], 128, 512, [mT, wo])
                        tm = tmp[nh]
                        K.op(DVE, [K.banks[by], cG], [tm], lambda e, by=by, tm=tm, nh=nh: e.tensor_tensor(out=tm[:], in0=K.bap(by), in1=cG[:, nh * 512:(nh + 1) * 512], op=ALU.mult))
                        K.op(POOL, [tm, xr], [xr], lambda e, tm=tm, xr=xr, nh=nh: e.tensor_tensor(out=xr[:, nh * 512:(nh + 1) * 512], in0=xr[:, nh * 512:(nh + 1) * 512], in1=tm[:], op=ALU.add))
                    K.dma(POOL, x2s[tile * 128:(tile + 1) * 128, :], xr[:], [xr], [dres("x2s%d" % tile)], xr, store=True)
        K.barrier()

    if upto >= 5:
        hyena_phases()
    if upto >= 6 or upto == -6:
        p6()
        ffn_phase(1, x2s, yout, "x2s", "yout")


_CONST_CACHE = {}


def _host_consts():
    if _CONST_CACHE:
        return _CONST_CACHE
    bf = ml_dtypes.bfloat16
    c = {}
    s = (np.arange(32)[None, :] * 128 + np.arange(128)[:, None]).astype(np.int64)
    inv = 10000.0 ** (-np.arange(16, dtype=np.float32) / 16.0)
    row = (s // 64).astype(np.float32)[..., None] * inv
    col = (s % 64).astype(np.float32)[..., None] * inv
    cr, sr, cc, sc = np.cos(row), np.sin(row), np.cos(col), np.sin(col)
    c["rope_c"] = np.concatenate([cr, cr, cc, cc], -1).astype(np.float32)
    c["rope_s"] = np.concatenate([-sr, sr, -sc, sc], -1).astype(np.float32)
    for n, tag in ((NLAT, "l"), (NCTX, "c")):
        N = 2 * n
        h = n // 2
        nmt = h // 128
        m = np.arange(h, dtype=np.int64)
        k = np.arange(h, dtype=np.int64)
        th = 2.0 * np.pi / N
        ae = th * ((2 * m[:, None] * k[None, :]) % N).astype(np.float64)
        ao = th * (((2 * m[:, None] + 1) * k[None, :]) % N).astype(np.float64)
        lay = lambda M: np.ascontiguousarray(M.reshape(nmt, 128, nmt, 128).transpose(2, 1, 0, 3)).astype(bf)
        c["ce_" + tag] = lay(np.cos(ae))
        c["se_" + tag] = lay(np.sin(ae))
        c["co_" + tag] = lay(np.cos(ao))
        c["so_" + tag] = lay(np.sin(ao))
        c["cot_" + tag] = lay(np.cos(ao).T)
        c["sot_" + tag] = lay(np.sin(ao).T)
        t = (np.arange(n, dtype=np.float32) / np.float32(max(n - 1, 1))).astype(np.float32)
        bands = np.arange(1, 17, dtype=np.float32)
        a = (np.float32(2.0 * math.pi) * t[:, None] * bands[None, :]).astype(np.float32)
        z = np.concatenate([t[:, None], np.cos(a), np.sin(a)], -1).astype(np.float32)
        c["zemb_" + tag] = np.ascontiguousarray(z.T)
        c["tcol_" + tag] = np.ascontiguousarray(t.reshape(nmt, 128, 2).transpose(1, 2, 0))
        wA = np.full((128, nmt + 1), 2.0 / N, np.float32)
        wB = np.full((128, nmt + 1), 2.0 / N, np.float32)
        wA[0, 0] = 1.0 / N
        wB[0, 0] = 1.0 / N
        wB[:, nmt] = 0.0
        c["wk_" + tag] = np.ascontiguousarray(np.stack([wA, wB], 1))
    c["nyq"] = np.tile(((-1.0) ** np.arange(128))[None, :], (128, 1)).astype(bf)
    c["nyqp"] = np.tile(((-1.0) ** np.arange(128))[:, None], (1, 2)).astype(bf)
    _CONST_CACHE.update(c)
    return c


def _core_inputs(core, inp, consts):
    f = lambda a: np.ascontiguousarray(np.asarray(a, dtype=np.float32))
    m = {}
    m["xin"] = np.concatenate([f(inp["x_sample"][core]), f(inp["x_prompt"][2 * core]), f(inp["x_prompt"][2 * core + 1])], 0)
    m["cvec"] = np.stack([f(inp["c"][core]), f(inp["c_ctx"])], 0)
    m["cache_k"] = f(inp["cache_k"][core, 0]).reshape(PAST, NKV * HD)
    m["cache_v"] = f(inp["cache_v"][core, 0]).reshape(PAST, NKV * HD)
    m["w_mod"] = f(inp["w_mod"][0])
    m["b_mod"] = f(inp["b_mod"][0])
    m["norms"] = np.stack([f(inp["norm_ffn1"][0]), f(inp["norm_mix"][0]), f(inp["norm_ffn2"][0])], 0)
    for k in ("ffn1_wi", "ffn1_wo", "ffn2_wi", "ffn2_wo", "w_in", "attn_sink", "conv_w", "conv_b", "filt_w1", "filt_b1",
              "filt_w2", "filt_b2", "filt_w3", "filt_b3", "filt_freq"):
        m[k] = f(inp[k][0])
    m["qk_norm"] = np.stack([f(inp["q_norm"][0]), f(inp["k_norm"][0])], 0)
    m["filt_decay"] = f(inp["filt_decay"][0]).reshape(4 * D)
    m["hyena_skip"] = f(inp["hyena_skip"][0]).reshape(2 * D)
    m["w_ab"] = f(inp["w_attn_branch"][0])
    m["w_hb"] = f(inp["w_hyena_branch"][0])
    m["w_out"] = f(inp["w_out"][0])
    m.update(consts)
    return m


_NC_CACHE = {}


def kernel(**inputs):
    consts = _host_consts()
    if "nc" not in _NC_CACHE:
        _NC_CACHE["nc"] = build_program()
    nc = _NC_CACHE["nc"]
    in_maps = [_core_inputs(c, inputs, consts) for c in range(8)]
    res = run_bass_kernel_spmd(nc, in_maps, core_ids=list(range(8)))
    y_prompt = np.zeros((16, NCTX, D), np.float32)
    y_sample = np.zeros((8, NLAT, D), np.float32)
    new_k = np.zeros((16, 1, NCTX, NKV, HD), np.float32)
    new_v = np.zeros((16, 1, NCTX, NKV, HD), np.float32)
    for c in range(8):
        r = res.results[c]
        y = np.asarray(r["yout"])
        y_sample[c] = y[:NLAT]
        y_prompt[2 * c] = y[NLAT:NLAT + NCTX]
        y_prompt[2 * c + 1] = y[NLAT + NCTX:]
        nk = np.asarray(r["newk"]).reshape(2, NCTX, NKV, HD)
        nv = np.asarray(r["newv"]).reshape(2, NCTX, NKV, HD)
        new_k[2 * c:2 * c + 2, 0] = nk
        new_v[2 * c:2 * c + 2, 0] = nv
    return (y_prompt, y_sample, new_k, new_v)
```
